# Optimizing a Trainium2 kernel written in Bass

```python
import jax, jax.numpy as jnp
from jax import lax
import numpy as np

D_MODEL = 1024
BATCH = 8
SEQ = 4096
DEPTH = 4

N_MIXERS = 3
HEAD_DIM = 64
N_HEADS = 12
MIX_WIDTH = N_HEADS * HEAD_DIM
N_MEM = 256
N_MEM_HEADS = 4
MEM_WIDTH = N_MEM_HEADS * HEAD_DIM
ROPE_DIM = HEAD_DIM // 4
ROPE_THETA = 500000.0
POS_OFFSET_RANGE = 8192
IDX_HEADS = 8
IDX_DIM = 64
TOPK_MAX = 256
QBLOCK_A = 64
DIL_PAIRS = ((128, 1), (512, 4), (2048, 16))
B_GROUP_HEADS = 4
B_V_DIM = MIX_WIDTH // B_GROUP_HEADS
MOBA_BLOCK = 256
MOBA_TOPK = 3
QBLOCK_C = 16
D_FF = -(-8 * D_MODEL // (3 * 256)) * 256
ALPHA = (2 * DEPTH) ** 0.25
BETA = (8 * DEPTH) ** -0.25
LN_EPS = 1e-5

WIDTH_A = 3 * MIX_WIDTH + IDX_HEADS * IDX_DIM + IDX_HEADS + IDX_DIM + MEM_WIDTH
WIDTH_BC = 3 * MIX_WIDTH + MEM_WIDTH

kernel_name = 'hybrid_dsa_dilated_moba_deepnorm'


def layer_norm(x, g, b):
    xf = x.astype(jnp.float32)
    mu = jnp.mean(xf, -1, keepdims=True)
    var = jnp.mean(jnp.square(xf - mu), -1, keepdims=True)
    return ((xf - mu) * lax.rsqrt(var + LN_EPS) * g + b).astype(x.dtype)


def split_cols(h, widths):
    cuts = [int(c) for c in np.cumsum(widths)[:-1]]
    return jnp.split(h, cuts, axis=-1)


def rope_tables(positions):
    half = ROPE_DIM // 2
    inv = ROPE_THETA ** (-jnp.arange(half, dtype=jnp.float32) / half)
    ang = positions.astype(jnp.float32)[..., None] * inv
    return jnp.cos(ang)[:, :, None, :], jnp.sin(ang)[:, :, None, :]


def apply_rope(x, cos, sin):
    half = ROPE_DIM // 2
    c, s = cos.astype(x.dtype), sin.astype(x.dtype)
    x1, x2, rest = x[..., :half], x[..., half:ROPE_DIM], x[..., ROPE_DIM:]
    return jnp.concatenate([x1 * c - x2 * s, x2 * c + x1 * s, rest], axis=-1)


def dsa_attention(q, k, v, q_idx, w_idx, k_idx):
    Bsz, S, H, dh = q.shape
    topk = min(TOPK_MAX, S // 4)
    key_pos = jnp.arange(S)
    scale = dh ** -0.5

    def block(i):
        t0 = i * QBLOCK_A
        qpos = t0 + jnp.arange(QBLOCK_A)
        qi = lax.dynamic_slice_in_dim(q_idx, t0, QBLOCK_A, axis=1)
        wi = lax.dynamic_slice_in_dim(w_idx, t0, QBLOCK_A, axis=1)
        dots = jnp.einsum('bqhd,bsd->bqhs', qi, k_idx).astype(jnp.float32)
        score = jnp.einsum('bqhs,bqh->bqs', jax.nn.relu(dots), wi.astype(jnp.float32))
        causal = key_pos[None, :] <= qpos[:, None]
        score = jnp.where(causal[None], score, -jnp.inf)
        _, sel = lax.top_k(score, topk)
        valid = sel <= qpos[None, :, None]
        ks = jax.vmap(lambda kb, ib: kb[ib])(k, sel)
        vs = jax.vmap(lambda vb, ib: vb[ib])(v, sel)
        qb = lax.dynamic_slice_in_dim(q, t0, QBLOCK_A, axis=1)
        logits = jnp.einsum('bqhd,bqkhd->bhqk', qb, ks).astype(jnp.float32) * scale
        logits = jnp.where(valid[:, None], logits, -jnp.inf)
        p = jax.nn.softmax(logits, axis=-1).astype(v.dtype)
        return jnp.einsum('bhqk,bqkhd->bqhd', p, vs)

    out = lax.map(block, jnp.arange(S // QBLOCK_A))
    return out.transpose(1, 0, 2, 3, 4).reshape(Bsz, S, H, dh)


def dilated_group(q, k, v, window, dilation):
    Bsz, S, G, dh = q.shape
    dv = v.shape[-1]
    n = window // dilation
    L = S // dilation
    nb = -(-L // n)
    Lp = nb * n

    def to_sub(a):
        a = a.reshape(Bsz, L, dilation, G, a.shape[-1]).transpose(0, 2, 1, 3, 4)
        a = jnp.pad(a, ((0, 0), (0, 0), (0, Lp - L), (0, 0), (0, 0)))
        return a.reshape(Bsz, dilation, nb, n, G, a.shape[-1])

    qs, ks, vs = to_sub(q), to_sub(k), to_sub(v)
    shift = ((0, 0), (0, 0), (1, 0), (0, 0), (0, 0), (0, 0))
    kk = jnp.concatenate([jnp.pad(ks[:, :, :-1], shift), ks], axis=3)
    vv = jnp.concatenate([jnp.pad(vs[:, :, :-1], shift), vs], axis=3)
    qi = jnp.arange(n)[:, None] + n
    ki = jnp.arange(2 * n)[None, :]
    dist = qi - ki
    ksub = jnp.arange(nb)[:, None, None] * n + ki[None] - n
    mask = ((dist >= 0) & (dist <= n))[None] & (ksub >= 0)
    logits = jnp.einsum('brnqgd,brnkgd->brngqk', qs, kk).astype(jnp.float32) * (dh ** -0.5)
    logits = jnp.where(mask[None, None, :, None], logits, -jnp.inf)
    lse = jax.nn.logsumexp(logits, axis=-1)
    p = jnp.exp(logits - lse[..., None]).astype(v.dtype)
    out = jnp.einsum('brngqk,brnkgd->brnqgd', p, vv)
    out = out.reshape(Bsz, dilation, Lp, G, dv)[:, :, :L]
    out = out.transpose(0, 2, 1, 3, 4).reshape(Bsz, S, G, dv)
    lse = lse.transpose(0, 1, 2, 4, 3).reshape(Bsz, dilation, Lp, G)[:, :, :L]
    lse = lse.transpose(0, 2, 1, 3).reshape(Bsz, S, G)
    return out, lse


def dilated_attention(q, k, v):
    outs, lses = [], []
    for g, (window, dilation) in enumerate(DIL_PAIRS):
        hs = slice(g * B_GROUP_HEADS, (g + 1) * B_GROUP_HEADS)
        o, l = dilated_group(q[:, :, hs], k[:, :, hs], v, window, dilation)
        outs.append(o)
        lses.append(l)
    wts = jax.nn.softmax(jnp.stack(lses, 0), axis=0).astype(v.dtype)
    return jnp.einsum('gbsh,gbshd->bshd', wts, jnp.stack(outs, 0))


def moba_attention(q, k, v):
    Bsz, S, H, dh = q.shape
    nblk = -(-S // MOBA_BLOCK)
    Sp = nblk * MOBA_BLOCK
    pad = lambda a: jnp.pad(a, ((0, 0), (0, Sp - S), (0, 0), (0, 0)))
    qp = pad(q)
    kbt = pad(k).reshape(Bsz, nblk, MOBA_BLOCK, H, dh).transpose(0, 3, 1, 2, 4)
    vbt = pad(v).reshape(Bsz, nblk, MOBA_BLOCK, H, dh).transpose(0, 3, 1, 2, 4)
    kmean = jnp.mean(kbt.astype(jnp.float32), axis=3)
    ksel = min(MOBA_TOPK, nblk)
    scale = dh ** -0.5
    bidx = jnp.arange(Bsz)[:, None, None, None]
    hidx = jnp.arange(H)[None, :, None, None]

    def block(i):
        t0 = i * QBLOCK_C
        own = t0 // MOBA_BLOCK
        qpos = t0 + jnp.arange(QBLOCK_C)
        qh = lax.dynamic_slice_in_dim(qp, t0, QBLOCK_C, axis=1).transpose(0, 2, 1, 3)
        gate = jnp.einsum('bhqd,bhnd->bhqn', qh.astype(jnp.float32), kmean)
        gate = jnp.where(jnp.arange(nblk) < own, gate, -jnp.inf)
        _, sel = lax.top_k(gate, ksel)
        valid = jnp.repeat(jnp.arange(ksel) < own, MOBA_BLOCK)
        kg = kbt[bidx, hidx, sel]
        vg = vbt[bidx, hidx, sel]
        ko = lax.dynamic_index_in_dim(kbt, own, axis=2, keepdims=False)
        vo = lax.dynamic_index_in_dim(vbt, own, axis=2, keepdims=False)
        kpos = own * MOBA_BLOCK + jnp.arange(MOBA_BLOCK)
        s_sel = jnp.einsum('bhqd,bhqnkd->bhqnk', qh, kg).astype(jnp.float32)
        s_sel = jnp.where(valid, s_sel.reshape(Bsz, H, QBLOCK_C, ksel * MOBA_BLOCK), -jnp.inf)
        s_own = jnp.einsum('bhqd,bhkd->bhqk', qh, ko).astype(jnp.float32)
        s_own = jnp.where(kpos[None, :] <= qpos[:, None], s_own, -jnp.inf)
        p = jax.nn.softmax(jnp.concatenate([s_sel, s_own], -1) * scale, axis=-1).astype(v.dtype)
        p_sel = p[..., :ksel * MOBA_BLOCK].reshape(Bsz, H, QBLOCK_C, ksel, MOBA_BLOCK)
        o = jnp.einsum('bhqnk,bhqnkd->bhqd', p_sel, vg) + jnp.einsum('bhqk,bhkd->bhqd', p[..., ksel * MOBA_BLOCK:], vo)
        return o.transpose(0, 2, 1, 3)

    out = lax.map(block, jnp.arange(Sp // QBLOCK_C))
    return out.transpose(1, 0, 2, 3, 4).reshape(Bsz, Sp, H, dh)[:, :S]


def memory_attention(q, mk, mv):
    logits = jnp.einsum('bshd,bnhd->bhsn', q, mk).astype(jnp.float32) * (HEAD_DIM ** -0.5)
    p = jax.nn.softmax(logits, axis=-1).astype(mv.dtype)
    return jnp.einsum('bhsn,bnhd->bshd', p, mv)


def swiglu(x, w_gate_up, w_down):
    g, u = jnp.split(x @ w_gate_up, 2, axis=-1)
    return (jax.nn.silu(g) * u) @ w_down


def setup_inputs(seed: int = 0) -> dict:
    key = jax.random.key(seed)
    ks = jax.random.split(key, 18)
    n_a, n_b, n_c = (DEPTH + 2) // 3, (DEPTH + 1) // 3, DEPTH // 3
    nrm = lambda k, shape, s: jax.random.normal(k, shape, jnp.float32) * s
    gain = lambda k, shape: 1.0 + nrm(k, shape, 0.02)
    offs = jax.random.randint(ks[2], (BATCH, 1), 0, POS_OFFSET_RANGE, dtype=jnp.int32)
    positions = offs + jnp.arange(SEQ, dtype=jnp.int32)[None, :]
    return {
        'x': nrm(ks[0], (BATCH, SEQ, D_MODEL), 1.0),
        'mem': nrm(ks[1], (BATCH, N_MEM, D_MODEL), 1.0),
        'positions': positions,
        'mem_ln_g': gain(ks[3], (D_MODEL,)),
        'mem_ln_b': nrm(ks[4], (D_MODEL,), 0.02),
        'w_in_a': nrm(ks[5], (n_a, D_MODEL, WIDTH_A), D_MODEL ** -0.5),
        'idx_kn_g': gain(ks[6], (n_a, IDX_DIM)),
        'idx_kn_b': nrm(ks[7], (n_a, IDX_DIM), 0.02),
        'w_in_b': nrm(ks[8], (n_b, D_MODEL, WIDTH_BC), D_MODEL ** -0.5),
        'w_in_c': nrm(ks[9], (n_c, D_MODEL, WIDTH_BC), D_MODEL ** -0.5),
        'w_mem_kv': nrm(ks[10], (DEPTH, D_MODEL, 2 * MEM_WIDTH), D_MODEL ** -0.5),
        'w_out': nrm(ks[11], (DEPTH, MIX_WIDTH + MEM_WIDTH, D_MODEL), BETA * (MIX_WIDTH + MEM_WIDTH) ** -0.5),
        'ln1_g': gain(ks[12], (DEPTH, D_MODEL)),
        'ln1_b': nrm(ks[13], (DEPTH, D_MODEL), 0.02),
        'w_gate_up': nrm(ks[14], (DEPTH, D_MODEL, 2 * D_FF), D_MODEL ** -0.5),
        'w_down': nrm(ks[15], (DEPTH, D_FF, D_MODEL), BETA * D_FF ** -0.5),
        'ln2_g': gain(ks[16], (DEPTH, D_MODEL)),
        'ln2_b': nrm(ks[17], (DEPTH, D_MODEL), 0.02),
    }


def reference(x, mem, positions, mem_ln_g, mem_ln_b, w_in_a, idx_kn_g, idx_kn_b, w_in_b, w_in_c,
              w_mem_kv, w_out, ln1_g, ln1_b, w_gate_up, w_down, ln2_g, ln2_b):
    Bsz, S, _ = x.shape
    cos, sin = rope_tables(positions)
    mem_n = layer_norm(mem, mem_ln_g, mem_ln_b)
    heads = lambda a, h, d: a.reshape(Bsz, S, h, d)
    for i in range(DEPTH):
        kind, j = i % N_MIXERS, i // N_MIXERS
        if kind == 0:
            q, k, v, qi, wi, ki, qm = split_cols(
                x @ w_in_a[j],
                (MIX_WIDTH, MIX_WIDTH, MIX_WIDTH, IDX_HEADS * IDX_DIM, IDX_HEADS, IDX_DIM, MEM_WIDTH))
            q = apply_rope(heads(q, N_HEADS, HEAD_DIM), cos, sin)
            k = apply_rope(heads(k, N_HEADS, HEAD_DIM), cos, sin)
            qi = apply_rope(heads(qi, IDX_HEADS, IDX_DIM), cos, sin)
            ki = apply_rope(layer_norm(ki, idx_kn_g[j], idx_kn_b[j])[:, :, None, :], cos, sin)[:, :, 0]
            wi = wi * (IDX_HEADS * IDX_DIM) ** -0.5
            mix = dsa_attention(q, k, heads(v, N_HEADS, HEAD_DIM), qi, wi, ki)
        elif kind == 1:
            q, k, v, qm = split_cols(x @ w_in_b[j], (MIX_WIDTH, MIX_WIDTH, MIX_WIDTH, MEM_WIDTH))
            q = apply_rope(heads(q, N_HEADS, HEAD_DIM), cos, sin)
            k = apply_rope(heads(k, N_HEADS, HEAD_DIM), cos, sin)
            mix = dilated_attention(q, k, heads(v, B_GROUP_HEADS, B_V_DIM))
        else:
            q, k, v, qm = split_cols(x @ w_in_c[j], (MIX_WIDTH, MIX_WIDTH, MIX_WIDTH, MEM_WIDTH))
            q = apply_rope(heads(q, N_HEADS, HEAD_DIM), cos, sin)
            k = apply_rope(heads(k, N_HEADS, HEAD_DIM), cos, sin)
            mix = moba_attention(q, k, heads(v, N_HEADS, HEAD_DIM))
        mk, mv = jnp.split(mem_n @ w_mem_kv[i], 2, axis=-1)
        mo = memory_attention(heads(qm, N_MEM_HEADS, HEAD_DIM),
                              mk.reshape(Bsz, N_MEM, N_MEM_HEADS, HEAD_DIM),
                              mv.reshape(Bsz, N_MEM, N_MEM_HEADS, HEAD_DIM))
        mixed = jnp.concatenate([mix.reshape(Bsz, S, MIX_WIDTH), mo.reshape(Bsz, S, MEM_WIDTH)], -1) @ w_out[i]
        x = layer_norm(ALPHA * x + mixed, ln1_g[i], ln1_b[i])
        x = layer_norm(ALPHA * x + swiglu(x, w_gate_up[i], w_down[i]), ln2_g[i], ln2_b[i])
    return x
```

```python
import numpy as np
import ml_dtypes
import concourse.bass as bass
import concourse.mybir as mybir
from concourse.bass_utils import run_bass_kernel_spmd
from contextlib import ExitStack

F32 = mybir.dt.float32
BF16 = mybir.dt.bfloat16
I32 = mybir.dt.int32
ALU = mybir.AluOpType
AF = mybir.ActivationFunctionType
AX = mybir.AxisListType

SEQ = 4096
DM = 1024
NT = SEQ // 128
NG = SEQ // 512
DFF = 2816
NFC = DFF // 128
DEPTH = 4
ALPHA = (2 * DEPTH) ** 0.25
LN_EPS = 1e-5
NEG = -30000.0
BIG = 1.0e30
WIDTH_A = 3144
WIDTH_BC = 2560
TOPK = 256
NBIS = 14
FP8 = mybir.dt.float8e5
NEG8 = -32768.0

ENGS = ['pe', 'act', 'dve', 'pool', 'sp']


class Op:
    __slots__ = ('eng', 'fn', 'deps', 'needed', 'dma', 'sig', 'slot', 'target', 'prev_target')

    def __init__(self, eng, fn, dma):
        self.eng = eng
        self.fn = fn
        self.dma = dma
        self.deps = []
        self.needed = False
        self.sig = 0
        self.slot = -1
        self.target = 0
        self.prev_target = 0


class Sched:
    def __init__(self, nc, n_dma_slots=32):
        self.nc = nc
        self.ops = {e: [] for e in ENGS}
        self.last_w = {}
        self.rd_eng = {}
        self.rd_dma = {}
        self.fence = {e: [] for e in ENGS}
        self.n_dma_slots = n_dma_slots
        self.qslots = {'sp': list(range(0, n_dma_slots // 2)), 'act': list(range(0, n_dma_slots // 2)),
                       'pool': list(range(n_dma_slots // 2, n_dma_slots))}
        self.slot_uses = [0] * n_dma_slots
        self.next_slot = {'sp': 0, 'act': 0, 'pool': 0}
        self.all_dma = []

    def add(self, eng, fn, reads=(), writes=(), dma=False):
        op = Op(eng, fn, dma)
        deps = {}
        excl = [r for r in reads if isinstance(r, str) and r.startswith('ps')]
        if excl:
            writes = list(writes) + excl
        for r in reads:
            w = self.last_w.get(r)
            if w is not None:
                deps[id(w)] = w
        for r in writes:
            w = self.last_w.get(r)
            if w is not None:
                deps[id(w)] = w
            for rd in self.rd_eng.get(r, {}).values():
                deps[id(rd)] = rd
            for rd in self.rd_dma.get(r, ()):
                deps[id(rd)] = rd
        for f in self.fence[eng]:
            deps[id(f)] = f
        self.fence[eng] = []
        for d in deps.values():
            if d.eng == 'pe' and eng == 'pe' and not d.dma and not dma:
                continue
            op.deps.append(d)
            d.needed = True
        for r in reads:
            if dma:
                self.rd_dma.setdefault(r, []).append(op)
            else:
                self.rd_eng.setdefault(r, {})[eng] = op
        for r in writes:
            self.last_w[r] = op
            self.rd_eng[r] = {}
            self.rd_dma[r] = []
        if dma:
            qs = self.qslots[eng]
            key = 'pool' if eng == 'pool' else 'sp'
            k = qs[self.next_slot[key] % len(qs)]
            self.next_slot[key] += 1
            op.slot = k
            op.prev_target = 16 * self.slot_uses[k]
            self.slot_uses[k] += 1
            op.target = 16 * self.slot_uses[k]
            op.needed = True
            self.all_dma.append(op)
        self.ops[eng].append(op)
        return op

    def barrier(self):
        lasts = []
        for e in ENGS:
            for op in reversed(self.ops[e]):
                if not op.dma:
                    lasts.append(op)
                    break
        lasts += self.all_dma[-self.n_dma_slots:]
        for e in ENGS:
            self.fence[e] = list(lasts)
        for d in lasts:
            d.needed = True

    def emit(self, final_wait_eng='sp'):
        nc = self.nc
        with ExitStack() as es:
            esem = {e: es.enter_context(nc.semaphore('s_' + e)) for e in ENGS}
            dsem = [es.enter_context(nc.semaphore('d_%d' % k)) for k in range(self.n_dma_slots)]
            for e in ENGS:
                c = 0
                for op in self.ops[e]:
                    if not op.dma and op.needed:
                        c += 1
                        op.sig = c
            block = es.enter_context(nc.Block())

            def run(ename, eobj):
                waited = {}

                def wait(key, sem, val):
                    if val <= 0 or waited.get(key, 0) >= val:
                        return
                    eobj.wait_ge(sem, val)
                    waited[key] = val

                for op in self.ops[ename]:
                    for d in op.deps:
                        if d.dma:
                            wait(('d', d.slot), dsem[d.slot], d.target)
                        else:
                            wait(('e', d.eng), esem[d.eng], d.sig)
                    if op.dma:
                        wait(('d', op.slot), dsem[op.slot], op.prev_target)
                        op.fn(eobj).then_inc(dsem[op.slot], 16)
                    else:
                        ins = op.fn(eobj)
                        if op.needed:
                            ins.then_inc(esem[ename], 1)
                if ename == final_wait_eng:
                    for k in range(self.n_dma_slots):
                        wait(('d', k), dsem[k], 16 * self.slot_uses[k])
                    for e2 in ENGS:
                        if e2 != ename:
                            n = sum(1 for o in self.ops[e2] if o.sig)
                            wait(('e', e2), esem[e2], n)

            @block.tensor
            def _(e):
                run('pe', e)

            @block.scalar
            def _(e):
                run('act', e)

            @block.vector
            def _(e):
                run('dve', e)

            @block.gpsimd
            def _(e):
                run('pool', e)

            @block.sync
            def _(e):
                run('sp', e)


class T:
    __slots__ = ('ap', 'res')

    def __init__(self, ap, res):
        self.ap = ap
        self.res = res if isinstance(res, (list, tuple)) else [res]

    def __getitem__(self, k):
        return T(self.ap[k], self.res)

    def r(self, pattern, **kw):
        return T(self.ap.rearrange(pattern, **kw), self.res)

    def named(self, res):
        return T(self.ap, res)


def _res(*ts):
    out = []
    for t in ts:
        if isinstance(t, T):
            out += t.res
    return out


def _a(x):
    return x.ap if isinstance(x, T) else x


class Carver:
    def __init__(self, arena_ap, prefix):
        self.a = arena_ap
        self.off = 0
        self.prefix = prefix
        self.n = 0
        self.cap = arena_ap.shape[1]

    def _take(self, nwords):
        nwords = (nwords + 7) // 8 * 8
        o = self.off
        self.off += nwords
        assert self.off <= self.cap, ('arena overflow', self.prefix, self.off, self.cap)
        self.n += 1
        return self.a[:, o:o + nwords], '%s_%d' % (self.prefix, self.n)

    def f32(self, n, parts=128, name=None):
        ap, nm = self._take(n)
        return T(ap[0:parts, 0:n], name or nm)

    def i32(self, n, parts=128):
        ap, nm = self._take(n)
        return T(ap[0:parts, 0:n].bitcast(I32), nm)

    def fp8(self, n, parts=128, name=None):
        ap, nm = self._take((n + 3) // 4)
        return T(ap.bitcast(FP8)[0:parts, 0:n], name or nm)

    def bf16(self, n, parts=128, name=None):
        ap, nm = self._take((n + 1) // 2)
        return T(ap.bitcast(BF16)[0:parts, 0:n], name or nm)


class MK:
    def __init__(self, nc):
        self.nc = nc
        self.S = Sched(nc)
        self.phase_id = 0
        rem = nc.sbuf_bytes_remaining
        self.PERS_WORDS = 6144
        self.ARENA_WORDS = (rem - 1024) // 4 - self.PERS_WORDS
        self.ARENA_WORDS = self.ARENA_WORDS // 8 * 8
        pers = nc.alloc_sbuf_tensor("pers", [128, self.PERS_WORDS], F32)
        arena = nc.alloc_sbuf_tensor("arena", [128, self.ARENA_WORDS], F32)
        self.pers = Carver(pers.ap(), 'pers')
        self.arena_ap = arena.ap()
        pst = nc.alloc_psum_tensor("psum", [128, 4096], F32)
        psap = pst.ap()
        self.ps = [T(psap[:, b * 512:(b + 1) * 512], 'ps%d' % b) for b in range(8)]
        self.psap = psap
        self.rr = {}

    def ps2(self, b):
        return T(self.psap[:, b * 512:(b + 2) * 512], ['ps%d' % b, 'ps%d' % (b + 1)])

    def phase(self, name):
        self.S.barrier()
        self.phase_id += 1
        return Carver(self.arena_ap, 'ph%d%s' % (self.phase_id, name))

    def u(self):
        self.ucnt = getattr(self, 'ucnt', 0) + 1
        return 'u%d' % self.ucnt

    def rot(self, key, items):
        i = self.rr.get(key, 0)
        self.rr[key] = i + 1
        return items[i % len(items)]

    def mm(self, out, lhsT, rhs, start=True, stop=True, skip=False):
        o, l, r = out.ap, lhsT.ap, rhs.ap
        if skip:
            f = lambda e: e.matmul(o, lhsT=l, rhs=r, start=start, stop=stop, skip_group_check=True)
        else:
            f = lambda e: e.matmul(o, lhsT=l, rhs=r, start=start, stop=stop)
        return self.S.add('pe', f, reads=_res(lhsT, rhs), writes=_res(out))

    def tr(self, out, in_, ident):
        o, i, d = out.ap, in_.ap, ident.ap
        return self.S.add('pe', lambda e: e.transpose(o, i, d), reads=_res(in_, ident), writes=_res(out))

    def act(self, out, in_, func, scale=1.0, bias=None, accum=None):
        o, i = out.ap, in_.ap
        kw = {}
        if bias is not None:
            kw['bias'] = _a(bias)
        if accum is not None:
            kw['accum_out'] = accum.ap
        sc = _a(scale)
        return self.S.add('act', lambda e: e.activation(out=o, in_=i, func=func, scale=sc, **kw),
                          reads=_res(in_, scale, bias), writes=_res(out, accum))

    def tsc(self, eng, out, in0, s1, s2, op0, op1=None, accum=None):
        o, i = out.ap, in0.ap
        a1, a2 = _a(s1), _a(s2)
        kw = {}
        if op1 is not None:
            kw['op1'] = op1
        if accum is not None:
            kw['accum_out'] = accum.ap
        if o.dtype == FP8:
            kw['saturate'] = False
        return self.S.add(eng, lambda e: e.tensor_scalar(out=o, in0=i, scalar1=a1, scalar2=a2, op0=op0, **kw),
                          reads=_res(in0, s1, s2), writes=_res(out, accum))

    def tt(self, eng, out, in0, in1, op):
        o, a, b = out.ap, in0.ap, in1.ap
        return self.S.add(eng, lambda e: e.tensor_tensor(out=o, in0=a, in1=b, op=op),
                          reads=_res(in0, in1), writes=_res(out))

    def stt(self, eng, out, in0, scalar, in1, op0, op1):
        o, a, b = out.ap, in0.ap, in1.ap
        s = _a(scalar)
        return self.S.add(eng, lambda e: e.scalar_tensor_tensor(out=o, in0=a, scalar=s, in1=b, op0=op0, op1=op1),
                          reads=_res(in0, in1, scalar), writes=_res(out))

    def cp(self, eng, out, in_):
        o, i = out.ap, in_.ap
        if eng == 'act':
            return self.S.add('act', lambda e: e.copy(out=o, in_=i), reads=_res(in_), writes=_res(out))
        if o.dtype == FP8:
            return self.S.add(eng, lambda e: e.tensor_copy(out=o, in_=i, saturate=False), reads=_res(in_), writes=_res(out))
        return self.S.add(eng, lambda e: e.tensor_copy(out=o, in_=i), reads=_res(in_), writes=_res(out))

    def memset(self, eng, out, val):
        o = out.ap
        return self.S.add(eng, lambda e: e.memset(o, val), writes=_res(out))

    def red(self, eng, out, in_, op, axis=AX.X):
        o, i = out.ap, in_.ap
        return self.S.add(eng, lambda e: e.tensor_reduce(out=o, in_=i, axis=axis, op=op),
                          reads=_res(in_), writes=_res(out))

    def dma(self, q, out, in_):
        o, i = out.ap, in_.ap
        return self.S.add(q, lambda e: e.dma_start(out=o, in_=i), reads=_res(in_), writes=_res(out), dma=True)


class Builder(MK):
    def __init__(self, nc, layers, dbg=False):
        super().__init__(nc)
        self.layers = layers
        self.dbg = dbg
        dt = nc.dram_tensor
        ext = lambda n, s, d=F32: dt(n, s, d, kind="ExternalInput").ap()
        self.x = ext("x", [SEQ, DM])
        self.mem = ext("mem", [256, DM])
        self.positions = ext("positions", [1, SEQ], I32)
        self.mem_ln_g = ext("mem_ln_g", [1, DM])
        self.mem_ln_b = ext("mem_ln_b", [1, DM])
        self.w_in_a = ext("w_in_a", [2, DM, WIDTH_A])
        self.idx_kn_g = ext("idx_kn_g", [2, 64, 1])
        self.idx_kn_b = ext("idx_kn_b", [2, 64, 1])
        self.w_in_b = ext("w_in_b", [1, DM, WIDTH_BC])
        self.w_in_c = ext("w_in_c", [1, DM, WIDTH_BC])
        self.w_mem_kv = ext("w_mem_kv", [DEPTH, DM, 512])
        self.w_out = ext("w_out", [DEPTH, DM, DM])
        self.ln1_g = ext("ln1_g", [DEPTH, DM])
        self.ln1_b = ext("ln1_b", [DEPTH, DM])
        self.w_gate_up = ext("w_gate_up", [DEPTH, DM, 2 * DFF])
        self.w_down = ext("w_down", [DEPTH, DFF, DM])
        self.ln2_g = ext("ln2_g", [DEPTH, DM])
        self.ln2_b = ext("ln2_b", [DEPTH, DM])
        self.c_ident = ext("c_ident", [128, 128], BF16)
        self.c_tri = ext("c_tri", [128, 128], BF16)
        self.c_r128 = ext("c_r128", [128, 128], BF16)
        self.c_causneg = ext("c_causneg", [128, 128])
        self.c_invcol = ext("c_invcol", [128, 1])
        self.c_ee = ext("c_ee", [16, 16 * 128], BF16)
        self.c_negown = ext("c_negown", [128, NT * 16])
        self.c_dmask = ext("c_dmask", [128, 256], BF16)
        self.y = dt("y", [SEQ, DM], F32, kind="ExternalOutput").ap()
        itn = lambda n, s, d=BF16: dt(n, s, d, kind=("ExternalOutput" if (dbg and n in dbg) else "Internal")).ap()
        self.XT = itn("XT", [8, 128, SEQ])
        self.X1T = itn("X1T", [8, 128, SEQ])
        self.XF = [itn("XFa", [SEQ, DM], F32), itn("XFb", [SEQ, DM], F32)]
        self.X1F = itn("X1F", [SEQ, DM], F32)
        self.QT = itn("QT", [12 * 64, SEQ])
        self.KT = itn("KT", [12 * 64, SEQ])
        self.QMT = itn("QMT", [4 * 64, SEQ])
        self.QIT = itn("QIT", [8 * 64, SEQ])
        self.KIT = itn("KIT", [64, SEQ])
        self.VAP = itn("VAP", [12, 128, NT, 65])
        self.VD = itn("VD", [SEQ, 772])
        self.ND = itn("ND", [3, SEQ, 772], F32)
        self.AT = itn("AT", [NFC, 128, SEQ])
        self.COSD = itn("COSD", [128, SEQ], F32)
        self.SIND = itn("SIND", [128, SEQ], F32)
        D = lambda ap, res: T(ap, res)
        self.D = D
        p = self.pers
        self.ident = p.bf16(128)
        self.tri = p.bf16(128)
        self.r128 = p.bf16(128)
        self.causneg = p.f32(128)
        self.invcol = p.f32(1)
        self.ee = p.bf16(16 * 128, parts=16)
        self.negown = p.f32(NT * 16)
        self.dmask = p.bf16(256)
        self.negpi = p.f32(1)
        self.epsc = p.f32(1)
        self.ones64 = p.f32(64, parts=64)
        self.onesbf = p.bf16(128)
        self.ident8 = p.fp8(128)
        self.pow2 = p.f32(NBIS)
        self.memnT = p.bf16(8 * 256)
        self.mkT = p.bf16(4 * 256, parts=64)
        self.mvaug = p.bf16(2 * 4 * 65)
        self.wi_all = p.f32(NT * 8)
        self.absw = p.f32(NT * 8)
        self.sgn = p.f32(NT * 8)
        self.kng = p.f32(1, parts=64)
        self.knb = p.f32(1, parts=64)

    def prologue(self):
        for (t, src) in [(self.ident, self.c_ident), (self.tri, self.c_tri), (self.r128, self.c_r128),
                         (self.causneg, self.c_causneg), (self.invcol, self.c_invcol), (self.ee, self.c_ee),
                         (self.negown, self.c_negown), (self.dmask, self.c_dmask)]:
            self.dma('sp', t, T(src, 'const'))
        self.memset('dve', self.negpi, -float(np.pi))
        self.memset('dve', self.epsc, LN_EPS)
        self.memset('dve', self.ones64, 1.0 / 64.0)
        self.memset('dve', self.onesbf, 1.0)
        self.memset('dve', self.mvaug, 1.0)
        for it in range(NBIS):
            self.memset('dve', self.pow2[:, it:it + 1], 2.0 ** (-(it + 1)))
        c = self.phase('rope')
        self.cp('dve', self.ident8, self.ident)
        H = 2048
        posi = c.i32(H)
        ang = c.f32(H)
        u = c.f32(H)
        kf = c.f32(H)
        tab = c.f32(H)
        for half in range(2):
            sl = slice(half * H, (half + 1) * H)
            self.dma('sp', posi, T(self.positions[:, sl].partition_broadcast(128), 'const'))
            self.cp('dve', ang, posi)
            self.tsc('dve', ang, ang, self.invcol[:, 0:1], None, ALU.mult)
            for (off, dst) in [(0.5, self.SIND), (0.75, self.COSD)]:
                self.tsc('dve', u, ang, 1.0 / (2.0 * np.pi), off, ALU.mult, ALU.add)
                self.cp('dve', posi, u)
                self.cp('dve', kf, posi)
                self.tt('dve', u, u, kf, ALU.subtract)
                self.tsc('dve', kf, u, 0.0, None, ALU.is_lt)
                self.tt('dve', u, u, kf, ALU.add)
                self.act(tab, u, AF.Sin, scale=2.0 * float(np.pi), bias=self.negpi[:, 0:1])
                self.dma('sp', T(dst[:, sl], self.u()), tab)
        c = self.phase('memln')
        g_t = c.f32(DM)
        b_t = c.f32(DM)
        self.dma('sp', g_t, T(self.mem_ln_g.partition_broadcast(128), 'const'))
        self.dma('sp', b_t, T(self.mem_ln_b.partition_broadcast(128), 'const'))
        lnb = self.ln_bufs(c)
        for mt in range(2):
            z = lnb['z']
            self.dma('sp', z, T(self.mem[mt * 128:(mt + 1) * 128, :], 'const'))
            xTt = self.ln_core(lnb, z, g_t, b_t, None)
            self.cp('dve', self.memnT.r("p (c t) -> p c t", c=8)[:, :, mt * 128:(mt + 1) * 128], xTt)
        c = self.phase('x2xT')
        xin = [c.f32(DM), c.f32(DM)]
        xb = [c.bf16(DM), c.bf16(DM)]
        xt = [c.bf16(DM), c.bf16(DM)]
        for tt_ in range(NT):
            a, b, o = xin[tt_ % 2], xb[tt_ % 2], xt[tt_ % 2]
            self.dma('sp', a, T(self.x[tt_ * 128:(tt_ + 1) * 128, :], 'const'))
            self.cp('act', b, a)
            pst = self.rot('pstr', [self.ps[4], self.ps[5]])
            psb = T(pst.ap.bitcast(BF16), pst.res)
            for cc in range(8):
                self.tr(psb[:, cc * 128:(cc + 1) * 128], b[:, cc * 128:(cc + 1) * 128], self.ident)
            self.cp('dve', o, psb[:, 0:1024])
            self.dma('pool', T(self.XT[:, :, tt_ * 128:(tt_ + 1) * 128].rearrange("c p t -> p c t"), self.u()),
                     o.r("p (c t) -> p c t", c=8))

    def ln_bufs(self, c, nz=0):
        return dict(z=c.f32(DM), z2=[c.f32(DM) for _ in range(nz)], st=c.f32(12), mv=c.f32(2), sd=c.f32(1), rs=c.f32(1),
                    xo=[c.f32(DM), c.f32(DM)], xb=c.bf16(DM), xTt=[c.bf16(DM), c.bf16(DM)])

    def ln_core(self, lnb, z, g_t, b_t, out_f32_dram, trps=None):
        st, mv, sd, rs = lnb['st'], lnb['mv'], lnb['sd'], lnb['rs']
        xo = self.rot(('xo', id(lnb)), lnb['xo'])
        xb = lnb['xb']
        xTt = self.rot(('xTt', id(lnb)), lnb['xTt'])
        za, zb, sa, sb_, ma = z.ap, z.ap, st.ap, st.ap, mv.ap
        self.S.add('dve', lambda e: e.bn_stats(out=sa[:, 0:6], in_=za[:, 0:512]), reads=_res(z), writes=_res(st))
        self.S.add('dve', lambda e: e.bn_stats(out=sa[:, 6:12], in_=za[:, 512:1024]), reads=_res(z), writes=_res(st))
        self.S.add('dve', lambda e: e.bn_aggr(out=ma, in_=sa), reads=_res(st), writes=_res(mv))
        self.act(sd, mv[:, 1:2], AF.Sqrt, bias=self.epsc[:, 0:1])
        sda, rsa = sd.ap, rs.ap
        self.S.add('dve', lambda e: e.reciprocal(out=rsa, in_=sda), reads=_res(sd), writes=_res(rs))
        self.tsc('dve', z, z, mv[:, 0:1], rs[:, 0:1], ALU.subtract, ALU.mult)
        self.tt('dve', z, z, g_t, ALU.mult)
        self.tt('pool', xo, z, b_t, ALU.add)
        if out_f32_dram is not None:
            self.dma('pool', out_f32_dram, xo)
        self.cp('act', xb, xo)
        pst = self.rot('pstr', trps or [self.ps[4], self.ps[5]])
        psb = T(pst.ap.bitcast(BF16), pst.res)
        for cc in range(8):
            self.tr(psb[:, cc * 128:(cc + 1) * 128], xb[:, cc * 128:(cc + 1) * 128], self.ident)
        self.cp('act', xTt, psb[:, 0:1024])
        return xTt.r("p (c t) -> p c t", c=8)

    def load_w_chunks(self, dst_bf, src_dram_fn, nchunks, ncols, stage, eng='pool'):
        for cc in range(nchunks):
            st = self.rot(('wst', id(stage[0])), stage)
            self.dma('sp', st[:, 0:ncols], src_dram_fn(cc))
            self.cp(eng, dst_bf[:, cc, :], st[:, 0:ncols])

    def phase_inproj(self, li, kind, w_in, j):
        c = self.phase('A')
        xT = c.bf16(8 * SEQ, name='xT').r("p (c t) -> p c t", c=8)
        for cc in range(8):
            self.dma('sp', xT[:, cc, :], T(self.XT[cc], 'const'))
        COS = c.f32(SEQ)
        SIN = c.f32(SEQ)
        self.dma('sp', COS, T(self.COSD, 'const'))
        self.dma('sp', SIN, T(self.SIND, 'const'))
        wst = [c.f32(1024), c.f32(1024)]
        wbf = [c.bf16(1024), c.bf16(1024)]
        qsb = [c.bf16(512), c.bf16(512)]
        t1 = [c.f32(512), c.f32(512)]
        t2 = [c.f32(512), c.f32(512)]
        osb = [c.bf16(512), c.bf16(512), c.bf16(512)]
        w_pcn = w_in.rearrange("(c p) n -> p c n", p=128)

        jobs = []

        def fm_prep(col0, M):
            ws = self.rot('wst', wst)
            wb = self.rot('wbf', wbf)
            wsv = ws[:, 0:8 * M].r("p (c n) -> p c n", c=8)
            wbv = wb[:, 0:8 * M].r("p (c n) -> p c n", c=8)
            self.dma('sp', wsv, T(w_pcn[:, :, col0:col0 + M], 'const'))
            self.cp('pool', wbv, wsv)
            return wbv

        def fm_job(col0, M, mode, dst):
            jobs.append((col0, M, mode, dst))

        def run_jobs():
            nxt = fm_prep(jobs[0][0], jobs[0][1])
            for i, (col0, M, mode, dst) in enumerate(jobs):
                wbv = nxt
                if i + 1 < len(jobs):
                    nxt = fm_prep(jobs[i + 1][0], jobs[i + 1][1])
                fm_run(wbv, M, mode, dst)

        def fm_run(wbv, M, mode, dst):
            for G in range(NG):
                gs = slice(G * 512, (G + 1) * 512)
                ps = self.rot('psA', self.ps[0:3])
                for cc in range(8):
                    self.mm(ps[0:M, :], wbv[:, cc, :], xT[:, cc, gs], start=(cc == 0), stop=(cc == 7))
                o = self.rot('osb', osb)
                if mode == 'plain':
                    self.cp('act', o[0:M, :], ps[0:M, :])
                elif mode == 'rope':
                    q = self.rot('qsb', qsb)
                    a1 = self.rot('t1', t1)
                    a2 = self.rot('t2', t2)
                    self.cp('act', q[0:M, :], ps[0:M, :])
                    pr = self.rot('psR', self.ps[3:5])
                    self.mm(pr[0:M, :], self.r128[0:M, 0:M], q[0:M, :])
                    self.tt('dve', a1[0:M, :], ps[0:M, :], COS[0:M, gs], ALU.mult)
                    self.tt('dve', a2[0:M, :], pr[0:M, :], SIN[0:M, gs], ALU.mult)
                    self.tt('pool', o[0:M, :], a1[0:M, :], a2[0:M, :], ALU.add)
                self.dma('pool', T(dst[:, gs], self.u()), o[0:M, :])

        def fm_region(col0, ncols, mode, dst):
            for i in range(ncols // 128):
                fm_job(col0 + i * 128, 128, mode, dst[i * 128:(i + 1) * 128, :])

        if self.stop == 'A00':
            return
        if self.stop in ('A01', 'A02', 'A03'):
            ws = self.rot('wst', wst)
            wb = self.rot('wbf', wbf)
            wsv = ws[:, 0:8 * 128].r("p (c n) -> p c n", c=8)
            wbv = wb[:, 0:8 * 128].r("p (c n) -> p c n", c=8)
            self.dma('sp', wsv, T(w_pcn[:, :, 0:128], 'const'))
            self.cp('pool', wbv, wsv)
            if self.stop == 'A01':
                return
            gs = slice(0, 512)
            ps = self.ps[0]
            for cc in range(8):
                self.mm(ps, wbv[:, cc, :], xT[:, cc, gs], start=(cc == 0), stop=(cc == 7))
            o = osb[0]
            if self.stop == 'A02':
                self.cp('act', o, ps)
            else:
                q = qsb[0]
                self.cp('act', q, ps)
                pr = self.ps[3]
                self.mm(pr, self.r128, q)
                self.tt('dve', t1[0], ps, COS[:, gs], ALU.mult)
                self.tt('dve', t2[0], pr, SIN[:, gs], ALU.mult)
                self.tt('pool', o, t1[0], t2[0], ALU.add)
            self.dma('pool', T(self.QT[0:128, gs], self.u()), o)
            return
        fm_region(0, 768, 'rope', self.QT)
        fm_region(768, 768, 'rope', self.KT)
        if kind == 0:
            fm_region(2304, 512, 'rope', self.QIT)
            fm_region(2888, 256, 'plain', self.QMT)
            run_jobs()
            self.inproj_ki(c, w_pcn, xT, COS, SIN, j)
        else:
            fm_region(2304, 256, 'plain', self.QMT)
            run_jobs()
        if self.stop == 'A2':
            return

        NV = 776 if kind == 0 else 768
        wv = c.bf16(8 * NV).r("p (c n) -> p c n", c=8)
        vst = [c.f32(NV), c.f32(NV)]
        for cc in range(8):
            st = self.rot('vst', vst)
            self.dma('sp', st[:, 0:768], T(w_in[cc * 128:(cc + 1) * 128, 1536:2304], 'const'))
            if kind == 0:
                self.dma('sp', st[:, 768:776], T(w_in[cc * 128:(cc + 1) * 128, 2816:2824], 'const'))
            self.cp('pool', wv[:, cc, :], st[:, 0:NV])
        nvh, dv = (4, 192) if kind == 1 else (12, 64)
        vaug = [c.bf16(nvh * (dv + 1)), c.bf16(nvh * (dv + 1))]
        for v in vaug:
            self.memset('pool', v, 1.0)
        for tt_ in range(NT):
            ts = slice(tt_ * 128, (tt_ + 1) * 128)
            psv = self.ps2(6)
            for (n0, n1) in [(0, 512), (512, NV)]:
                for cc in range(8):
                    self.mm(psv[:, n0:n1], xT[:, cc, ts], wv[:, cc, n0:n1], start=(cc == 0), stop=(cc == 7))
            va = self.rot('vaug', vaug)
            va3 = va.r("p (h e) -> p h e", h=nvh)
            self.cp('act', va3[:, :, 0:dv], psv[:, 0:768].r("p (h d) -> p h d", h=nvh))
            if kind == 0:
                self.cp('dve', self.wi_all[:, tt_ * 8:(tt_ + 1) * 8], psv[:, 768:776])
            if kind == 1:
                self.dma('pool', T(self.VD[ts, :], self.u()), va)
            else:
                self.dma('pool', T(self.VAP[:, :, tt_, :].rearrange("h p e -> p h e"), self.u()), va3)

        if self.stop == 'A3':
            return
        wm = self.w_mem_kv[li]
        wmb = c.bf16(8 * 512).r("p (c n) -> p c n", c=8)
        mst = [c.f32(512), c.f32(512)]
        for cc in range(8):
            st = self.rot('mst', mst)
            self.dma('sp', st, T(wm[cc * 128:(cc + 1) * 128, :], 'const'))
            self.cp('pool', wmb[:, cc, :], st)
        memn = self.memnT.r("p (c t) -> p c t", c=8)
        mk3 = self.mkT.r("p (h t) -> p h t", h=4)
        for h in range(4):
            ps = self.rot('psA', self.ps[0:3])
            for cc in range(8):
                self.mm(ps[0:64, 0:256], wmb[:, cc, h * 64:(h + 1) * 64], memn[:, cc, :], start=(cc == 0), stop=(cc == 7))
            self.cp('act', mk3[:, h, :], ps[0:64, 0:256])
        mv4 = self.mvaug.r("p (t h e) -> p t h e", t=2, h=4)
        for mt in range(2):
            ps = self.rot('psA', self.ps[0:3])
            for cc in range(8):
                self.mm(ps[:, 0:256], memn[:, cc, mt * 128:(mt + 1) * 128], wmb[:, cc, 256:512], start=(cc == 0), stop=(cc == 7))
            self.cp('act', mv4[:, mt, :, 0:64], ps[:, 0:256].r("p (h d) -> p h d", h=4))

    def inproj_ki(self, c, w_pcn, xT, COS, SIN, j):
        self.dma('sp', self.kng, T(self.idx_kn_g[j], 'const'))
        self.dma('sp', self.knb, T(self.idx_kn_b[j], 'const'))
        ws = c.f32(8 * 64).r("p (c n) -> p c n", c=8)
        wb = c.bf16(8 * 64).r("p (c n) -> p c n", c=8)
        self.dma('sp', ws, T(w_pcn[:, :, 2824:2888], 'const'))
        self.cp('pool', wb, ws)
        P = 64
        xs = c.f32(512, parts=P)
        x2 = c.f32(512, parts=P)
        msb = c.f32(512, parts=P)
        var = c.f32(512, parts=P)
        xn = c.f32(512, parts=P)
        xnb = c.bf16(512, parts=P)
        a1 = c.f32(512, parts=P)
        a2 = c.f32(512, parts=P)
        o = [c.bf16(512, parts=P), c.bf16(512, parts=P)]
        for G in range(NG):
            gs = slice(G * 512, (G + 1) * 512)
            ps = self.rot('psA', self.ps[0:3])
            for cc in range(8):
                self.mm(ps[0:P, :], wb[:, cc, :], xT[:, cc, gs], start=(cc == 0), stop=(cc == 7))
            self.cp('act', xs, ps[0:P, :])
            self.act(x2, ps[0:P, :], AF.Square)
            pm = self.ps[3]
            pe2 = self.ps[4]
            self.mm(pm[0:P, :], self.ones64, xs)
            self.mm(pe2[0:P, :], self.ones64, x2)
            self.act(msb, pm[0:P, :], AF.Square)
            self.tt('dve', var, pe2[0:P, :], msb, ALU.subtract)
            self.act(var, var, AF.Sqrt, bias=self.epsc[0:P, 0:1])
            va = var.ap
            self.S.add('dve', lambda e, va=va: e.reciprocal(out=va, in_=va), reads=_res(var), writes=_res(var))
            self.tt('dve', xn, xs, pm[0:P, :], ALU.subtract)
            self.tt('dve', xn, xn, var, ALU.mult)
            self.tsc('dve', xn, xn, self.kng[:, 0:1], self.knb[:, 0:1], ALU.mult, ALU.add)
            self.cp('act', xnb, xn)
            pr = self.ps[5]
            self.mm(pr[0:P, :], self.r128[0:P, 0:P], xnb)
            self.tt('dve', a1, xn, COS[0:P, gs], ALU.mult)
            self.tt('dve', a2, pr[0:P, :], SIN[0:P, gs], ALU.mult)
            oo = self.rot('kio', o)
            self.tt('pool', oo, a1, a2, ALU.add)
            self.dma('pool', T(self.KIT[:, gs], self.u()), oo)

    def attn_head(self, G, kT, qTg, vaugh, nkt, causal, bias_fn, mix3, col0, pT, scale=0.125):
        po = self.rot('psO', [self.ps[3], self.ps[4]])
        pend = []
        first = [True]

        def pv(kt, c0, pt):
            for jj in range(c0, 4):
                self.mm(po[:, jj * 65:(jj + 1) * 65], pt[:, jj * 128:(jj + 1) * 128], vaugh[:, kt, :],
                        start=first[0], stop=(kt == nkt - 1 and jj == 3))
                first[0] = False

        for kt in range(nkt):
            c0 = max(0, kt - 4 * G) if causal else 0
            psS = self.rot('psS', self.ps[0:3])
            cols = slice(c0 * 128, 512)
            biases = bias_fn(kt, c0, psS) if bias_fn is not None else []
            self.mm(psS[:, cols], kT[:, kt * 128:(kt + 1) * 128], qTg[:, cols], start=True, stop=(len(biases) == 0))
            for bi, (o, l, r) in enumerate(biases):
                self.mm(o, l, r, start=False, stop=(bi == len(biases) - 1))
            pt = self.rot('pT', pT)
            self.act(pt[:, cols], psS[:, cols], AF.Exp, scale=scale)
            pend.append((kt, c0, pt))
            if len(pend) > 1:
                pv(*pend.pop(0))
        while pend:
            pv(*pend.pop(0))
        po3 = po[:, 0:260].r("p (j e) -> p j e", j=4)
        rden = self.rot('rden', self.rdens)
        lnd = self.rot('lnd', self.lnds)
        self.act(lnd, po3[:, :, 64], AF.Ln)
        self.act(rden, lnd, AF.Exp, scale=-1.0)
        for jj in range(4):
            self.act(mix3[:, jj, col0:col0 + 64], po3[:, jj, 0:64], AF.Copy, scale=rden[:, jj:jj + 1])

    def phase_attn(self, li, kind):
        c = self.phase('B')
        wout = c.bf16(8 * DM).r("p (c n) -> p c n", c=8)
        xin = [c.f32(DM), c.f32(DM)]
        self.load_w_chunks(wout, lambda cc: T(self.w_out[li][cc * 128:(cc + 1) * 128, :], 'const'), 8, DM, xin)
        g_t = c.f32(DM)
        b_t = c.f32(DM)
        self.dma('sp', g_t, T(self.ln1_g[li:li + 1, :].partition_broadcast(128), 'const'))
        self.dma('sp', b_t, T(self.ln1_b[li:li + 1, :].partition_broadcast(128), 'const'))
        lnb = self.ln_bufs(c)
        mix = c.bf16(4 * DM)
        mix3 = mix.r("p (j n) -> p j n", j=4)
        mixT = [c.bf16(DM), c.bf16(DM)]
        self.rdens = [c.f32(4), c.f32(4)]
        self.lnds = [c.f32(4), c.f32(4)]
        xres = self.x if self.is_first else self.XF[li % 2]
        if kind == 1:
            self.dilated_all(c, li)
        else:
            kTb = [c.bf16(SEQ, parts=64), c.bf16(SEQ, parts=64)]
            vab = [c.bf16(NT * 65), c.bf16(NT * 65)]
            qTb = [c.bf16(512, parts=64), c.bf16(512, parts=64)]
            pT = [c.bf16(512) for _ in range(4)]
            if kind == 0:
                I = c.f32(SEQ)
                Mb2 = [[c.fp8(SEQ) for _ in range(4)] for _ in range(2)]
                kiT = c.bf16(SEQ, parts=64)
                qiT = c.bf16(8 * 512, parts=64).r("p (h t) -> p h t", h=8)
                rbuf = [c.bf16(512) for _ in range(3)]
                diag = [c.bf16(8 * 128), c.bf16(8 * 128)]
                sm = dict(lo=c.f32(1), hi=c.f32(1), d0=c.f32(1), cth=c.f32(1), cnt=c.f32(1), step=c.f32(1), dtab=c.f32(NBIS))
                self.dma('sp', kiT, T(self.KIT, 'const'))
                self.act(self.absw, self.wi_all, AF.Abs)
                self.tsc('dve', self.sgn, self.wi_all, 0.0, None, ALU.is_ge)
                self.tsc('dve', self.sgn, self.sgn, 2.0, -1.0, ALU.mult, ALU.add)
            else:
                kmean = c.f32(16, parts=64)
                kmb = c.bf16(16, parts=64)
                self.memset('dve', kmb, 0.0)
                gm = c.f32(64)
                sel = c.f32(64)
                m8 = c.f32(8)
                selb = c.bf16(64)
                selbT = [c.bf16(512, parts=16), c.bf16(512, parts=16)]
        for G in range(NG):
            gs = slice(G * 512, (G + 1) * 512)
            nkeys = (G + 1) * 512
            nkt = (G + 1) * 4
            if kind == 0:
                Mb = Mb2[G % 2]

                def idx_group(Gn):
                    gsn = slice(Gn * 512, (Gn + 1) * 512)
                    self.dma('sp', qiT, T(self.QIT[:, gsn].rearrange("(h d) t -> d h t", h=8), 'const'))

                if G == 0:
                    idx_group(0)
                    for jq in range(4):
                        self.dsa_index(0, jq, I, Mb2[0][jq], kiT, qiT, rbuf, diag, sm)
                if G + 1 < NG:
                    idx_group(G + 1)
            if kind != 1:
                def prep(h):
                    kT = self.rot('kTb', kTb)
                    va = self.rot('vab', vab)
                    qTg = self.rot('qTb', qTb)
                    self.dma('sp', kT[:, 0:nkeys], T(self.KT[h * 64:(h + 1) * 64, 0:nkeys], 'const'))
                    va3 = va.r("p (t e) -> p t e", t=NT)
                    self.dma('sp', va3[:, 0:nkt, :], T(self.VAP[h, :, 0:nkt, :], 'const'))
                    self.dma('sp', qTg, T(self.QT[h * 64:(h + 1) * 64, gs], 'const'))
                    sT = None
                    if kind == 2:
                        sT = self.rot('selbT', selbT)
                        self.moba_gate(G, h, kT, qTg, kmean, kmb, gm, sel, m8, selb, sT)
                    return kT, va3, qTg, sT

                nxt = prep(0)
                for h in range(12):
                    kT, va3, qTg, sT = nxt
                    if kind == 0 and G + 1 < NG and h % 3 == 0:
                        self.dsa_index(G + 1, h // 3, I, Mb2[(G + 1) % 2][h // 3], kiT, qiT, rbuf, diag, sm)
                    if h + 1 < 12:
                        nxt = prep(h + 1)
                    if kind == 0:
                        def bias_fn(kt, c0, psS, Mb=Mb):
                            return [(psS[:, jj * 128:(jj + 1) * 128], Mb[jj][:, kt * 128:(kt + 1) * 128], self.ident8)
                                    for jj in range(c0, 4)]
                    else:
                        def bias_fn(kt, c0, psS, sT=sT):
                            n = kt // 2
                            out = []
                            run = None
                            for jj in range(c0, 4):
                                qt = 4 * G + jj
                                if qt // 2 == n:
                                    if run is not None:
                                        out.append(run)
                                        run = None
                                    if qt == kt:
                                        out.append((psS[:, jj * 128:(jj + 1) * 128], self.tri, self.ident))
                                else:
                                    if run is None:
                                        run = [jj, jj + 1]
                                    else:
                                        run[1] = jj + 1
                            if run is not None:
                                out.append(run)
                            res = []
                            for it in out:
                                if isinstance(it, list):
                                    cs = slice(it[0] * 128, it[1] * 128)
                                    res.append((psS[:, cs], self.ee[:, n * 128:(n + 1) * 128], sT[:, cs]))
                                else:
                                    res.append(it)
                            return res
                    self.attn_head(G, kT, qTg, va3, nkt, True, bias_fn, mix3, h * 64, pT)
            if kind == 1:
                qTb = self.dil_qTb
                pT = self.dil_pT
            mk3 = self.mkT.r("p (h t) -> p h t", h=4)
            mv4 = self.mvaug.r("p (t h e) -> p t h e", t=2, h=4)
            for m in range(4):
                qTg = self.rot('qTb', qTb)
                self.dma('sp', qTg, T(self.QMT[m * 64:(m + 1) * 64, gs], 'const'))
                self.attn_head(G, mk3[:, m, :], qTg, mv4[:, :, m, :], 2, False, None, mix3, 768 + m * 64, pT)
            for jq in range(4):
                tt_ = 4 * G + jq
                ts = slice(tt_ * 128, (tt_ + 1) * 128)
                if kind == 1:
                    self.dilated_combine(tt_, mix3[:, jq, 0:768])
                xi = self.rot('xin', xin)
                self.dma('sp', xi, T(xres[ts, :], 'const'))
                pst = self.ps[5]
                psb = T(pst.ap.bitcast(BF16), pst.res)
                for cc in range(8):
                    self.tr(psb[:, cc * 128:(cc + 1) * 128], mix3[:, jq, cc * 128:(cc + 1) * 128], self.ident)
                mT = self.rot('mixT', mixT)
                self.cp('act', mT, psb[:, 0:1024])
                mT3 = mT.r("p (c t) -> p c t", c=8)
                py = self.ps2(6)
                for nh in range(2):
                    for cc in range(8):
                        self.mm(py[:, nh * 512:(nh + 1) * 512], mT3[:, cc, :], wout[:, cc, nh * 512:(nh + 1) * 512],
                                start=(cc == 0), stop=(cc == 7))
                z = lnb['z']
                self.stt('dve', z, xi, ALPHA, py, ALU.mult, ALU.add)
                xTt = self.ln_core(lnb, z, g_t, b_t, T(self.X1F[ts, :], self.u()), trps=[self.ps[5]])
                self.dma('pool', T(self.X1T[:, :, ts].rearrange("c p t -> p c t"), self.u()), xTt)

    def dsa_index(self, G, jq, I, Mbj, kiT, qiT, rbuf, diag, sm):
        qt = 4 * G + jq
        nk = (qt + 1) * 128
        dg = self.rot('diag', diag).r("p (h k) -> p h k", h=8)
        for h in range(8):
            self.tsc('pool', dg[:, h, :], self.ident, self.sgn[:, qt * 8 + h:qt * 8 + h + 1], None, ALU.mult)
        nch = (nk + 511) // 512
        LAG = 2
        for ch in range(nch):
            w = min(512, nk - ch * 512)
            pa = self.rot('psO', [self.ps[3], self.ps[4]])
            rr = [None] * 8
            for h in range(8 + LAG):
                if h < 8:
                    pd = self.rot('psS', self.ps[0:3])
                    self.mm(pd[:, 0:w], qiT[:, h, jq * 128:(jq + 1) * 128], kiT[:, ch * 512:ch * 512 + w])
                    rr[h] = self.rot('rbuf', rbuf)
                    self.act(rr[h][:, 0:w], pd[:, 0:w], AF.Relu, scale=self.absw[:, qt * 8 + h:qt * 8 + h + 1])
                if h >= LAG:
                    hh = h - LAG
                    self.mm(pa[:, 0:w], dg[:, hh, :], rr[hh][:, 0:w], start=(hh == 0), stop=(hh == 7))
            self.cp('dve', I[:, ch * 512:ch * 512 + w], pa[:, 0:w])
        dsl = slice(qt * 128, (qt + 1) * 128)
        lo, hi, d0, cth, cnt, step = sm['lo'], sm['hi'], sm['d0'], sm['cth'], sm['cnt'], sm['step']
        if qt >= 2:
            self.red('dve', lo, I[:, 0:nk], ALU.min)
            self.tt('pool', I[:, dsl], I[:, dsl], self.causneg, ALU.add)
            self.red('dve', hi, I[:, 0:nk], ALU.max)
            self.tt('dve', d0, hi, lo, ALU.subtract)
            dtab = sm['dtab']
            self.tsc('dve', dtab, self.pow2, d0[:, 0:1], None, ALU.mult)
            self.stt('dve', lo, d0, 0.5, lo, ALU.mult, ALU.add)
            for it in range(NBIS):
                self.tsc('dve', Mbj[:, 0:nk], I[:, 0:nk], lo[:, 0:1], 0.0, ALU.is_ge, ALU.add, accum=cnt)
                self.tsc('dve', step, cnt, TOPK - 0.5, 0.5, ALU.is_ge, ALU.subtract)
                self.stt('dve', lo, dtab[:, it:it + 1], step[:, 0:1], lo, ALU.mult, ALU.add)
        else:
            self.tt('pool', I[:, dsl], I[:, dsl], self.causneg, ALU.add)
            self.memset('dve', lo, -1.0e29)
        self.tsc('dve', Mbj[:, 0:nk], I[:, 0:nk], lo[:, 0:1], NEG8, ALU.is_lt, ALU.mult)

    def moba_gate(self, G, h, kT, qTg, kmean, kmb, gm, sel, m8, selb, sT):
        nb = 2 * G + 2
        self.red('dve', kmean[:, 0:nb], kT[:, 0:nb * 256].r("p (n k) -> p n k", k=256), ALU.add)
        self.tsc('dve', kmb[:, 0:nb], kmean[:, 0:nb], 1.0 / 256.0, None, ALU.mult)
        pg = self.rot('psS', self.ps[0:3])
        for jj in range(4):
            self.mm(pg[:, jj * 16:(jj + 1) * 16], qTg[:, jj * 128:(jj + 1) * 128], kmb, start=(jj == 0), stop=(jj == 3))
        self.tt('dve', gm, pg[:, 0:64], self.negown[:, G * 64:(G + 1) * 64], ALU.add)
        for jj in range(4):
            ma, ga = m8.ap, gm.ap[:, jj * 16:(jj + 1) * 16]
            self.S.add('dve', lambda e, ma=ma, ga=ga: e.max(out=ma, in_=ga), reads=_res(gm), writes=_res(m8))
            self.tsc('dve', sel[:, jj * 16:(jj + 1) * 16], gm[:, jj * 16:(jj + 1) * 16], m8[:, 2:3], None, ALU.is_ge)
        self.tsc('dve', selb, sel, -NEG, NEG, ALU.mult, ALU.add)
        pst = self.ps[5]
        psb = T(pst.ap.bitcast(BF16), pst.res)
        for jj in range(4):
            self.tr(psb[0:16, jj * 128:(jj + 1) * 128], selb[:, jj * 16:(jj + 1) * 16], self.ident)
        self.cp('act', sT, psb[0:16, 0:512])

    def dilated_all(self, c, li):
        qd = [c.bf16(SEQ, parts=64) for _ in range(4)]
        kd = [c.bf16(SEQ, parts=64) for _ in range(4)]
        vsub = [c.bf16(772) for _ in range(3)]
        pTo = [c.bf16(512), c.bf16(512)]
        pTp = [c.bf16(512), c.bf16(512)]
        osb = [c.f32(772), c.f32(772)]
        dm4 = c.bf16(1024)
        self.dil_qTb = [c.bf16(512, parts=64), c.bf16(512, parts=64)]
        self.dil_pT = [c.bf16(512) for _ in range(4)]
        self.dil_nd = [c.f32(772) for _ in range(3)]
        for hh in range(4):
            self.cp('dve', dm4[:, hh * 128:(hh + 1) * 128], self.dmask[:, 0:128])
            self.cp('dve', dm4[:, 512 + hh * 128:512 + (hh + 1) * 128], self.dmask[:, 128:256])
        mU, mL = dm4[:, 0:512], dm4[:, 512:1024]
        for g, (window, d) in enumerate(((128, 1), (512, 4), (2048, 16))):
            for hh in range(4):
                h = 4 * g + hh
                self.dma('sp', qd[hh], T(self.QT[h * 64:(h + 1) * 64, :], 'const'))
                self.dma('sp', kd[hh], T(self.KT[h * 64:(h + 1) * 64, :], 'const'))
            L = SEQ // d
            vdr = self.VD.rearrange("(j d) e -> d j e", d=d)
            ndr = self.ND[g].rearrange("(j d) e -> d j e", d=d)
            for r in range(d):
                vprev = None
                for jt in range(L // 128):
                    js = slice(jt * 128, (jt + 1) * 128)
                    jp = slice((jt - 1) * 128, jt * 128)
                    vown = self.rot('vsub', vsub)
                    self.dma('sp', vown, T(vdr[r, js, :], 'const'))
                    pso = self.rot('psS', self.ps[0:4])
                    psp = self.rot('psS', self.ps[0:4]) if jt > 0 else None
                    for hh in range(4):
                        qv = qd[hh].r("p (j d) -> p d j", d=d)
                        kv = kd[hh].r("p (j d) -> p d j", d=d)
                        self.mm(pso[:, hh * 128:(hh + 1) * 128], kv[:, r, js], qv[:, r, js], start=(hh == 0), stop=(hh == 3))
                    if jt > 0:
                        for hh in range(4):
                            qv = qd[hh].r("p (j d) -> p d j", d=d)
                            kv = kd[hh].r("p (j d) -> p d j", d=d)
                            self.mm(psp[:, hh * 128:(hh + 1) * 128], kv[:, r, jp], qv[:, r, js], start=(hh == 0), stop=(hh == 3))
                    po_ = self.rot('pTo', pTo)
                    self.act(po_, pso, AF.Exp, scale=0.125)
                    self.tt('dve', po_, po_, mL, ALU.mult)
                    if jt > 0:
                        pp_ = self.rot('pTp', pTp)
                        self.act(pp_, psp, AF.Exp, scale=0.125)
                        self.tt('pool', pp_, pp_, mU, ALU.mult)
                    py = self.ps2(6)
                    for hh in range(4):
                        oc = slice(hh * 256, hh * 256 + 193)
                        st = (hh % 2 == 0)
                        sp_ = (hh % 2 == 1)
                        if jt > 0:
                            self.mm(py[:, oc], pp_[:, hh * 128:(hh + 1) * 128], vprev[:, hh * 193:(hh + 1) * 193], start=st, stop=False)
                            self.mm(py[:, oc], po_[:, hh * 128:(hh + 1) * 128], vown[:, hh * 193:(hh + 1) * 193], start=False, stop=sp_)
                        else:
                            self.mm(py[:, oc], po_[:, hh * 128:(hh + 1) * 128], vown[:, hh * 193:(hh + 1) * 193], start=st, stop=sp_)
                    ob = self.rot('dosb', osb)
                    self.cp('act', ob.r("p (h e) -> p h e", h=4), py.r("p (h e) -> p h e", h=4)[:, :, 0:193])
                    self.dma('pool', T(ndr[r, js, :], self.u()), ob)
                    vprev = vown
        self.S.barrier()

    def dilated_combine(self, tt_, mix_out):
        ts = slice(tt_ * 128, (tt_ + 1) * 128)
        nd = self.dil_nd
        for g in range(3):
            self.dma('sp', nd[g], T(self.ND[g][ts, :], 'const'))
        self.tt('pool', nd[0], nd[0], nd[1], ALU.add)
        self.tt('pool', nd[0], nd[0], nd[2], ALU.add)
        a3 = nd[0].r("p (h e) -> p h e", h=4)
        rden = self.rot('rden', self.rdens)
        ra, da = rden.ap, a3.ap[:, :, 192]
        self.S.add('dve', lambda e: e.reciprocal(out=ra, in_=da), reads=_res(nd[0]), writes=_res(rden))
        for hh in range(4):
            self.tsc('dve', mix_out[:, hh * 192:(hh + 1) * 192], a3[:, hh, 0:192], rden[:, hh:hh + 1], None, ALU.mult)

    def phase_ffn1(self, li):
        c = self.phase('C')
        xT = c.bf16(8 * SEQ, name='xT1').r("p (c t) -> p c t", c=8)
        for cc in range(8):
            self.dma('sp', xT[:, cc, :], T(self.X1T[cc], 'const'))
        wst = [c.f32(1024) for _ in range(4)]
        wbf = [c.bf16(1024) for _ in range(4)]
        sg = [c.f32(512), c.f32(512)]
        aT = [c.bf16(512) for _ in range(3)]
        wgu = self.w_gate_up[li].rearrange("(c p) n -> p c n", p=128)
        def prepw(f):
            wb = []
            for col0 in (f * 128, DFF + f * 128):
                ws = self.rot('wst', wst).r("p (c n) -> p c n", c=8)
                w_ = self.rot('wbf', wbf).r("p (c n) -> p c n", c=8)
                self.dma('sp', ws, T(wgu[:, :, col0:col0 + 128], 'const'))
                self.cp('pool', w_, ws)
                wb.append(w_)
            return wb

        nxtw = prepw(0)
        for f in range(NFC):
            wb = nxtw
            if f + 1 < NFC:
                nxtw = prepw(f + 1)
            for G in range(NG):
                gs = slice(G * 512, (G + 1) * 512)
                pair = self.rot('psFF', [(0, 1), (2, 3), (4, 5)])
                pg, pu = self.ps[pair[0]], self.ps[pair[1]]
                for cc in range(8):
                    self.mm(pg, wb[0][:, cc, :], xT[:, cc, gs], start=(cc == 0), stop=(cc == 7))
                for cc in range(8):
                    self.mm(pu, wb[1][:, cc, :], xT[:, cc, gs], start=(cc == 0), stop=(cc == 7))
                s_ = self.rot('sg', sg)
                a_ = self.rot('aT', aT)
                self.act(s_, pg, AF.Silu)
                self.tt('dve', a_, s_, pu, ALU.mult)
                self.dma('pool', T(self.AT[f, :, gs], self.u()), a_)

    def phase_ffn2(self, li, last):
        c = self.phase('D')
        wd = c.bf16(NFC * DM).r("p (c n) -> p c n", c=NFC)
        xin = [c.f32(DM), c.f32(DM)]
        self.load_w_chunks(wd, lambda cc: T(self.w_down[li][cc * 128:(cc + 1) * 128, :], 'const'), NFC, DM, xin)
        g_t = c.f32(DM)
        b_t = c.f32(DM)
        self.dma('sp', g_t, T(self.ln2_g[li:li + 1, :].partition_broadcast(128), 'const'))
        self.dma('sp', b_t, T(self.ln2_b[li:li + 1, :].partition_broadcast(128), 'const'))
        lnb = self.ln_bufs(c, nz=1)
        aTg = [c.bf16(NFC * 512), c.bf16(NFC * 512)]
        for G in range(NG):
            gs = slice(G * 512, (G + 1) * 512)
            ag = self.rot('aTg', aTg).r("p (f t) -> p f t", f=NFC)
            self.dma('sp', ag, T(self.AT[:, :, gs].rearrange("f p t -> p f t"), 'const'))
            for jq in range(4):
                tt_ = 4 * G + jq
                ts = slice(tt_ * 128, (tt_ + 1) * 128)
                xi = self.rot('xin', xin)
                self.dma('sp', xi, T(self.X1F[ts, :], 'const'))
                py = self.rot('pyD', [self.ps2(0), self.ps2(2), self.ps2(6)])
                for nh in range(2):
                    for f in range(NFC):
                        self.mm(py[:, nh * 512:(nh + 1) * 512], ag[:, f, jq * 128:(jq + 1) * 128],
                                wd[:, f, nh * 512:(nh + 1) * 512], start=(f == 0), stop=(f == NFC - 1))
                z = self.rot('zD', [lnb['z']] + lnb['z2'])
                self.stt('dve', z, xi, ALPHA, py, ALU.mult, ALU.add)
                dst = self.y if last else self.XF[(li + 1) % 2]
                xTt = self.ln_core(lnb, z, g_t, b_t, T(dst[ts, :], self.u()))
                if not last:
                    self.dma('pool', T(self.XT[:, :, ts].rearrange("c p t -> p c t"), self.u()), xTt)

    def build_all(self, stop=None):
        self.stop = stop
        self.prologue()
        n = len(self.layers)
        for idx, li in enumerate(self.layers):
            if stop == 'P':
                break
            kind, j = li % 3, li // 3
            self.is_first = (idx == 0)
            w_in = (self.w_in_a, self.w_in_b, self.w_in_c)[kind][j]
            self.phase_inproj(li, kind, w_in, j)
            if stop and stop[0] == 'A':
                break
            self.phase_attn(li, kind)
            if stop == 'B':
                break
            self.phase_ffn1(li)
            if stop == 'C':
                break
            self.phase_ffn2(li, idx == n - 1)
        self.S.emit()


def make_consts():
    bf = ml_dtypes.bfloat16
    q = np.arange(128)[:, None]
    k = np.arange(128)[None, :]
    c = {}
    c['c_ident'] = np.eye(128, dtype=np.float32).astype(bf)
    c['c_tri'] = np.where(k <= q, 0.0, NEG).astype(np.float32).astype(bf)
    c['c_causneg'] = np.where(k <= q, 0.0, -BIG).astype(np.float32)
    r = np.zeros((128, 128), np.float32)
    inv = (500000.0 ** (-np.arange(8, dtype=np.float32) / 8.0)).astype(np.float32)
    invcol = np.zeros((128, 1), np.float32)
    for base in (0, 64):
        for d in range(8):
            r[base + d + 8, base + d] = -1.0
            r[base + d, base + d + 8] = 1.0
            invcol[base + d, 0] = inv[d]
            invcol[base + d + 8, 0] = inv[d]
    c['c_r128'] = r.astype(bf)
    c['c_invcol'] = invcol
    ee = np.zeros((16, 16, 128), np.float32)
    for n in range(16):
        ee[n, n, :] = 1.0
    c['c_ee'] = ee.reshape(16, 16 * 128).astype(bf)
    no = np.zeros((128, NT, 16), np.float32)
    for qt in range(NT):
        for n in range(16):
            if not (n < qt // 2):
                no[:, qt, n] = -BIG
    c['c_negown'] = no.reshape(128, NT * 16)
    kk = np.arange(128)[:, None]
    qq = np.arange(128)[None, :]
    dm = np.concatenate([(kk >= qq), (kk <= qq)], axis=1).astype(np.float32)
    c['c_dmask'] = dm.astype(bf)
    return c


_CACHE = {}


def get_nc(layers, dbg=False, stop=None):
    key = (tuple(layers), str(dbg), stop)
    if key not in _CACHE:
        nc = bass.Bass("TRN2", target_bir_lowering=False)
        b = Builder(nc, list(layers), dbg)
        b.build_all(stop)
        _CACHE[key] = nc
    return _CACHE[key]


def make_in_maps(inputs, n_cores=8):
    f = lambda a: np.ascontiguousarray(np.asarray(a, dtype=np.float32))
    consts = make_consts()
    shared = {
        'mem_ln_g': f(inputs['mem_ln_g']).reshape(1, DM),
        'mem_ln_b': f(inputs['mem_ln_b']).reshape(1, DM),
        'w_in_a': f(inputs['w_in_a']),
        'idx_kn_g': f(inputs['idx_kn_g']).reshape(2, 64, 1),
        'idx_kn_b': f(inputs['idx_kn_b']).reshape(2, 64, 1),
        'w_in_b': f(inputs['w_in_b']),
        'w_in_c': f(inputs['w_in_c']),
        'w_mem_kv': f(inputs['w_mem_kv']),
        'w_out': f(inputs['w_out']),
        'ln1_g': f(inputs['ln1_g']), 'ln1_b': f(inputs['ln1_b']),
        'w_gate_up': f(inputs['w_gate_up']),
        'w_down': f(inputs['w_down']),
        'ln2_g': f(inputs['ln2_g']), 'ln2_b': f(inputs['ln2_b']),
    }
    shared.update(consts)
    x = f(inputs['x'])
    mem = f(inputs['mem'])
    pos = np.ascontiguousarray(np.asarray(inputs['positions'], dtype=np.int32))
    maps = []
    for b in range(n_cores):
        m = dict(shared)
        m['x'] = x[b]
        m['mem'] = mem[b]
        m['positions'] = pos[b:b + 1]
        maps.append(m)
    return maps


def kernel(**inputs):
    nc = get_nc(range(DEPTH))
    maps = make_in_maps(inputs, 8)
    res = run_bass_kernel_spmd(nc, maps, core_ids=list(range(8)))
    return np.stack([np.asarray(r['y'], dtype=np.float32) for r in res.results], axis=0)
```

```python
import numpy as np
import ml_dtypes
import concourse.bass as bass
import concourse.mybir as mybir
from concourse.bass_utils import run_bass_kernel_spmd
from contextlib import ExitStack

F32 = mybir.dt.float32
BF16 = mybir.dt.bfloat16
I32 = mybir.dt.int32
ALU = mybir.AluOpType
AF = mybir.ActivationFunctionType
AX = mybir.AxisListType

SEQ = 4096
DM = 1024
NT = SEQ // 128
NG = SEQ // 512
DFF = 2816
NFC = DFF // 128
DEPTH = 4
ALPHA = (2 * DEPTH) ** 0.25
LN_EPS = 1e-5
NEG = -30000.0
BIG = 1.0e30
WIDTH_A = 3144
WIDTH_BC = 2560
TOPK = 256
NBIS = 14
FP8 = mybir.dt.float8e5
NEG8 = -32768.0

ENGS = ['pe', 'act', 'dve', 'pool', 'sp']


class Op:
    __slots__ = ('eng', 'fn', 'deps', 'needed', 'dma', 'sig', 'slot', 'target', 'prev_target')

    def __init__(self, eng, fn, dma):
        self.eng = eng
        self.fn = fn
        self.dma = dma
        self.deps = []
        self.needed = False
        self.sig = 0
        self.slot = -1
        self.target = 0
        self.prev_target = 0


class Sched:
    def __init__(self, nc, n_dma_slots=32):
        self.nc = nc
        self.ops = {e: [] for e in ENGS}
        self.last_w = {}
        self.rd_eng = {}
        self.rd_dma = {}
        self.fence = {e: [] for e in ENGS}
        self.n_dma_slots = n_dma_slots
        self.qslots = {'sp': list(range(0, n_dma_slots // 2)), 'act': list(range(0, n_dma_slots // 2)),
                       'pool': list(range(n_dma_slots // 2, n_dma_slots))}
        self.slot_uses = [0] * n_dma_slots
        self.next_slot = {'sp': 0, 'act': 0, 'pool': 0}
        self.all_dma = []

    def add(self, eng, fn, reads=(), writes=(), dma=False):
        op = Op(eng, fn, dma)
        deps = {}
        excl = [r for r in reads if isinstance(r, str) and r.startswith('ps')]
        if excl:
            writes = list(writes) + excl
        for r in reads:
            w = self.last_w.get(r)
            if w is not None:
                deps[id(w)] = w
        for r in writes:
            w = self.last_w.get(r)
            if w is not None:
                deps[id(w)] = w
            for rd in self.rd_eng.get(r, {}).values():
                deps[id(rd)] = rd
            for rd in self.rd_dma.get(r, ()):
                deps[id(rd)] = rd
        for f in self.fence[eng]:
            deps[id(f)] = f
        self.fence[eng] = []
        for d in deps.values():
            if d.eng == 'pe' and eng == 'pe' and not d.dma and not dma:
                continue
            op.deps.append(d)
            d.needed = True
        for r in reads:
            if dma:
                self.rd_dma.setdefault(r, []).append(op)
            else:
                self.rd_eng.setdefault(r, {})[eng] = op
        for r in writes:
            self.last_w[r] = op
            self.rd_eng[r] = {}
            self.rd_dma[r] = []
        if dma:
            qs = self.qslots[eng]
            key = 'pool' if eng == 'pool' else 'sp'
            k = qs[self.next_slot[key] % len(qs)]
            self.next_slot[key] += 1
            op.slot = k
            op.prev_target = 16 * self.slot_uses[k]
            self.slot_uses[k] += 1
            op.target = 16 * self.slot_uses[k]
            op.needed = True
            self.all_dma.append(op)
        self.ops[eng].append(op)
        return op

    def barrier(self):
        lasts = []
        for e in ENGS:
            for op in reversed(self.ops[e]):
                if not op.dma:
                    lasts.append(op)
                    break
        lasts += self.all_dma[-self.n_dma_slots:]
        for e in ENGS:
            self.fence[e] = list(lasts)
        for d in lasts:
            d.needed = True

    def emit(self, final_wait_eng='sp'):
        nc = self.nc
        with ExitStack() as es:
            esem = {e: es.enter_context(nc.semaphore('s_' + e)) for e in ENGS}
            dsem = [es.enter_context(nc.semaphore('d_%d' % k)) for k in range(self.n_dma_slots)]
            for e in ENGS:
                c = 0
                for op in self.ops[e]:
                    if not op.dma and op.needed:
                        c += 1
                        op.sig = c
            block = es.enter_context(nc.Block())

            def run(ename, eobj):
                waited = {}

                def wait(key, sem, val):
                    if val <= 0 or waited.get(key, 0) >= val:
                        return
                    eobj.wait_ge(sem, val)
                    waited[key] = val

                for op in self.ops[ename]:
                    for d in op.deps:
                        if d.dma:
                            wait(('d', d.slot), dsem[d.slot], d.target)
                        else:
                            wait(('e', d.eng), esem[d.eng], d.sig)
                    if op.dma:
                        wait(('d', op.slot), dsem[op.slot], op.prev_target)
                        op.fn(eobj).then_inc(dsem[op.slot], 16)
                    else:
                        ins = op.fn(eobj)
                        if op.needed:
                            ins.then_inc(esem[ename], 1)
                if ename == final_wait_eng:
                    for k in range(self.n_dma_slots):
                        wait(('d', k), dsem[k], 16 * self.slot_uses[k])
                    for e2 in ENGS:
                        if e2 != ename:
                            n = sum(1 for o in self.ops[e2] if o.sig)
                            wait(('e', e2), esem[e2], n)

            @block.tensor
            def _(e):
                run('pe', e)

            @block.scalar
            def _(e):
                run('act', e)

            @block.vector
            def _(e):
                run('dve', e)

            @block.gpsimd
            def _(e):
                run('pool', e)

            @block.sync
            def _(e):
                run('sp', e)


class T:
    __slots__ = ('ap', 'res')

    def __init__(self, ap, res):
        self.ap = ap
        self.res = res if isinstance(res, (list, tuple)) else [res]

    def __getitem__(self, k):
        return T(self.ap[k], self.res)

    def r(self, pattern, **kw):
        return T(self.ap.rearrange(pattern, **kw), self.res)

    def named(self, res):
        return T(self.ap, res)


def _res(*ts):
    out = []
    for t in ts:
        if isinstance(t, T):
            out += t.res
    return out


def _a(x):
    return x.ap if isinstance(x, T) else x


class Carver:
    def __init__(self, arena_ap, prefix):
        self.a = arena_ap
        self.off = 0
        self.prefix = prefix
        self.n = 0
        self.cap = arena_ap.shape[1]

    def _take(self, nwords):
        nwords = (nwords + 7) // 8 * 8
        o = self.off
        self.off += nwords
        assert self.off <= self.cap, ('arena overflow', self.prefix, self.off, self.cap)
        self.n += 1
        return self.a[:, o:o + nwords], '%s_%d' % (self.prefix, self.n)

    def f32(self, n, parts=128, name=None):
        ap, nm = self._take(n)
        return T(ap[0:parts, 0:n], name or nm)

    def i32(self, n, parts=128):
        ap, nm = self._take(n)
        return T(ap[0:parts, 0:n].bitcast(I32), nm)

    def fp8(self, n, parts=128, name=None):
        ap, nm = self._take((n + 3) // 4)
        return T(ap.bitcast(FP8)[0:parts, 0:n], name or nm)

    def bf16(self, n, parts=128, name=None):
        ap, nm = self._take((n + 1) // 2)
        return T(ap.bitcast(BF16)[0:parts, 0:n], name or nm)


class MK:
    def __init__(self, nc):
        self.nc = nc
        self.S = Sched(nc)
        self.phase_id = 0
        rem = nc.sbuf_bytes_remaining
        self.PERS_WORDS = 6144
        self.ARENA_WORDS = (rem - 1024) // 4 - self.PERS_WORDS
        self.ARENA_WORDS = self.ARENA_WORDS // 8 * 8
        pers = nc.alloc_sbuf_tensor("pers", [128, self.PERS_WORDS], F32)
        arena = nc.alloc_sbuf_tensor("arena", [128, self.ARENA_WORDS], F32)
        self.pers = Carver(pers.ap(), 'pers')
        self.arena_ap = arena.ap()
        pst = nc.alloc_psum_tensor("psum", [128, 4096], F32)
        psap = pst.ap()
        self.ps = [T(psap[:, b * 512:(b + 1) * 512], 'ps%d' % b) for b in range(8)]
        self.psap = psap
        self.rr = {}

    def ps2(self, b):
        return T(self.psap[:, b * 512:(b + 2) * 512], ['ps%d' % b, 'ps%d' % (b + 1)])

    def flush_pend(self):
        pend = getattr(self, 'pend', [])
        self.pend = []
        for (part2, store) in pend:
            store(part2())

    def phase(self, name):
        self.flush_pend()
        self.S.barrier()
        self.phase_id += 1
        return Carver(self.arena_ap, 'ph%d%s' % (self.phase_id, name))

    def u(self):
        self.ucnt = getattr(self, 'ucnt', 0) + 1
        return 'u%d' % self.ucnt

    def rot(self, key, items):
        i = self.rr.get(key, 0)
        self.rr[key] = i + 1
        return items[i % len(items)]

    def mm(self, out, lhsT, rhs, start=True, stop=True, skip=False):
        o, l, r = out.ap, lhsT.ap, rhs.ap
        if skip:
            f = lambda e: e.matmul(o, lhsT=l, rhs=r, start=start, stop=stop, skip_group_check=True)
        else:
            f = lambda e: e.matmul(o, lhsT=l, rhs=r, start=start, stop=stop)
        return self.S.add('pe', f, reads=_res(lhsT, rhs), writes=_res(out))

    def tr(self, out, in_, ident):
        o, i, d = out.ap, in_.ap, ident.ap
        return self.S.add('pe', lambda e: e.transpose(o, i, d), reads=_res(in_, ident), writes=_res(out))

    def act(self, out, in_, func, scale=1.0, bias=None, accum=None):
        o, i = out.ap, in_.ap
        kw = {}
        if bias is not None:
            kw['bias'] = _a(bias)
        if accum is not None:
            kw['accum_out'] = accum.ap
        sc = _a(scale)
        return self.S.add('act', lambda e: e.activation(out=o, in_=i, func=func, scale=sc, **kw),
                          reads=_res(in_, scale, bias), writes=_res(out, accum))

    def tsc(self, eng, out, in0, s1, s2, op0, op1=None, accum=None):
        o, i = out.ap, in0.ap
        a1, a2 = _a(s1), _a(s2)
        kw = {}
        if op1 is not None:
            kw['op1'] = op1
        if accum is not None:
            kw['accum_out'] = accum.ap
        if o.dtype == FP8:
            kw['saturate'] = False
        return self.S.add(eng, lambda e: e.tensor_scalar(out=o, in0=i, scalar1=a1, scalar2=a2, op0=op0, **kw),
                          reads=_res(in0, s1, s2), writes=_res(out, accum))

    def tt(self, eng, out, in0, in1, op):
        o, a, b = out.ap, in0.ap, in1.ap
        return self.S.add(eng, lambda e: e.tensor_tensor(out=o, in0=a, in1=b, op=op),
                          reads=_res(in0, in1), writes=_res(out))

    def stt(self, eng, out, in0, scalar, in1, op0, op1):
        o, a, b = out.ap, in0.ap, in1.ap
        s = _a(scalar)
        return self.S.add(eng, lambda e: e.scalar_tensor_tensor(out=o, in0=a, scalar=s, in1=b, op0=op0, op1=op1),
                          reads=_res(in0, in1, scalar), writes=_res(out))

    def cp(self, eng, out, in_):
        o, i = out.ap, in_.ap
        if eng == 'act':
            return self.S.add('act', lambda e: e.copy(out=o, in_=i), reads=_res(in_), writes=_res(out))
        if o.dtype == FP8:
            return self.S.add(eng, lambda e: e.tensor_copy(out=o, in_=i, saturate=False), reads=_res(in_), writes=_res(out))
        return self.S.add(eng, lambda e: e.tensor_copy(out=o, in_=i), reads=_res(in_), writes=_res(out))

    def memset(self, eng, out, val):
        o = out.ap
        return self.S.add(eng, lambda e: e.memset(o, val), writes=_res(out))

    def red(self, eng, out, in_, op, axis=AX.X):
        o, i = out.ap, in_.ap
        return self.S.add(eng, lambda e: e.tensor_reduce(out=o, in_=i, axis=axis, op=op),
                          reads=_res(in_), writes=_res(out))

    def dma(self, q, out, in_):
        o, i = out.ap, in_.ap
        return self.S.add(q, lambda e: e.dma_start(out=o, in_=i), reads=_res(in_), writes=_res(out), dma=True)


class Builder(MK):
    def __init__(self, nc, layers, dbg=False):
        super().__init__(nc)
        self.layers = layers
        self.dbg = dbg
        dt = nc.dram_tensor
        ext = lambda n, s, d=F32: dt(n, s, d, kind="ExternalInput").ap()
        self.x = ext("x", [SEQ, DM])
        self.mem = ext("mem", [256, DM])
        self.positions = ext("positions", [1, SEQ], I32)
        self.mem_ln_g = ext("mem_ln_g", [1, DM])
        self.mem_ln_b = ext("mem_ln_b", [1, DM])
        self.w_in_a = ext("w_in_a", [2, DM, WIDTH_A])
        self.idx_kn_g = ext("idx_kn_g", [2, 64, 1])
        self.idx_kn_b = ext("idx_kn_b", [2, 64, 1])
        self.w_in_b = ext("w_in_b", [1, DM, WIDTH_BC])
        self.w_in_c = ext("w_in_c", [1, DM, WIDTH_BC])
        self.w_mem_kv = ext("w_mem_kv", [DEPTH, DM, 512])
        self.w_out = ext("w_out", [DEPTH, DM, DM])
        self.ln1_g = ext("ln1_g", [DEPTH, DM])
        self.ln1_b = ext("ln1_b", [DEPTH, DM])
        self.w_gate_up = ext("w_gate_up", [DEPTH, DM, 2 * DFF])
        self.w_down = ext("w_down", [DEPTH, DFF, DM])
        self.ln2_g = ext("ln2_g", [DEPTH, DM])
        self.ln2_b = ext("ln2_b", [DEPTH, DM])
        self.c_ident = ext("c_ident", [128, 128], BF16)
        self.c_tri = ext("c_tri", [128, 128], BF16)
        self.c_r128 = ext("c_r128", [128, 128], BF16)
        self.c_causneg = ext("c_causneg", [128, 128])
        self.c_invcol = ext("c_invcol", [128, 1])
        self.c_ee = ext("c_ee", [16, 16 * 128], BF16)
        self.c_negown = ext("c_negown", [128, NT * 16])
        self.c_dmask = ext("c_dmask", [128, 256], BF16)
        self.y = dt("y", [SEQ, DM], F32, kind="ExternalOutput").ap()
        itn = lambda n, s, d=BF16: dt(n, s, d, kind=("ExternalOutput" if (dbg and n in dbg) else "Internal")).ap()
        self.XT = itn("XT", [8, 128, SEQ])
        self.X1T = itn("X1T", [8, 128, SEQ])
        self.XF = [itn("XFa", [SEQ, DM], F32), itn("XFb", [SEQ, DM], F32)]
        self.X1F = itn("X1F", [SEQ, DM], F32)
        self.QT = itn("QT", [12 * 64, SEQ])
        self.KT = itn("KT", [12 * 64, SEQ])
        self.QMT = itn("QMT", [4 * 64, SEQ])
        self.QIT = itn("QIT", [8 * 64, SEQ])
        self.KIT = itn("KIT", [64, SEQ])
        self.VAP = itn("VAP", [12, 128, NT, 65])
        self.VD = itn("VD", [SEQ, 772])
        self.ND = itn("ND", [3, SEQ, 772], F32)
        self.AT = itn("AT", [NFC, 128, SEQ])
        self.COSD = itn("COSD", [128, SEQ], F32)
        self.SIND = itn("SIND", [128, SEQ], F32)
        D = lambda ap, res: T(ap, res)
        self.D = D
        p = self.pers
        self.ident = p.bf16(128)
        self.tri = p.bf16(128)
        self.r128 = p.bf16(128)
        self.causneg = p.f32(128)
        self.invcol = p.f32(1)
        self.ee = p.bf16(16 * 128, parts=16)
        self.negown = p.f32(NT * 16)
        self.dmask = p.bf16(256)
        self.negpi = p.f32(1)
        self.epsc = p.f32(1)
        self.ones64 = p.f32(64, parts=64)
        self.onesbf = p.bf16(128)
        self.ident8 = p.fp8(128)
        self.pow2 = p.f32(NBIS)
        self.memnT = p.bf16(8 * 256)
        self.mkT = p.bf16(4 * 256, parts=64)
        self.mvaug = p.bf16(2 * 4 * 65)
        self.wi_all = p.f32(NT * 8)
        self.absw = p.f32(NT * 8)
        self.sgn = p.f32(NT * 8)
        self.kng = p.f32(1, parts=64)
        self.knb = p.f32(1, parts=64)

    def prologue(self):
        for (t, src) in [(self.ident, self.c_ident), (self.tri, self.c_tri), (self.r128, self.c_r128),
                         (self.causneg, self.c_causneg), (self.invcol, self.c_invcol), (self.ee, self.c_ee),
                         (self.negown, self.c_negown), (self.dmask, self.c_dmask)]:
            self.dma('sp', t, T(src, 'const'))
        self.memset('dve', self.negpi, -float(np.pi))
        self.memset('dve', self.epsc, LN_EPS)
        self.memset('dve', self.ones64, 1.0 / 64.0)
        self.memset('dve', self.onesbf, 1.0)
        self.memset('dve', self.mvaug, 1.0)
        for it in range(NBIS):
            self.memset('dve', self.pow2[:, it:it + 1], 2.0 ** (-(it + 1)))
        c = self.phase('rope')
        self.cp('dve', self.ident8, self.ident)
        H = 2048
        posi = c.i32(H)
        ang = c.f32(H)
        u = c.f32(H)
        kf = c.f32(H)
        tab = c.f32(H)
        for half in range(2):
            sl = slice(half * H, (half + 1) * H)
            self.dma('sp', posi, T(self.positions[:, sl].partition_broadcast(128), 'const'))
            self.cp('dve', ang, posi)
            self.tsc('dve', ang, ang, self.invcol[:, 0:1], None, ALU.mult)
            for (off, dst) in [(0.5, self.SIND), (0.75, self.COSD)]:
                self.tsc('dve', u, ang, 1.0 / (2.0 * np.pi), off, ALU.mult, ALU.add)
                self.cp('dve', posi, u)
                self.cp('dve', kf, posi)
                self.tt('dve', u, u, kf, ALU.subtract)
                self.tsc('dve', kf, u, 0.0, None, ALU.is_lt)
                self.tt('dve', u, u, kf, ALU.add)
                self.act(tab, u, AF.Sin, scale=2.0 * float(np.pi), bias=self.negpi[:, 0:1])
                self.dma('sp', T(dst[:, sl], self.u()), tab)
        c = self.phase('memln')
        g_t = c.f32(DM)
        b_t = c.f32(DM)
        self.dma('sp', g_t, T(self.mem_ln_g.partition_broadcast(128), 'const'))
        self.dma('sp', b_t, T(self.mem_ln_b.partition_broadcast(128), 'const'))
        lnb = self.ln_bufs(c)
        for mt in range(2):
            z = lnb['z']
            self.dma('sp', z, T(self.mem[mt * 128:(mt + 1) * 128, :], 'const'))
            xTt = self.ln_core(lnb, z, g_t, b_t, None)
            self.cp('dve', self.memnT.r("p (c t) -> p c t", c=8)[:, :, mt * 128:(mt + 1) * 128], xTt)
        c = self.phase('x2xT')
        xin = [c.f32(DM), c.f32(DM)]
        xb = [c.bf16(DM), c.bf16(DM)]
        xt = [c.bf16(DM), c.bf16(DM)]
        for tt_ in range(NT):
            a, b, o = xin[tt_ % 2], xb[tt_ % 2], xt[tt_ % 2]
            self.dma('sp', a, T(self.x[tt_ * 128:(tt_ + 1) * 128, :], 'const'))
            self.cp('act', b, a)
            pst = self.rot('pstr', [self.ps[4], self.ps[5]])
            psb = T(pst.ap.bitcast(BF16), pst.res)
            for cc in range(8):
                self.tr(psb[:, cc * 128:(cc + 1) * 128], b[:, cc * 128:(cc + 1) * 128], self.ident)
            self.cp('dve', o, psb[:, 0:1024])
            self.dma('pool', T(self.XT[:, :, tt_ * 128:(tt_ + 1) * 128].rearrange("c p t -> p c t"), self.u()),
                     o.r("p (c t) -> p c t", c=8))

    def ln_bufs(self, c, nz=0):
        return dict(z=c.f32(DM), z2=[c.f32(DM) for _ in range(nz)], st=c.f32(12), mv=c.f32(2), sd=c.f32(1), rs=c.f32(1),
                    xo=[c.f32(DM), c.f32(DM)], xb=[c.bf16(DM), c.bf16(DM)], xTt=[c.bf16(DM), c.bf16(DM)])

    def ln_core(self, lnb, z, g_t, b_t, out_f32_dram, trps=None, defer=False, notr=False):
        st, mv, sd, rs = lnb['st'], lnb['mv'], lnb['sd'], lnb['rs']
        xo = self.rot(('xo', id(lnb)), lnb['xo'])
        xb = self.rot(('xb', id(lnb)), lnb['xb'])
        xTt = self.rot(('xTt', id(lnb)), lnb['xTt'])
        za, zb, sa, sb_, ma = z.ap, z.ap, st.ap, st.ap, mv.ap
        self.S.add('dve', lambda e: e.bn_stats(out=sa[:, 0:6], in_=za[:, 0:512]), reads=_res(z), writes=_res(st))
        self.S.add('dve', lambda e: e.bn_stats(out=sa[:, 6:12], in_=za[:, 512:1024]), reads=_res(z), writes=_res(st))
        self.S.add('dve', lambda e: e.bn_aggr(out=ma, in_=sa), reads=_res(st), writes=_res(mv))
        self.act(sd, mv[:, 1:2], AF.Sqrt, bias=self.epsc[:, 0:1])
        sda, rsa = sd.ap, rs.ap
        self.S.add('dve', lambda e: e.reciprocal(out=rsa, in_=sda), reads=_res(sd), writes=_res(rs))
        self.tsc('dve', z, z, mv[:, 0:1], rs[:, 0:1], ALU.subtract, ALU.mult)
        self.tt('dve', z, z, g_t, ALU.mult)
        self.tt('pool', xo, z, b_t, ALU.add)
        if out_f32_dram is not None:
            self.dma('pool', out_f32_dram, xo)
        if notr:
            return None
        self.cp('act', xb, xo)

        def part2():
            pst = self.rot('pstr', trps or [self.ps[4], self.ps[5]])
            psb = T(pst.ap.bitcast(BF16), pst.res)
            for cc in range(8):
                self.tr(psb[:, cc * 128:(cc + 1) * 128], xb[:, cc * 128:(cc + 1) * 128], self.ident)
            self.cp('act', xTt, psb[:, 0:1024])
            return xTt.r("p (c t) -> p c t", c=8)

        if defer:
            return part2
        return part2()

    def load_w_chunks(self, dst_bf, src_dram_fn, nchunks, ncols, stage, eng='pool'):
        for cc in range(nchunks):
            st = self.rot(('wst', id(stage[0])), stage)
            self.dma('sp', st[:, 0:ncols], src_dram_fn(cc))
            self.cp(eng, dst_bf[:, cc, :], st[:, 0:ncols])

    def phase_inproj(self, li, kind, w_in, j):
        c = self.phase('A')
        xT = c.bf16(8 * SEQ, name='xT').r("p (c t) -> p c t", c=8)
        for cc in range(8):
            self.dma('sp', xT[:, cc, :], T(self.XT[cc], 'const'))
        COS = c.f32(SEQ)
        SIN = c.f32(SEQ)
        self.dma('sp', COS, T(self.COSD, 'const'))
        self.dma('sp', SIN, T(self.SIND, 'const'))
        wst = [c.f32(1024), c.f32(1024)]
        wbf = [c.bf16(1024), c.bf16(1024)]
        qsb = [c.bf16(512), c.bf16(512)]
        t1 = [c.f32(512), c.f32(512)]
        t2 = [c.f32(512), c.f32(512)]
        osb = [c.bf16(512), c.bf16(512), c.bf16(512)]
        w_pcn = w_in.rearrange("(c p) n -> p c n", p=128)

        jobs = []

        def fm_prep(col0, M):
            ws = self.rot('wst', wst)
            wb = self.rot('wbf', wbf)
            wsv = ws[:, 0:8 * M].r("p (c n) -> p c n", c=8)
            wbv = wb[:, 0:8 * M].r("p (c n) -> p c n", c=8)
            self.dma('sp', wsv, T(w_pcn[:, :, col0:col0 + M], 'const'))
            self.cp('pool', wbv, wsv)
            return wbv

        def fm_job(col0, M, mode, dst):
            jobs.append((col0, M, mode, dst))

        def run_jobs():
            nxt = fm_prep(jobs[0][0], jobs[0][1])
            for i, (col0, M, mode, dst) in enumerate(jobs):
                wbv = nxt
                if i + 1 < len(jobs):
                    nxt = fm_prep(jobs[i + 1][0], jobs[i + 1][1])
                fm_run(wbv, M, mode, dst)

        def fm_run(wbv, M, mode, dst):
            for G in range(NG):
                gs = slice(G * 512, (G + 1) * 512)
                ps = self.rot('psA', self.ps[0:3])
                for cc in range(8):
                    self.mm(ps[0:M, :], wbv[:, cc, :], xT[:, cc, gs], start=(cc == 0), stop=(cc == 7))
                o = self.rot('osb', osb)
                if mode == 'plain':
                    self.cp('act', o[0:M, :], ps[0:M, :])
                elif mode == 'rope':
                    q = self.rot('qsb', qsb)
                    a1 = self.rot('t1', t1)
                    a2 = self.rot('t2', t2)
                    self.cp('act', q[0:M, :], ps[0:M, :])
                    pr = self.rot('psR', self.ps[3:5])
                    self.mm(pr[0:M, :], self.r128[0:M, 0:M], q[0:M, :])
                    self.tt('dve', a1[0:M, :], ps[0:M, :], COS[0:M, gs], ALU.mult)
                    self.tt('dve', a2[0:M, :], pr[0:M, :], SIN[0:M, gs], ALU.mult)
                    self.tt('pool', o[0:M, :], a1[0:M, :], a2[0:M, :], ALU.add)
                self.dma('pool', T(dst[:, gs], self.u()), o[0:M, :])

        def fm_region(col0, ncols, mode, dst):
            for i in range(ncols // 128):
                fm_job(col0 + i * 128, 128, mode, dst[i * 128:(i + 1) * 128, :])

        if self.stop == 'A00':
            return
        if self.stop in ('A01', 'A02', 'A03'):
            ws = self.rot('wst', wst)
            wb = self.rot('wbf', wbf)
            wsv = ws[:, 0:8 * 128].r("p (c n) -> p c n", c=8)
            wbv = wb[:, 0:8 * 128].r("p (c n) -> p c n", c=8)
            self.dma('sp', wsv, T(w_pcn[:, :, 0:128], 'const'))
            self.cp('pool', wbv, wsv)
            if self.stop == 'A01':
                return
            gs = slice(0, 512)
            ps = self.ps[0]
            for cc in range(8):
                self.mm(ps, wbv[:, cc, :], xT[:, cc, gs], start=(cc == 0), stop=(cc == 7))
            o = osb[0]
            if self.stop == 'A02':
                self.cp('act', o, ps)
            else:
                q = qsb[0]
                self.cp('act', q, ps)
                pr = self.ps[3]
                self.mm(pr, self.r128, q)
                self.tt('dve', t1[0], ps, COS[:, gs], ALU.mult)
                self.tt('dve', t2[0], pr, SIN[:, gs], ALU.mult)
                self.tt('pool', o, t1[0], t2[0], ALU.add)
            self.dma('pool', T(self.QT[0:128, gs], self.u()), o)
            return
        fm_region(0, 768, 'rope', self.QT)
        fm_region(768, 768, 'rope', self.KT)
        if kind == 0:
            fm_region(2304, 512, 'rope', self.QIT)
            fm_region(2888, 256, 'plain', self.QMT)
            run_jobs()
            self.inproj_ki(c, w_pcn, xT, COS, SIN, j)
        else:
            fm_region(2304, 256, 'plain', self.QMT)
            run_jobs()
        if self.stop == 'A2':
            return

        NV = 776 if kind == 0 else 768
        wv = c.bf16(8 * NV).r("p (c n) -> p c n", c=8)
        vst = [c.f32(NV), c.f32(NV)]
        for cc in range(8):
            st = self.rot('vst', vst)
            self.dma('sp', st[:, 0:768], T(w_in[cc * 128:(cc + 1) * 128, 1536:2304], 'const'))
            if kind == 0:
                self.dma('sp', st[:, 768:776], T(w_in[cc * 128:(cc + 1) * 128, 2816:2824], 'const'))
            self.cp('pool', wv[:, cc, :], st[:, 0:NV])
        nvh, dv = (4, 192) if kind == 1 else (12, 64)
        vaug = [c.bf16(nvh * (dv + 1)), c.bf16(nvh * (dv + 1))]
        for v in vaug:
            self.memset('pool', v, 1.0)
        for tt_ in range(NT):
            ts = slice(tt_ * 128, (tt_ + 1) * 128)
            psv = self.ps2(6)
            for (n0, n1) in [(0, 512), (512, NV)]:
                for cc in range(8):
                    self.mm(psv[:, n0:n1], xT[:, cc, ts], wv[:, cc, n0:n1], start=(cc == 0), stop=(cc == 7))
            va = self.rot('vaug', vaug)
            va3 = va.r("p (h e) -> p h e", h=nvh)
            self.cp('act', va3[:, :, 0:dv], psv[:, 0:768].r("p (h d) -> p h d", h=nvh))
            if kind == 0:
                self.cp('dve', self.wi_all[:, tt_ * 8:(tt_ + 1) * 8], psv[:, 768:776])
            if kind == 1:
                self.dma('pool', T(self.VD[ts, :], self.u()), va)
            else:
                self.dma('pool', T(self.VAP[:, :, tt_, :].rearrange("h p e -> p h e"), self.u()), va3)

        if self.stop == 'A3':
            return
        wm = self.w_mem_kv[li]
        wmb = c.bf16(8 * 512).r("p (c n) -> p c n", c=8)
        mst = [c.f32(512), c.f32(512)]
        for cc in range(8):
            st = self.rot('mst', mst)
            self.dma('sp', st, T(wm[cc * 128:(cc + 1) * 128, :], 'const'))
            self.cp('pool', wmb[:, cc, :], st)
        memn = self.memnT.r("p (c t) -> p c t", c=8)
        mk3 = self.mkT.r("p (h t) -> p h t", h=4)
        for h in range(4):
            ps = self.rot('psA', self.ps[0:3])
            for cc in range(8):
                self.mm(ps[0:64, 0:256], wmb[:, cc, h * 64:(h + 1) * 64], memn[:, cc, :], start=(cc == 0), stop=(cc == 7))
            self.cp('act', mk3[:, h, :], ps[0:64, 0:256])
        mv4 = self.mvaug.r("p (t h e) -> p t h e", t=2, h=4)
        for mt in range(2):
            ps = self.rot('psA', self.ps[0:3])
            for cc in range(8):
                self.mm(ps[:, 0:256], memn[:, cc, mt * 128:(mt + 1) * 128], wmb[:, cc, 256:512], start=(cc == 0), stop=(cc == 7))
            self.cp('act', mv4[:, mt, :, 0:64], ps[:, 0:256].r("p (h d) -> p h d", h=4))

    def inproj_ki(self, c, w_pcn, xT, COS, SIN, j):
        self.dma('sp', self.kng, T(self.idx_kn_g[j], 'const'))
        self.dma('sp', self.knb, T(self.idx_kn_b[j], 'const'))
        ws = c.f32(8 * 64).r("p (c n) -> p c n", c=8)
        wb = c.bf16(8 * 64).r("p (c n) -> p c n", c=8)
        self.dma('sp', ws, T(w_pcn[:, :, 2824:2888], 'const'))
        self.cp('pool', wb, ws)
        P = 64
        xs = c.f32(512, parts=P)
        x2 = c.f32(512, parts=P)
        msb = c.f32(512, parts=P)
        var = c.f32(512, parts=P)
        xn = c.f32(512, parts=P)
        xnb = c.bf16(512, parts=P)
        a1 = c.f32(512, parts=P)
        a2 = c.f32(512, parts=P)
        o = [c.bf16(512, parts=P), c.bf16(512, parts=P)]
        for G in range(NG):
            gs = slice(G * 512, (G + 1) * 512)
            ps = self.rot('psA', self.ps[0:3])
            for cc in range(8):
                self.mm(ps[0:P, :], wb[:, cc, :], xT[:, cc, gs], start=(cc == 0), stop=(cc == 7))
            self.cp('act', xs, ps[0:P, :])
            self.act(x2, ps[0:P, :], AF.Square)
            pm = self.ps[3]
            pe2 = self.ps[4]
            self.mm(pm[0:P, :], self.ones64, xs)
            self.mm(pe2[0:P, :], self.ones64, x2)
            self.act(msb, pm[0:P, :], AF.Square)
            self.tt('dve', var, pe2[0:P, :], msb, ALU.subtract)
            self.act(var, var, AF.Sqrt, bias=self.epsc[0:P, 0:1])
            va = var.ap
            self.S.add('dve', lambda e, va=va: e.reciprocal(out=va, in_=va), reads=_res(var), writes=_res(var))
            self.tt('dve', xn, xs, pm[0:P, :], ALU.subtract)
            self.tt('dve', xn, xn, var, ALU.mult)
            self.tsc('dve', xn, xn, self.kng[:, 0:1], self.knb[:, 0:1], ALU.mult, ALU.add)
            self.cp('act', xnb, xn)
            pr = self.ps[5]
            self.mm(pr[0:P, :], self.r128[0:P, 0:P], xnb)
            self.tt('dve', a1, xn, COS[0:P, gs], ALU.mult)
            self.tt('dve', a2, pr[0:P, :], SIN[0:P, gs], ALU.mult)
            oo = self.rot('kio', o)
            self.tt('pool', oo, a1, a2, ALU.add)
            self.dma('pool', T(self.KIT[:, gs], self.u()), oo)

    def attn_head(self, G, kT, qTg, vaugh, nkt, causal, bias_fn, mix3, col0, pT, scale=0.125):
        po = self.rot('psO', [self.ps[3], self.ps[4]])
        pend = []
        first = [True]

        def pv(kt, c0, pt):
            for jj in range(c0, 4):
                self.mm(po[:, jj * 65:(jj + 1) * 65], pt[:, jj * 128:(jj + 1) * 128], vaugh[:, kt, :],
                        start=first[0], stop=(kt == nkt - 1 and jj == 3))
                first[0] = False

        for kt in range(nkt):
            c0 = max(0, kt - 4 * G) if causal else 0
            psS = self.rot('psS', self.ps[0:3])
            cols = slice(c0 * 128, 512)
            biases = bias_fn(kt, c0, psS) if bias_fn is not None else []
            self.mm(psS[:, cols], kT[:, kt * 128:(kt + 1) * 128], qTg[:, cols], start=True, stop=(len(biases) == 0))
            for bi, (o, l, r) in enumerate(biases):
                self.mm(o, l, r, start=False, stop=(bi == len(biases) - 1))
            pt = self.rot('pT', pT)
            self.act(pt[:, cols], psS[:, cols], AF.Exp, scale=scale)
            pend.append((kt, c0, pt))
            if len(pend) > 1:
                pv(*pend.pop(0))
        while pend:
            pv(*pend.pop(0))
        self.flush_pend()
        po3 = po[:, 0:260].r("p (j e) -> p j e", j=4)
        rden = self.rot('rden', self.rdens)
        lnd = self.rot('lnd', self.lnds)
        self.act(lnd, po3[:, :, 64], AF.Ln)
        self.act(rden, lnd, AF.Exp, scale=-1.0)
        for jj in range(4):
            self.act(mix3[:, jj, col0:col0 + 64], po3[:, jj, 0:64], AF.Copy, scale=rden[:, jj:jj + 1])

    def phase_attn(self, li, kind):
        c = self.phase('B')
        wout = c.bf16(8 * DM).r("p (c n) -> p c n", c=8)
        xin = [c.f32(DM), c.f32(DM)]
        self.load_w_chunks(wout, lambda cc: T(self.w_out[li][cc * 128:(cc + 1) * 128, :], 'const'), 8, DM, xin)
        g_t = c.f32(DM)
        b_t = c.f32(DM)
        self.dma('sp', g_t, T(self.ln1_g[li:li + 1, :].partition_broadcast(128), 'const'))
        self.dma('sp', b_t, T(self.ln1_b[li:li + 1, :].partition_broadcast(128), 'const'))
        lnb = self.ln_bufs(c)
        mix = c.bf16(4 * DM)
        mix3 = mix.r("p (j n) -> p j n", j=4)
        mixT = [c.bf16(DM), c.bf16(DM)]
        self.rdens = [c.f32(4), c.f32(4)]
        self.lnds = [c.f32(4), c.f32(4)]
        xres = self.x if self.is_first else self.XF[li % 2]
        if kind == 1:
            self.dilated_all(c, li)
        else:
            kTb = [c.bf16(SEQ, parts=64), c.bf16(SEQ, parts=64)]
            vab = [c.bf16(NT * 65), c.bf16(NT * 65)]
            qTb = [c.bf16(512, parts=64), c.bf16(512, parts=64)]
            pT = [c.bf16(512) for _ in range(4)]
            if kind == 0:
                I = c.f32(SEQ)
                Mb2 = [[c.fp8(SEQ) for _ in range(4)] for _ in range(2)]
                kiT = c.bf16(SEQ, parts=64)
                qiT = c.bf16(8 * 512, parts=64).r("p (h t) -> p h t", h=8)
                rbuf = [c.bf16(512) for _ in range(3)]
                diag = [c.bf16(8 * 128), c.bf16(8 * 128)]
                sm = dict(lo=c.f32(1), hi=c.f32(1), d0=c.f32(1), cth=c.f32(1), cnt=c.f32(1), step=c.f32(1), dtab=c.f32(NBIS))
                self.dma('sp', kiT, T(self.KIT, 'const'))
                self.act(self.absw, self.wi_all, AF.Abs)
                self.tsc('dve', self.sgn, self.wi_all, 0.0, None, ALU.is_ge)
                self.tsc('dve', self.sgn, self.sgn, 2.0, -1.0, ALU.mult, ALU.add)
            else:
                kmean = c.f32(16, parts=64)
                kmb = c.bf16(16, parts=64)
                self.memset('dve', kmb, 0.0)
                gm = c.f32(64)
                sel = c.f32(64)
                m8 = c.f32(8)
                selb = c.bf16(64)
                selbT = [c.bf16(512, parts=16), c.bf16(512, parts=16)]
        for G in range(NG):
            gs = slice(G * 512, (G + 1) * 512)
            nkeys = (G + 1) * 512
            nkt = (G + 1) * 4
            if kind == 0:
                Mb = Mb2[G % 2]

                def idx_group(Gn):
                    gsn = slice(Gn * 512, (Gn + 1) * 512)
                    self.dma('sp', qiT, T(self.QIT[:, gsn].rearrange("(h d) t -> d h t", h=8), 'const'))

                if G == 0:
                    idx_group(0)
                    for jq in range(4):
                        self.dsa_index(0, jq, I, Mb2[0][jq], kiT, qiT, rbuf, diag, sm)
                if G + 1 < NG:
                    idx_group(G + 1)
            if kind != 1:
                def prep(h):
                    kT = self.rot('kTb', kTb)
                    va = self.rot('vab', vab)
                    qTg = self.rot('qTb', qTb)
                    self.dma('sp', kT[:, 0:nkeys], T(self.KT[h * 64:(h + 1) * 64, 0:nkeys], 'const'))
                    va3 = va.r("p (t e) -> p t e", t=NT)
                    self.dma('sp', va3[:, 0:nkt, :], T(self.VAP[h, :, 0:nkt, :], 'const'))
                    self.dma('sp', qTg, T(self.QT[h * 64:(h + 1) * 64, gs], 'const'))
                    sT = None
                    if kind == 2:
                        sT = self.rot('selbT', selbT)
                        self.moba_gate(G, h, kT, qTg, kmean, kmb, gm, sel, m8, selb, sT)
                    return kT, va3, qTg, sT

                nxt = prep(0)
                for h in range(12):
                    kT, va3, qTg, sT = nxt
                    if kind == 0 and G + 1 < NG and h % 3 == 0:
                        self.dsa_index(G + 1, h // 3, I, Mb2[(G + 1) % 2][h // 3], kiT, qiT, rbuf, diag, sm)
                    if h + 1 < 12:
                        nxt = prep(h + 1)
                    if kind == 0:
                        def bias_fn(kt, c0, psS, Mb=Mb):
                            return [(psS[:, jj * 128:(jj + 1) * 128], Mb[jj][:, kt * 128:(kt + 1) * 128], self.ident8)
                                    for jj in range(c0, 4)]
                    else:
                        def bias_fn(kt, c0, psS, sT=sT):
                            n = kt // 2
                            out = []
                            run = None
                            for jj in range(c0, 4):
                                qt = 4 * G + jj
                                if qt // 2 == n:
                                    if run is not None:
                                        out.append(run)
                                        run = None
                                    if qt == kt:
                                        out.append((psS[:, jj * 128:(jj + 1) * 128], self.tri, self.ident))
                                else:
                                    if run is None:
                                        run = [jj, jj + 1]
                                    else:
                                        run[1] = jj + 1
                            if run is not None:
                                out.append(run)
                            res = []
                            for it in out:
                                if isinstance(it, list):
                                    cs = slice(it[0] * 128, it[1] * 128)
                                    res.append((psS[:, cs], self.ee[:, n * 128:(n + 1) * 128], sT[:, cs]))
                                else:
                                    res.append(it)
                            return res
                    self.attn_head(G, kT, qTg, va3, nkt, True, bias_fn, mix3, h * 64, pT)
            if kind == 1:
                qTb = self.dil_qTb
                pT = self.dil_pT
            mk3 = self.mkT.r("p (h t) -> p h t", h=4)
            mv4 = self.mvaug.r("p (t h e) -> p t h e", t=2, h=4)
            for m in range(4):
                qTg = self.rot('qTb', qTb)
                self.dma('sp', qTg, T(self.QMT[m * 64:(m + 1) * 64, gs], 'const'))
                self.attn_head(G, mk3[:, m, :], qTg, mv4[:, :, m, :], 2, False, None, mix3, 768 + m * 64, pT)
            for jq in range(4):
                tt_ = 4 * G + jq
                ts = slice(tt_ * 128, (tt_ + 1) * 128)
                if kind == 1:
                    self.dilated_combine(tt_, mix3[:, jq, 0:768])
                xi = self.rot('xin', xin)
                self.dma('sp', xi, T(xres[ts, :], 'const'))
                pst = self.ps[5]
                psb = T(pst.ap.bitcast(BF16), pst.res)
                for cc in range(8):
                    self.tr(psb[:, cc * 128:(cc + 1) * 128], mix3[:, jq, cc * 128:(cc + 1) * 128], self.ident)
                mT = self.rot('mixT', mixT)
                self.cp('act', mT, psb[:, 0:1024])
                mT3 = mT.r("p (c t) -> p c t", c=8)
                py = self.ps2(6)
                for nh in range(2):
                    for cc in range(8):
                        self.mm(py[:, nh * 512:(nh + 1) * 512], mT3[:, cc, :], wout[:, cc, nh * 512:(nh + 1) * 512],
                                start=(cc == 0), stop=(cc == 7))
                z = lnb['z']
                self.stt('dve', z, xi, ALPHA, py, ALU.mult, ALU.add)
                p2 = self.ln_core(lnb, z, g_t, b_t, T(self.X1F[ts, :], self.u()), trps=[self.ps[5]], defer=True)
                self.flush_pend()
                self.pend.append((p2, lambda xTt, ts=ts: self.dma(
                    'pool', T(self.X1T[:, :, ts].rearrange("c p t -> p c t"), self.u()), xTt)))

    def dsa_index(self, G, jq, I, Mbj, kiT, qiT, rbuf, diag, sm):
        qt = 4 * G + jq
        nk = (qt + 1) * 128
        dg = self.rot('diag', diag).r("p (h k) -> p h k", h=8)
        for h in range(8):
            self.tsc('pool', dg[:, h, :], self.ident, self.sgn[:, qt * 8 + h:qt * 8 + h + 1], None, ALU.mult)
        nch = (nk + 511) // 512
        LAG = 2
        for ch in range(nch):
            w = min(512, nk - ch * 512)
            pa = self.rot('psO', [self.ps[3], self.ps[4]])
            rr = [None] * 8
            for h in range(8 + LAG):
                if h < 8:
                    pd = self.rot('psS', self.ps[0:3])
                    self.mm(pd[:, 0:w], qiT[:, h, jq * 128:(jq + 1) * 128], kiT[:, ch * 512:ch * 512 + w])
                    rr[h] = self.rot('rbuf', rbuf)
                    self.act(rr[h][:, 0:w], pd[:, 0:w], AF.Relu, scale=self.absw[:, qt * 8 + h:qt * 8 + h + 1])
                if h >= LAG:
                    hh = h - LAG
                    self.mm(pa[:, 0:w], dg[:, hh, :], rr[hh][:, 0:w], start=(hh == 0), stop=(hh == 7))
            self.cp('dve', I[:, ch * 512:ch * 512 + w], pa[:, 0:w])
        dsl = slice(qt * 128, (qt + 1) * 128)
        lo, hi, d0, cth, cnt, step = sm['lo'], sm['hi'], sm['d0'], sm['cth'], sm['cnt'], sm['step']
        if qt >= 2:
            self.red('dve', lo, I[:, 0:nk], ALU.min)
            self.tt('pool', I[:, dsl], I[:, dsl], self.causneg, ALU.add)
            self.red('dve', hi, I[:, 0:nk], ALU.max)
            self.tt('dve', d0, hi, lo, ALU.subtract)
            dtab = sm['dtab']
            self.tsc('dve', dtab, self.pow2, d0[:, 0:1], None, ALU.mult)
            self.stt('dve', lo, d0, 0.5, lo, ALU.mult, ALU.add)
            for it in range(NBIS):
                self.tsc('dve', Mbj[:, 0:nk], I[:, 0:nk], lo[:, 0:1], 0.0, ALU.is_ge, ALU.add, accum=cnt)
                self.tsc('dve', step, cnt, TOPK - 0.5, 0.5, ALU.is_ge, ALU.subtract)
                self.stt('dve', lo, dtab[:, it:it + 1], step[:, 0:1], lo, ALU.mult, ALU.add)
        else:
            self.tt('pool', I[:, dsl], I[:, dsl], self.causneg, ALU.add)
            self.memset('dve', lo, -1.0e29)
        self.tsc('dve', Mbj[:, 0:nk], I[:, 0:nk], lo[:, 0:1], NEG8, ALU.is_lt, ALU.mult)

    def moba_gate(self, G, h, kT, qTg, kmean, kmb, gm, sel, m8, selb, sT):
        nb = 2 * G + 2
        self.red('dve', kmean[:, 0:nb], kT[:, 0:nb * 256].r("p (n k) -> p n k", k=256), ALU.add)
        self.tsc('dve', kmb[:, 0:nb], kmean[:, 0:nb], 1.0 / 256.0, None, ALU.mult)
        pg = self.rot('psS', self.ps[0:3])
        for jj in range(4):
            self.mm(pg[:, jj * 16:(jj + 1) * 16], qTg[:, jj * 128:(jj + 1) * 128], kmb, start=(jj == 0), stop=(jj == 3))
        self.tt('dve', gm, pg[:, 0:64], self.negown[:, G * 64:(G + 1) * 64], ALU.add)
        for jj in range(4):
            ma, ga = m8.ap, gm.ap[:, jj * 16:(jj + 1) * 16]
            self.S.add('dve', lambda e, ma=ma, ga=ga: e.max(out=ma, in_=ga), reads=_res(gm), writes=_res(m8))
            self.tsc('dve', sel[:, jj * 16:(jj + 1) * 16], gm[:, jj * 16:(jj + 1) * 16], m8[:, 2:3], None, ALU.is_ge)
        self.tsc('dve', selb, sel, -NEG, NEG, ALU.mult, ALU.add)
        pst = self.ps[5]
        psb = T(pst.ap.bitcast(BF16), pst.res)
        for jj in range(4):
            self.tr(psb[0:16, jj * 128:(jj + 1) * 128], selb[:, jj * 16:(jj + 1) * 16], self.ident)
        self.cp('act', sT, psb[0:16, 0:512])

    def dilated_all(self, c, li):
        qd = [c.bf16(SEQ, parts=64) for _ in range(4)]
        kd = [c.bf16(SEQ, parts=64) for _ in range(4)]
        vsub = [c.bf16(772) for _ in range(3)]
        pTo = [c.bf16(512), c.bf16(512)]
        pTp = [c.bf16(512), c.bf16(512)]
        osb = [c.f32(772), c.f32(772)]
        dm4 = c.bf16(1024)
        self.dil_qTb = [c.bf16(512, parts=64), c.bf16(512, parts=64)]
        self.dil_pT = [c.bf16(512) for _ in range(4)]
        self.dil_nd = [c.f32(772) for _ in range(3)]
        for hh in range(4):
            self.cp('dve', dm4[:, hh * 128:(hh + 1) * 128], self.dmask[:, 0:128])
            self.cp('dve', dm4[:, 512 + hh * 128:512 + (hh + 1) * 128], self.dmask[:, 128:256])
        mU, mL = dm4[:, 0:512], dm4[:, 512:1024]
        for g, (window, d) in enumerate(((128, 1), (512, 4), (2048, 16))):
            for hh in range(4):
                h = 4 * g + hh
                self.dma('sp', qd[hh], T(self.QT[h * 64:(h + 1) * 64, :], 'const'))
                self.dma('sp', kd[hh], T(self.KT[h * 64:(h + 1) * 64, :], 'const'))
            L = SEQ // d
            vdr = self.VD.rearrange("(j d) e -> d j e", d=d)
            ndr = self.ND[g].rearrange("(j d) e -> d j e", d=d)
            for r in range(d):
                vprev = None
                for jt in range(L // 128):
                    js = slice(jt * 128, (jt + 1) * 128)
                    jp = slice((jt - 1) * 128, jt * 128)
                    vown = self.rot('vsub', vsub)
                    self.dma('sp', vown, T(vdr[r, js, :], 'const'))
                    pso = self.rot('psS', self.ps[0:4])
                    psp = self.rot('psS', self.ps[0:4]) if jt > 0 else None
                    for hh in range(4):
                        qv = qd[hh].r("p (j d) -> p d j", d=d)
                        kv = kd[hh].r("p (j d) -> p d j", d=d)
                        self.mm(pso[:, hh * 128:(hh + 1) * 128], kv[:, r, js], qv[:, r, js], start=(hh == 0), stop=(hh == 3))
                    if jt > 0:
                        for hh in range(4):
                            qv = qd[hh].r("p (j d) -> p d j", d=d)
                            kv = kd[hh].r("p (j d) -> p d j", d=d)
                            self.mm(psp[:, hh * 128:(hh + 1) * 128], kv[:, r, jp], qv[:, r, js], start=(hh == 0), stop=(hh == 3))
                    po_ = self.rot('pTo', pTo)
                    self.act(po_, pso, AF.Exp, scale=0.125)
                    self.tt('dve', po_, po_, mL, ALU.mult)
                    if jt > 0:
                        pp_ = self.rot('pTp', pTp)
                        self.act(pp_, psp, AF.Exp, scale=0.125)
                        self.tt('pool', pp_, pp_, mU, ALU.mult)
                    py = self.ps2(6)
                    for hh in range(4):
                        oc = slice(hh * 256, hh * 256 + 193)
                        st = (hh % 2 == 0)
                        sp_ = (hh % 2 == 1)
                        if jt > 0:
                            self.mm(py[:, oc], pp_[:, hh * 128:(hh + 1) * 128], vprev[:, hh * 193:(hh + 1) * 193], start=st, stop=False)
                            self.mm(py[:, oc], po_[:, hh * 128:(hh + 1) * 128], vown[:, hh * 193:(hh + 1) * 193], start=False, stop=sp_)
                        else:
                            self.mm(py[:, oc], po_[:, hh * 128:(hh + 1) * 128], vown[:, hh * 193:(hh + 1) * 193], start=st, stop=sp_)
                    ob = self.rot('dosb', osb)
                    self.cp('act', ob.r("p (h e) -> p h e", h=4), py.r("p (h e) -> p h e", h=4)[:, :, 0:193])
                    self.dma('pool', T(ndr[r, js, :], self.u()), ob)
                    vprev = vown
        self.S.barrier()

    def dilated_combine(self, tt_, mix_out):
        ts = slice(tt_ * 128, (tt_ + 1) * 128)
        nd = self.dil_nd
        for g in range(3):
            self.dma('sp', nd[g], T(self.ND[g][ts, :], 'const'))
        self.tt('pool', nd[0], nd[0], nd[1], ALU.add)
        self.tt('pool', nd[0], nd[0], nd[2], ALU.add)
        a3 = nd[0].r("p (h e) -> p h e", h=4)
        rden = self.rot('rden', self.rdens)
        ra, da = rden.ap, a3.ap[:, :, 192]
        self.S.add('dve', lambda e: e.reciprocal(out=ra, in_=da), reads=_res(nd[0]), writes=_res(rden))
        for hh in range(4):
            self.tsc('dve', mix_out[:, hh * 192:(hh + 1) * 192], a3[:, hh, 0:192], rden[:, hh:hh + 1], None, ALU.mult)

    def phase_ffn1(self, li):
        c = self.phase('C')
        xT = c.bf16(8 * SEQ, name='xT1').r("p (c t) -> p c t", c=8)
        for cc in range(8):
            self.dma('sp', xT[:, cc, :], T(self.X1T[cc], 'const'))
        wst = [c.f32(1024) for _ in range(4)]
        wbf = [c.bf16(1024) for _ in range(4)]
        sg = [c.f32(512), c.f32(512)]
        aT = [c.bf16(512) for _ in range(3)]
        wgu = self.w_gate_up[li].rearrange("(c p) n -> p c n", p=128)
        def prepw(f):
            wb = []
            for col0 in (f * 128, DFF + f * 128):
                ws = self.rot('wst', wst).r("p (c n) -> p c n", c=8)
                w_ = self.rot('wbf', wbf).r("p (c n) -> p c n", c=8)
                self.dma('sp', ws, T(wgu[:, :, col0:col0 + 128], 'const'))
                self.cp('pool', w_, ws)
                wb.append(w_)
            return wb

        nxtw = prepw(0)
        for f in range(NFC):
            wb = nxtw
            if f + 1 < NFC:
                nxtw = prepw(f + 1)
            for G in range(NG):
                gs = slice(G * 512, (G + 1) * 512)
                pair = self.rot('psFF', [(0, 1), (2, 3), (4, 5)])
                pg, pu = self.ps[pair[0]], self.ps[pair[1]]
                for cc in range(8):
                    self.mm(pg, wb[0][:, cc, :], xT[:, cc, gs], start=(cc == 0), stop=(cc == 7))
                for cc in range(8):
                    self.mm(pu, wb[1][:, cc, :], xT[:, cc, gs], start=(cc == 0), stop=(cc == 7))
                s_ = self.rot('sg', sg)
                a_ = self.rot('aT', aT)
                self.act(s_, pg, AF.Silu)
                self.tt('dve', a_, s_, pu, ALU.mult)
                self.dma('pool', T(self.AT[f, :, gs], self.u()), a_)

    def phase_ffn2(self, li, last):
        c = self.phase('D')
        wd = c.bf16(NFC * DM).r("p (c n) -> p c n", c=NFC)
        xin = [c.f32(DM), c.f32(DM)]
        self.load_w_chunks(wd, lambda cc: T(self.w_down[li][cc * 128:(cc + 1) * 128, :], 'const'), NFC, DM, xin)
        g_t = c.f32(DM)
        b_t = c.f32(DM)
        self.dma('sp', g_t, T(self.ln2_g[li:li + 1, :].partition_broadcast(128), 'const'))
        self.dma('sp', b_t, T(self.ln2_b[li:li + 1, :].partition_broadcast(128), 'const'))
        lnb = self.ln_bufs(c, nz=1)
        aTg = [c.bf16(NFC * 512), c.bf16(NFC * 512)]
        for G in range(NG):
            gs = slice(G * 512, (G + 1) * 512)
            ag = self.rot('aTg', aTg).r("p (f t) -> p f t", f=NFC)
            self.dma('sp', ag, T(self.AT[:, :, gs].rearrange("f p t -> p f t"), 'const'))
            for jq in range(4):
                tt_ = 4 * G + jq
                ts = slice(tt_ * 128, (tt_ + 1) * 128)
                xi = self.rot('xin', xin)
                self.dma('sp', xi, T(self.X1F[ts, :], 'const'))
                py = self.rot('pyD', [self.ps2(0), self.ps2(2), self.ps2(6)])
                for nh in range(2):
                    for f in range(NFC):
                        self.mm(py[:, nh * 512:(nh + 1) * 512], ag[:, f, jq * 128:(jq + 1) * 128],
                                wd[:, f, nh * 512:(nh + 1) * 512], start=(f == 0), stop=(f == NFC - 1))
                z = self.rot('zD', [lnb['z']] + lnb['z2'])
                self.stt('dve', z, xi, ALPHA, py, ALU.mult, ALU.add)
                dst = self.y if last else self.XF[(li + 1) % 2]
                if last:
                    self.ln_core(lnb, z, g_t, b_t, T(dst[ts, :], self.u()), notr=True)
                else:
                    p2 = self.ln_core(lnb, z, g_t, b_t, T(dst[ts, :], self.u()), defer=True)
                    self.flush_pend()
                    self.pend.append((p2, lambda xTt, ts=ts: self.dma(
                        'pool', T(self.XT[:, :, ts].rearrange("c p t -> p c t"), self.u()), xTt)))

    def build_all(self, stop=None):
        self.stop = stop
        self.prologue()
        n = len(self.layers)
        for idx, li in enumerate(self.layers):
            if stop == 'P':
                break
            kind, j = li % 3, li // 3
            self.is_first = (idx == 0)
            w_in = (self.w_in_a, self.w_in_b, self.w_in_c)[kind][j]
            self.phase_inproj(li, kind, w_in, j)
            if stop and stop[0] == 'A':
                break
            self.phase_attn(li, kind)
            if stop == 'B':
                break
            self.phase_ffn1(li)
            if stop == 'C':
                break
            self.phase_ffn2(li, idx == n - 1)
        self.flush_pend()
        self.S.emit()


def make_consts():
    bf = ml_dtypes.bfloat16
    q = np.arange(128)[:, None]
    k = np.arange(128)[None, :]
    c = {}
    c['c_ident'] = np.eye(128, dtype=np.float32).astype(bf)
    c['c_tri'] = np.where(k <= q, 0.0, NEG).astype(np.float32).astype(bf)
    c['c_causneg'] = np.where(k <= q, 0.0, -BIG).astype(np.float32)
    r = np.zeros((128, 128), np.float32)
    inv = (500000.0 ** (-np.arange(8, dtype=np.float32) / 8.0)).astype(np.float32)
    invcol = np.zeros((128, 1), np.float32)
    for base in (0, 64):
        for d in range(8):
            r[base + d + 8, base + d] = -1.0
            r[base + d, base + d + 8] = 1.0
            invcol[base + d, 0] = inv[d]
            invcol[base + d + 8, 0] = inv[d]
    c['c_r128'] = r.astype(bf)
    c['c_invcol'] = invcol
    ee = np.zeros((16, 16, 128), np.float32)
    for n in range(16):
        ee[n, n, :] = 1.0
    c['c_ee'] = ee.reshape(16, 16 * 128).astype(bf)
    no = np.zeros((128, NT, 16), np.float32)
    for qt in range(NT):
        for n in range(16):
            if not (n < qt // 2):
                no[:, qt, n] = -BIG
    c['c_negown'] = no.reshape(128, NT * 16)
    kk = np.arange(128)[:, None]
    qq = np.arange(128)[None, :]
    dm = np.concatenate([(kk >= qq), (kk <= qq)], axis=1).astype(np.float32)
    c['c_dmask'] = dm.astype(bf)
    return c


_CACHE = {}


def get_nc(layers, dbg=False, stop=None):
    key = (tuple(layers), str(dbg), stop)
    if key not in _CACHE:
        nc = bass.Bass("TRN2", target_bir_lowering=False)
        b = Builder(nc, list(layers), dbg)
        b.build_all(stop)
        _CACHE[key] = nc
    return _CACHE[key]


def make_in_maps(inputs, n_cores=8):
    f = lambda a: np.ascontiguousarray(np.asarray(a, dtype=np.float32))
    consts = make_consts()
    shared = {
        'mem_ln_g': f(inputs['mem_ln_g']).reshape(1, DM),
        'mem_ln_b': f(inputs['mem_ln_b']).reshape(1, DM),
        'w_in_a': f(inputs['w_in_a']),
        'idx_kn_g': f(inputs['idx_kn_g']).reshape(2, 64, 1),
        'idx_kn_b': f(inputs['idx_kn_b']).reshape(2, 64, 1),
        'w_in_b': f(inputs['w_in_b']),
        'w_in_c': f(inputs['w_in_c']),
        'w_mem_kv': f(inputs['w_mem_kv']),
        'w_out': f(inputs['w_out']),
        'ln1_g': f(inputs['ln1_g']), 'ln1_b': f(inputs['ln1_b']),
        'w_gate_up': f(inputs['w_gate_up']),
        'w_down': f(inputs['w_down']),
        'ln2_g': f(inputs['ln2_g']), 'ln2_b': f(inputs['ln2_b']),
    }
    shared.update(consts)
    x = f(inputs['x'])
    mem = f(inputs['mem'])
    pos = np.ascontiguousarray(np.asarray(inputs['positions'], dtype=np.int32))
    maps = []
    for b in range(n_cores):
        m = dict(shared)
        m['x'] = x[b]
        m['mem'] = mem[b]
        m['positions'] = pos[b:b + 1]
        maps.append(m)
    return maps


def kernel(**inputs):
    nc = get_nc(range(DEPTH))
    maps = make_in_maps(inputs, 8)
    res = run_bass_kernel_spmd(nc, maps, core_ids=list(range(8)))
    return np.stack([np.asarray(r['y'], dtype=np.float32) for r in res.results], axis=0)
```

```python
import numpy as np
import ml_dtypes
import concourse.bass as bass
import concourse.mybir as mybir
from concourse.bass_utils import run_bass_kernel_spmd
from contextlib import ExitStack

F32 = mybir.dt.float32
BF16 = mybir.dt.bfloat16
I32 = mybir.dt.int32
ALU = mybir.AluOpType
AF = mybir.ActivationFunctionType
AX = mybir.AxisListType

SEQ = 4096
DM = 1024
NT = SEQ // 128
NG = SEQ // 512
DFF = 2816
NFC = DFF // 128
DEPTH = 4
ALPHA = (2 * DEPTH) ** 0.25
LN_EPS = 1e-5
NEG = -30000.0
BIG = 1.0e30
WIDTH_A = 3144
WIDTH_BC = 2560
TOPK = 256
NBIS = 14
ROPE_DEFER = 0
FP8 = mybir.dt.float8e5
NEG8 = -32768.0

ENGS = ['pe', 'act', 'dve', 'pool', 'sp']


class Op:
    __slots__ = ('eng', 'fn', 'deps', 'needed', 'dma', 'sig', 'slot', 'target', 'prev_target')

    def __init__(self, eng, fn, dma):
        self.eng = eng
        self.fn = fn
        self.dma = dma
        self.deps = []
        self.needed = False
        self.sig = 0
        self.slot = -1
        self.target = 0
        self.prev_target = 0


class Sched:
    def __init__(self, nc, n_dma_slots=32):
        self.nc = nc
        self.ops = {e: [] for e in ENGS}
        self.last_w = {}
        self.rd_eng = {}
        self.rd_dma = {}
        self.fence = {e: [] for e in ENGS}
        self.n_dma_slots = n_dma_slots
        self.qslots = {'sp': list(range(0, n_dma_slots // 2)), 'act': list(range(0, n_dma_slots // 2)),
                       'pool': list(range(n_dma_slots // 2, n_dma_slots))}
        self.slot_uses = [0] * n_dma_slots
        self.next_slot = {'sp': 0, 'act': 0, 'pool': 0}
        self.all_dma = []

    def add(self, eng, fn, reads=(), writes=(), dma=False):
        op = Op(eng, fn, dma)
        deps = {}
        excl = [r for r in reads if isinstance(r, str) and r.startswith('ps')]
        if excl:
            writes = list(writes) + excl
        for r in reads:
            w = self.last_w.get(r)
            if w is not None:
                deps[id(w)] = w
        for r in writes:
            w = self.last_w.get(r)
            if w is not None:
                deps[id(w)] = w
            for rd in self.rd_eng.get(r, {}).values():
                deps[id(rd)] = rd
            for rd in self.rd_dma.get(r, ()):
                deps[id(rd)] = rd
        for f in self.fence[eng]:
            deps[id(f)] = f
        self.fence[eng] = []
        for d in deps.values():
            if d.eng == 'pe' and eng == 'pe' and not d.dma and not dma:
                continue
            op.deps.append(d)
            d.needed = True
        for r in reads:
            if dma:
                self.rd_dma.setdefault(r, []).append(op)
            else:
                self.rd_eng.setdefault(r, {})[eng] = op
        for r in writes:
            self.last_w[r] = op
            self.rd_eng[r] = {}
            self.rd_dma[r] = []
        if dma:
            qs = self.qslots[eng]
            key = 'pool' if eng == 'pool' else 'sp'
            k = qs[self.next_slot[key] % len(qs)]
            self.next_slot[key] += 1
            op.slot = k
            op.prev_target = 16 * self.slot_uses[k]
            self.slot_uses[k] += 1
            op.target = 16 * self.slot_uses[k]
            op.needed = True
            self.all_dma.append(op)
        self.ops[eng].append(op)
        return op

    def barrier(self):
        lasts = []
        for e in ENGS:
            for op in reversed(self.ops[e]):
                if not op.dma:
                    lasts.append(op)
                    break
        lasts += self.all_dma[-self.n_dma_slots:]
        for e in ENGS:
            self.fence[e] = list(lasts)
        for d in lasts:
            d.needed = True

    def emit(self, final_wait_eng='sp'):
        nc = self.nc
        with ExitStack() as es:
            esem = {e: es.enter_context(nc.semaphore('s_' + e)) for e in ENGS}
            dsem = [es.enter_context(nc.semaphore('d_%d' % k)) for k in range(self.n_dma_slots)]
            for e in ENGS:
                c = 0
                for op in self.ops[e]:
                    if not op.dma and op.needed:
                        c += 1
                        op.sig = c
            block = es.enter_context(nc.Block())

            def run(ename, eobj):
                waited = {}

                def wait(key, sem, val):
                    if val <= 0 or waited.get(key, 0) >= val:
                        return
                    eobj.wait_ge(sem, val)
                    waited[key] = val

                for op in self.ops[ename]:
                    for d in op.deps:
                        if d.dma:
                            wait(('d', d.slot), dsem[d.slot], d.target)
                        else:
                            wait(('e', d.eng), esem[d.eng], d.sig)
                    if op.dma:
                        wait(('d', op.slot), dsem[op.slot], op.prev_target)
                        op.fn(eobj).then_inc(dsem[op.slot], 16)
                    else:
                        ins = op.fn(eobj)
                        if op.needed:
                            ins.then_inc(esem[ename], 1)
                if ename == final_wait_eng:
                    for k in range(self.n_dma_slots):
                        wait(('d', k), dsem[k], 16 * self.slot_uses[k])
                    for e2 in ENGS:
                        if e2 != ename:
                            n = sum(1 for o in self.ops[e2] if o.sig)
                            wait(('e', e2), esem[e2], n)

            @block.tensor
            def _(e):
                run('pe', e)

            @block.scalar
            def _(e):
                run('act', e)

            @block.vector
            def _(e):
                run('dve', e)

            @block.gpsimd
            def _(e):
                run('pool', e)

            @block.sync
            def _(e):
                run('sp', e)


class T:
    __slots__ = ('ap', 'res')

    def __init__(self, ap, res):
        self.ap = ap
        self.res = res if isinstance(res, (list, tuple)) else [res]

    def __getitem__(self, k):
        return T(self.ap[k], self.res)

    def r(self, pattern, **kw):
        return T(self.ap.rearrange(pattern, **kw), self.res)

    def named(self, res):
        return T(self.ap, res)


def _res(*ts):
    out = []
    for t in ts:
        if isinstance(t, T):
            out += t.res
    return out


def _a(x):
    return x.ap if isinstance(x, T) else x


class Carver:
    def __init__(self, arena_ap, prefix):
        self.a = arena_ap
        self.off = 0
        self.prefix = prefix
        self.n = 0
        self.cap = arena_ap.shape[1]

    def _take(self, nwords):
        nwords = (nwords + 7) // 8 * 8
        o = self.off
        self.off += nwords
        assert self.off <= self.cap, ('arena overflow', self.prefix, self.off, self.cap)
        self.n += 1
        return self.a[:, o:o + nwords], '%s_%d' % (self.prefix, self.n)

    def f32(self, n, parts=128, name=None):
        ap, nm = self._take(n)
        return T(ap[0:parts, 0:n], name or nm)

    def i32(self, n, parts=128):
        ap, nm = self._take(n)
        return T(ap[0:parts, 0:n].bitcast(I32), nm)

    def fp8(self, n, parts=128, name=None):
        ap, nm = self._take((n + 3) // 4)
        return T(ap.bitcast(FP8)[0:parts, 0:n], name or nm)

    def bf16(self, n, parts=128, name=None):
        ap, nm = self._take((n + 1) // 2)
        return T(ap.bitcast(BF16)[0:parts, 0:n], name or nm)


class MK:
    def __init__(self, nc):
        self.nc = nc
        self.S = Sched(nc)
        self.phase_id = 0
        rem = nc.sbuf_bytes_remaining
        self.PERS_WORDS = 6144
        self.ARENA_WORDS = (rem - 1024) // 4 - self.PERS_WORDS
        self.ARENA_WORDS = self.ARENA_WORDS // 8 * 8
        pers = nc.alloc_sbuf_tensor("pers", [128, self.PERS_WORDS], F32)
        arena = nc.alloc_sbuf_tensor("arena", [128, self.ARENA_WORDS], F32)
        self.pers = Carver(pers.ap(), 'pers')
        self.arena_ap = arena.ap()
        pst = nc.alloc_psum_tensor("psum", [128, 4096], F32)
        psap = pst.ap()
        self.ps = [T(psap[:, b * 512:(b + 1) * 512], 'ps%d' % b) for b in range(8)]
        self.psap = psap
        self.rr = {}

    def ps2(self, b):
        return T(self.psap[:, b * 512:(b + 2) * 512], ['ps%d' % b, 'ps%d' % (b + 1)])

    def flush_pend(self):
        pend = getattr(self, 'pend', [])
        self.pend = []
        for (part2, store) in pend:
            store(part2())

    def phase(self, name):
        self.flush_pend()
        self.S.barrier()
        self.phase_id += 1
        return Carver(self.arena_ap, 'ph%d%s' % (self.phase_id, name))

    def u(self):
        self.ucnt = getattr(self, 'ucnt', 0) + 1
        return 'u%d' % self.ucnt

    def rot(self, key, items):
        i = self.rr.get(key, 0)
        self.rr[key] = i + 1
        return items[i % len(items)]

    def mm(self, out, lhsT, rhs, start=True, stop=True, skip=False):
        o, l, r = out.ap, lhsT.ap, rhs.ap
        if skip:
            f = lambda e: e.matmul(o, lhsT=l, rhs=r, start=start, stop=stop, skip_group_check=True)
        else:
            f = lambda e: e.matmul(o, lhsT=l, rhs=r, start=start, stop=stop)
        return self.S.add('pe', f, reads=_res(lhsT, rhs), writes=_res(out))

    def tr(self, out, in_, ident):
        o, i, d = out.ap, in_.ap, ident.ap
        return self.S.add('pe', lambda e: e.transpose(o, i, d), reads=_res(in_, ident), writes=_res(out))

    def act(self, out, in_, func, scale=1.0, bias=None, accum=None):
        o, i = out.ap, in_.ap
        kw = {}
        if bias is not None:
            kw['bias'] = _a(bias)
        if accum is not None:
            kw['accum_out'] = accum.ap
        sc = _a(scale)
        return self.S.add('act', lambda e: e.activation(out=o, in_=i, func=func, scale=sc, **kw),
                          reads=_res(in_, scale, bias), writes=_res(out, accum))

    def tsc(self, eng, out, in0, s1, s2, op0, op1=None, accum=None):
        o, i = out.ap, in0.ap
        a1, a2 = _a(s1), _a(s2)
        kw = {}
        if op1 is not None:
            kw['op1'] = op1
        if accum is not None:
            kw['accum_out'] = accum.ap
        if o.dtype == FP8:
            kw['saturate'] = False
        return self.S.add(eng, lambda e: e.tensor_scalar(out=o, in0=i, scalar1=a1, scalar2=a2, op0=op0, **kw),
                          reads=_res(in0, s1, s2), writes=_res(out, accum))

    def tt(self, eng, out, in0, in1, op):
        o, a, b = out.ap, in0.ap, in1.ap
        return self.S.add(eng, lambda e: e.tensor_tensor(out=o, in0=a, in1=b, op=op),
                          reads=_res(in0, in1), writes=_res(out))

    def stt(self, eng, out, in0, scalar, in1, op0, op1):
        o, a, b = out.ap, in0.ap, in1.ap
        s = _a(scalar)
        return self.S.add(eng, lambda e: e.scalar_tensor_tensor(out=o, in0=a, scalar=s, in1=b, op0=op0, op1=op1),
                          reads=_res(in0, in1, scalar), writes=_res(out))

    def cp(self, eng, out, in_):
        o, i = out.ap, in_.ap
        if eng == 'act':
            return self.S.add('act', lambda e: e.copy(out=o, in_=i), reads=_res(in_), writes=_res(out))
        if o.dtype == FP8:
            return self.S.add(eng, lambda e: e.tensor_copy(out=o, in_=i, saturate=False), reads=_res(in_), writes=_res(out))
        return self.S.add(eng, lambda e: e.tensor_copy(out=o, in_=i), reads=_res(in_), writes=_res(out))

    def memset(self, eng, out, val):
        o = out.ap
        return self.S.add(eng, lambda e: e.memset(o, val), writes=_res(out))

    def red(self, eng, out, in_, op, axis=AX.X):
        o, i = out.ap, in_.ap
        return self.S.add(eng, lambda e: e.tensor_reduce(out=o, in_=i, axis=axis, op=op),
                          reads=_res(in_), writes=_res(out))

    def dma(self, q, out, in_):
        o, i = out.ap, in_.ap
        return self.S.add(q, lambda e: e.dma_start(out=o, in_=i), reads=_res(in_), writes=_res(out), dma=True)


class Builder(MK):
    def __init__(self, nc, layers, dbg=False):
        super().__init__(nc)
        self.layers = layers
        self.dbg = dbg
        dt = nc.dram_tensor
        ext = lambda n, s, d=F32: dt(n, s, d, kind="ExternalInput").ap()
        self.x = ext("x", [SEQ, DM])
        self.mem = ext("mem", [256, DM])
        self.positions = ext("positions", [1, SEQ], I32)
        self.mem_ln_g = ext("mem_ln_g", [1, DM])
        self.mem_ln_b = ext("mem_ln_b", [1, DM])
        self.w_in_a = ext("w_in_a", [2, DM, WIDTH_A])
        self.idx_kn_g = ext("idx_kn_g", [2, 64, 1])
        self.idx_kn_b = ext("idx_kn_b", [2, 64, 1])
        self.w_in_b = ext("w_in_b", [1, DM, WIDTH_BC])
        self.w_in_c = ext("w_in_c", [1, DM, WIDTH_BC])
        self.w_mem_kv = ext("w_mem_kv", [DEPTH, DM, 512])
        self.w_out = ext("w_out", [DEPTH, DM, DM])
        self.ln1_g = ext("ln1_g", [DEPTH, DM])
        self.ln1_b = ext("ln1_b", [DEPTH, DM])
        self.w_gate_up = ext("w_gate_up", [DEPTH, DM, 2 * DFF])
        self.w_down = ext("w_down", [DEPTH, DFF, DM])
        self.ln2_g = ext("ln2_g", [DEPTH, DM])
        self.ln2_b = ext("ln2_b", [DEPTH, DM])
        self.c_ident = ext("c_ident", [128, 128], BF16)
        self.c_tri = ext("c_tri", [128, 128], BF16)
        self.c_r128 = ext("c_r128", [128, 128], BF16)
        self.c_causneg = ext("c_causneg", [128, 128])
        self.c_invcol = ext("c_invcol", [128, 1])
        self.c_ee = ext("c_ee", [16, 16 * 128], BF16)
        self.c_negown = ext("c_negown", [128, NT * 16])
        self.c_dmask = ext("c_dmask", [128, 256], BF16)
        self.y = dt("y", [SEQ, DM], F32, kind="ExternalOutput").ap()
        itn = lambda n, s, d=BF16: dt(n, s, d, kind=("ExternalOutput" if (dbg and n in dbg) else "Internal")).ap()
        self.XT = itn("XT", [8, 128, SEQ])
        self.X1T = itn("X1T", [8, 128, SEQ])
        self.XF = [itn("XFa", [SEQ, DM], F32), itn("XFb", [SEQ, DM], F32)]
        self.X1F = itn("X1F", [SEQ, DM], F32)
        self.QT = itn("QT", [12 * 64, SEQ])
        self.KT = itn("KT", [12 * 64, SEQ])
        self.QMT = itn("QMT", [4 * 64, SEQ])
        self.QIT = itn("QIT", [8 * 64, SEQ])
        self.KIT = itn("KIT", [64, SEQ])
        self.VAP = itn("VAP", [12, 128, NT, 65])
        self.VD = itn("VD", [SEQ, 772])
        self.ND = itn("ND", [3, SEQ, 772], F32)
        self.AT = itn("AT", [NFC, 128, SEQ])
        self.COSD = itn("COSD", [128, SEQ], F32)
        self.SIND = itn("SIND", [128, SEQ], F32)
        D = lambda ap, res: T(ap, res)
        self.D = D
        p = self.pers
        self.ident = p.bf16(128)
        self.tri = p.bf16(128)
        self.r128 = p.bf16(128)
        self.causneg = p.f32(128)
        self.invcol = p.f32(1)
        self.ee = p.bf16(16 * 128, parts=16)
        self.negown = p.f32(NT * 16)
        self.dmask = p.bf16(256)
        self.negpi = p.f32(1)
        self.epsc = p.f32(1)
        self.ones64 = p.f32(64, parts=64)
        self.onesbf = p.bf16(128)
        self.ident8 = p.fp8(128)
        self.pow2 = p.f32(NBIS)
        self.memnT = p.bf16(8 * 256)
        self.mkT = p.bf16(4 * 256, parts=64)
        self.mvaug = p.bf16(2 * 4 * 65)
        self.wi_all = p.f32(NT * 8)
        self.absw = p.f32(NT * 8)
        self.sgn = p.f32(NT * 8)
        self.kng = p.f32(1, parts=64)
        self.knb = p.f32(1, parts=64)

    def prologue(self):
        for (t, src) in [(self.ident, self.c_ident), (self.tri, self.c_tri), (self.r128, self.c_r128),
                         (self.causneg, self.c_causneg), (self.invcol, self.c_invcol), (self.ee, self.c_ee),
                         (self.negown, self.c_negown), (self.dmask, self.c_dmask)]:
            self.dma('sp', t, T(src, 'const'))
        self.memset('dve', self.negpi, -float(np.pi))
        self.memset('dve', self.epsc, LN_EPS)
        self.memset('dve', self.ones64, 1.0 / 64.0)
        self.memset('dve', self.onesbf, 1.0)
        self.memset('dve', self.mvaug, 1.0)
        for it in range(NBIS):
            self.memset('dve', self.pow2[:, it:it + 1], 2.0 ** (-(it + 1)))
        c = self.phase('rope')
        self.cp('dve', self.ident8, self.ident)
        H = 2048
        posi = c.i32(H)
        ang = c.f32(H)
        u = c.f32(H)
        kf = c.f32(H)
        tab = c.f32(H)
        for half in range(2):
            sl = slice(half * H, (half + 1) * H)
            self.dma('sp', posi, T(self.positions[:, sl].partition_broadcast(128), 'const'))
            self.cp('dve', ang, posi)
            self.tsc('dve', ang, ang, self.invcol[:, 0:1], None, ALU.mult)
            for (off, dst) in [(0.5, self.SIND), (0.75, self.COSD)]:
                self.tsc('dve', u, ang, 1.0 / (2.0 * np.pi), off, ALU.mult, ALU.add)
                self.cp('dve', posi, u)
                self.cp('dve', kf, posi)
                self.tt('dve', u, u, kf, ALU.subtract)
                self.tsc('dve', kf, u, 0.0, None, ALU.is_lt)
                self.tt('dve', u, u, kf, ALU.add)
                self.act(tab, u, AF.Sin, scale=2.0 * float(np.pi), bias=self.negpi[:, 0:1])
                self.dma('sp', T(dst[:, sl], self.u()), tab)
        c = self.phase('memln')
        g_t = c.f32(DM)
        b_t = c.f32(DM)
        self.dma('sp', g_t, T(self.mem_ln_g.partition_broadcast(128), 'const'))
        self.dma('sp', b_t, T(self.mem_ln_b.partition_broadcast(128), 'const'))
        lnb = self.ln_bufs(c)
        for mt in range(2):
            z = lnb['z']
            self.dma('sp', z, T(self.mem[mt * 128:(mt + 1) * 128, :], 'const'))
            xTt = self.ln_core(lnb, z, g_t, b_t, None)
            self.cp('dve', self.memnT.r("p (c t) -> p c t", c=8)[:, :, mt * 128:(mt + 1) * 128], xTt)
        c = self.phase('x2xT')
        xin = [c.f32(DM), c.f32(DM)]
        xb = [c.bf16(DM), c.bf16(DM)]
        xt = [c.bf16(DM), c.bf16(DM)]
        for tt_ in range(NT):
            a, b, o = xin[tt_ % 2], xb[tt_ % 2], xt[tt_ % 2]
            self.dma('sp', a, T(self.x[tt_ * 128:(tt_ + 1) * 128, :], 'const'))
            self.cp('act', b, a)
            pst = self.rot('pstr', [self.ps[4], self.ps[5]])
            psb = T(pst.ap.bitcast(BF16), pst.res)
            for cc in range(8):
                self.tr(psb[:, cc * 128:(cc + 1) * 128], b[:, cc * 128:(cc + 1) * 128], self.ident)
            self.cp('dve', o, psb[:, 0:1024])
            self.dma('pool', T(self.XT[:, :, tt_ * 128:(tt_ + 1) * 128].rearrange("c p t -> p c t"), self.u()),
                     o.r("p (c t) -> p c t", c=8))

    def ln_bufs(self, c, nz=0):
        return dict(z=c.f32(DM), z2=[c.f32(DM) for _ in range(nz)], st=c.f32(12), mv=c.f32(2), sd=c.f32(1), rs=c.f32(1),
                    xo=[c.f32(DM), c.f32(DM)], xb=[c.bf16(DM), c.bf16(DM)], xTt=[c.bf16(DM), c.bf16(DM)])

    def ln_core(self, lnb, z, g_t, b_t, out_f32_dram, trps=None, defer=False, notr=False):
        st, mv, sd, rs = lnb['st'], lnb['mv'], lnb['sd'], lnb['rs']
        xo = self.rot(('xo', id(lnb)), lnb['xo'])
        xb = self.rot(('xb', id(lnb)), lnb['xb'])
        xTt = self.rot(('xTt', id(lnb)), lnb['xTt'])
        za, zb, sa, sb_, ma = z.ap, z.ap, st.ap, st.ap, mv.ap
        self.S.add('dve', lambda e: e.bn_stats(out=sa[:, 0:6], in_=za[:, 0:512]), reads=_res(z), writes=_res(st))
        self.S.add('dve', lambda e: e.bn_stats(out=sa[:, 6:12], in_=za[:, 512:1024]), reads=_res(z), writes=_res(st))
        self.S.add('dve', lambda e: e.bn_aggr(out=ma, in_=sa), reads=_res(st), writes=_res(mv))
        self.act(sd, mv[:, 1:2], AF.Sqrt, bias=self.epsc[:, 0:1])
        sda, rsa = sd.ap, rs.ap
        self.S.add('dve', lambda e: e.reciprocal(out=rsa, in_=sda), reads=_res(sd), writes=_res(rs))
        self.tsc('dve', z, z, mv[:, 0:1], rs[:, 0:1], ALU.subtract, ALU.mult)
        self.tt('dve', z, z, g_t, ALU.mult)
        self.tt('pool', xo, z, b_t, ALU.add)
        if out_f32_dram is not None:
            self.dma('pool', out_f32_dram, xo)
        if notr:
            return None
        self.cp('act', xb, xo)

        def part2():
            pst = self.rot('pstr', trps or [self.ps[4], self.ps[5]])
            psb = T(pst.ap.bitcast(BF16), pst.res)
            for cc in range(8):
                self.tr(psb[:, cc * 128:(cc + 1) * 128], xb[:, cc * 128:(cc + 1) * 128], self.ident)
            self.cp('act', xTt, psb[:, 0:1024])
            return xTt.r("p (c t) -> p c t", c=8)

        if defer:
            return part2
        return part2()

    def load_w_chunks(self, dst_bf, src_dram_fn, nchunks, ncols, stage, eng='pool'):
        for cc in range(nchunks):
            st = self.rot(('wst', id(stage[0])), stage)
            self.dma('sp', st[:, 0:ncols], src_dram_fn(cc))
            self.cp(eng, dst_bf[:, cc, :], st[:, 0:ncols])

    def phase_inproj(self, li, kind, w_in, j):
        c = self.phase('A')
        xT = c.bf16(8 * SEQ, name='xT').r("p (c t) -> p c t", c=8)
        for cc in range(8):
            self.dma('sp', xT[:, cc, :], T(self.XT[cc], 'const'))
        COS = c.f32(SEQ)
        SIN = c.f32(SEQ)
        self.dma('sp', COS, T(self.COSD, 'const'))
        self.dma('sp', SIN, T(self.SIND, 'const'))
        wst = [c.f32(1024), c.f32(1024)]
        wbf = [c.bf16(1024), c.bf16(1024)]
        qsb = [c.bf16(512), c.bf16(512)]
        t1 = [c.f32(512), c.f32(512)]
        t2 = [c.f32(512), c.f32(512)]
        osb = [c.bf16(512), c.bf16(512), c.bf16(512)]
        w_pcn = w_in.rearrange("(c p) n -> p c n", p=128)

        jobs = []
        self.ropetails = []

        def fm_prep(col0, M):
            ws = self.rot('wst', wst)
            wb = self.rot('wbf', wbf)
            wsv = ws[:, 0:8 * M].r("p (c n) -> p c n", c=8)
            wbv = wb[:, 0:8 * M].r("p (c n) -> p c n", c=8)
            self.dma('sp', wsv, T(w_pcn[:, :, col0:col0 + M], 'const'))
            self.cp('pool', wbv, wsv)
            return wbv

        def fm_job(col0, M, mode, dst):
            jobs.append((col0, M, mode, dst))

        def run_jobs():
            nxt = fm_prep(jobs[0][0], jobs[0][1])
            for i, (col0, M, mode, dst) in enumerate(jobs):
                wbv = nxt
                if i + 1 < len(jobs):
                    nxt = fm_prep(jobs[i + 1][0], jobs[i + 1][1])
                fm_run(wbv, M, mode, dst)
            while self.ropetails:
                self.ropetails.pop(0)()

        def fm_run(wbv, M, mode, dst):
            for G in range(NG):
                gs = slice(G * 512, (G + 1) * 512)
                ps = self.rot('psA', self.ps[0:3])
                for cc in range(8):
                    self.mm(ps[0:M, :], wbv[:, cc, :], xT[:, cc, gs], start=(cc == 0), stop=(cc == 7))
                o = self.rot('osb', osb)
                if mode == 'plain':
                    self.cp('act', o[0:M, :], ps[0:M, :])
                    self.dma('pool', T(dst[:, gs], self.u()), o[0:M, :])
                elif mode == 'rope':
                    q = self.rot('qsb', qsb)
                    self.cp('act', q[0:M, :], ps[0:M, :])

                    def tail(ps=ps, q=q, o=o, gs=gs, dst=dst, M=M):
                        a1 = self.rot('t1', t1)
                        a2 = self.rot('t2', t2)
                        pr = self.rot('psR', self.ps[3:5])
                        self.mm(pr[0:M, :], self.r128[0:M, 0:M], q[0:M, :])
                        self.tt('dve', a1[0:M, :], ps[0:M, :], COS[0:M, gs], ALU.mult)
                        self.tt('dve', a2[0:M, :], pr[0:M, :], SIN[0:M, gs], ALU.mult)
                        self.tt('pool', o[0:M, :], a1[0:M, :], a2[0:M, :], ALU.add)
                        self.dma('pool', T(dst[:, gs], self.u()), o[0:M, :])

                    tails = self.ropetails
                    tails.append(tail)
                    if len(tails) > ROPE_DEFER:
                        tails.pop(0)()

        def fm_region(col0, ncols, mode, dst):
            for i in range(ncols // 128):
                fm_job(col0 + i * 128, 128, mode, dst[i * 128:(i + 1) * 128, :])

        if self.stop == 'A00':
            return
        if self.stop in ('A01', 'A02', 'A03'):
            ws = self.rot('wst', wst)
            wb = self.rot('wbf', wbf)
            wsv = ws[:, 0:8 * 128].r("p (c n) -> p c n", c=8)
            wbv = wb[:, 0:8 * 128].r("p (c n) -> p c n", c=8)
            self.dma('sp', wsv, T(w_pcn[:, :, 0:128], 'const'))
            self.cp('pool', wbv, wsv)
            if self.stop == 'A01':
                return
            gs = slice(0, 512)
            ps = self.ps[0]
            for cc in range(8):
                self.mm(ps, wbv[:, cc, :], xT[:, cc, gs], start=(cc == 0), stop=(cc == 7))
            o = osb[0]
            if self.stop == 'A02':
                self.cp('act', o, ps)
            else:
                q = qsb[0]
                self.cp('act', q, ps)
                pr = self.ps[3]
                self.mm(pr, self.r128, q)
                self.tt('dve', t1[0], ps, COS[:, gs], ALU.mult)
                self.tt('dve', t2[0], pr, SIN[:, gs], ALU.mult)
                self.tt('pool', o, t1[0], t2[0], ALU.add)
            self.dma('pool', T(self.QT[0:128, gs], self.u()), o)
            return
        fm_region(0, 768, 'rope', self.QT)
        fm_region(768, 768, 'rope', self.KT)
        if kind == 0:
            fm_region(2304, 512, 'rope', self.QIT)
            fm_region(2888, 256, 'plain', self.QMT)
            run_jobs()
            self.inproj_ki(c, w_pcn, xT, COS, SIN, j)
        else:
            fm_region(2304, 256, 'plain', self.QMT)
            run_jobs()
        if self.stop == 'A2':
            return

        NV = 776 if kind == 0 else 768
        wv = c.bf16(8 * NV).r("p (c n) -> p c n", c=8)
        vst = [c.f32(NV), c.f32(NV)]
        for cc in range(8):
            st = self.rot('vst', vst)
            self.dma('sp', st[:, 0:768], T(w_in[cc * 128:(cc + 1) * 128, 1536:2304], 'const'))
            if kind == 0:
                self.dma('sp', st[:, 768:776], T(w_in[cc * 128:(cc + 1) * 128, 2816:2824], 'const'))
            self.cp('pool', wv[:, cc, :], st[:, 0:NV])
        nvh, dv = (4, 192) if kind == 1 else (12, 64)
        vaug = [c.bf16(nvh * (dv + 1)), c.bf16(nvh * (dv + 1))]
        for v in vaug:
            self.memset('pool', v, 1.0)
        for tt_ in range(NT):
            ts = slice(tt_ * 128, (tt_ + 1) * 128)
            psv = self.ps2(6)
            for (n0, n1) in [(0, 512), (512, NV)]:
                for cc in range(8):
                    self.mm(psv[:, n0:n1], xT[:, cc, ts], wv[:, cc, n0:n1], start=(cc == 0), stop=(cc == 7))
            va = self.rot('vaug', vaug)
            va3 = va.r("p (h e) -> p h e", h=nvh)
            self.cp('act', va3[:, :, 0:dv], psv[:, 0:768].r("p (h d) -> p h d", h=nvh))
            if kind == 0:
                self.cp('dve', self.wi_all[:, tt_ * 8:(tt_ + 1) * 8], psv[:, 768:776])
            if kind == 1:
                self.dma('pool', T(self.VD[ts, :], self.u()), va)
            else:
                self.dma('pool', T(self.VAP[:, :, tt_, :].rearrange("h p e -> p h e"), self.u()), va3)

        if self.stop == 'A3':
            return
        wm = self.w_mem_kv[li]
        wmb = c.bf16(8 * 512).r("p (c n) -> p c n", c=8)
        mst = [c.f32(512), c.f32(512)]
        for cc in range(8):
            st = self.rot('mst', mst)
            self.dma('sp', st, T(wm[cc * 128:(cc + 1) * 128, :], 'const'))
            self.cp('pool', wmb[:, cc, :], st)
        memn = self.memnT.r("p (c t) -> p c t", c=8)
        mk3 = self.mkT.r("p (h t) -> p h t", h=4)
        for h in range(4):
            ps = self.rot('psA', self.ps[0:3])
            for cc in range(8):
                self.mm(ps[0:64, 0:256], wmb[:, cc, h * 64:(h + 1) * 64], memn[:, cc, :], start=(cc == 0), stop=(cc == 7))
            self.cp('act', mk3[:, h, :], ps[0:64, 0:256])
        mv4 = self.mvaug.r("p (t h e) -> p t h e", t=2, h=4)
        for mt in range(2):
            ps = self.rot('psA', self.ps[0:3])
            for cc in range(8):
                self.mm(ps[:, 0:256], memn[:, cc, mt * 128:(mt + 1) * 128], wmb[:, cc, 256:512], start=(cc == 0), stop=(cc == 7))
            self.cp('act', mv4[:, mt, :, 0:64], ps[:, 0:256].r("p (h d) -> p h d", h=4))

    def inproj_ki(self, c, w_pcn, xT, COS, SIN, j):
        self.dma('sp', self.kng, T(self.idx_kn_g[j], 'const'))
        self.dma('sp', self.knb, T(self.idx_kn_b[j], 'const'))
        ws = c.f32(8 * 64).r("p (c n) -> p c n", c=8)
        wb = c.bf16(8 * 64).r("p (c n) -> p c n", c=8)
        self.dma('sp', ws, T(w_pcn[:, :, 2824:2888], 'const'))
        self.cp('pool', wb, ws)
        P = 64
        xs = c.f32(512, parts=P)
        x2 = c.f32(512, parts=P)
        msb = c.f32(512, parts=P)
        var = c.f32(512, parts=P)
        xn = c.f32(512, parts=P)
        xnb = c.bf16(512, parts=P)
        a1 = c.f32(512, parts=P)
        a2 = c.f32(512, parts=P)
        o = [c.bf16(512, parts=P), c.bf16(512, parts=P)]
        for G in range(NG):
            gs = slice(G * 512, (G + 1) * 512)
            ps = self.rot('psA', self.ps[0:3])
            for cc in range(8):
                self.mm(ps[0:P, :], wb[:, cc, :], xT[:, cc, gs], start=(cc == 0), stop=(cc == 7))
            self.cp('act', xs, ps[0:P, :])
            self.act(x2, ps[0:P, :], AF.Square)
            pm = self.ps[3]
            pe2 = self.ps[4]
            self.mm(pm[0:P, :], self.ones64, xs)
            self.mm(pe2[0:P, :], self.ones64, x2)
            self.act(msb, pm[0:P, :], AF.Square)
            self.tt('dve', var, pe2[0:P, :], msb, ALU.subtract)
            self.act(var, var, AF.Sqrt, bias=self.epsc[0:P, 0:1])
            va = var.ap
            self.S.add('dve', lambda e, va=va: e.reciprocal(out=va, in_=va), reads=_res(var), writes=_res(var))
            self.tt('dve', xn, xs, pm[0:P, :], ALU.subtract)
            self.tt('dve', xn, xn, var, ALU.mult)
            self.tsc('dve', xn, xn, self.kng[:, 0:1], self.knb[:, 0:1], ALU.mult, ALU.add)
            self.cp('act', xnb, xn)
            pr = self.ps[5]
            self.mm(pr[0:P, :], self.r128[0:P, 0:P], xnb)
            self.tt('dve', a1, xn, COS[0:P, gs], ALU.mult)
            self.tt('dve', a2, pr[0:P, :], SIN[0:P, gs], ALU.mult)
            oo = self.rot('kio', o)
            self.tt('pool', oo, a1, a2, ALU.add)
            self.dma('pool', T(self.KIT[:, gs], self.u()), oo)

    def attn_head(self, G, kT, qTg, vaugh, nkt, causal, bias_fn, mix3, col0, pT, scale=0.125):
        po = self.rot('psO', [self.ps[3], self.ps[4]])
        pend = []
        first = [True]

        def pv(kt, c0, pt):
            for jj in range(c0, 4):
                self.mm(po[:, jj * 65:(jj + 1) * 65], pt[:, jj * 128:(jj + 1) * 128], vaugh[:, kt, :],
                        start=first[0], stop=(kt == nkt - 1 and jj == 3))
                first[0] = False

        for kt in range(nkt):
            c0 = max(0, kt - 4 * G) if causal else 0
            psS = self.rot('psS', self.ps[0:3])
            cols = slice(c0 * 128, 512)
            biases = bias_fn(kt, c0, psS) if bias_fn is not None else []
            self.mm(psS[:, cols], kT[:, kt * 128:(kt + 1) * 128], qTg[:, cols], start=True, stop=(len(biases) == 0))
            for bi, (o, l, r) in enumerate(biases):
                self.mm(o, l, r, start=False, stop=(bi == len(biases) - 1))
            pt = self.rot('pT', pT)
            self.act(pt[:, cols], psS[:, cols], AF.Exp, scale=scale)
            pend.append((kt, c0, pt))
            if len(pend) > 1:
                pv(*pend.pop(0))
            if kt == 2 and getattr(self, 'mid_cb', None) is not None:
                cb, self.mid_cb = self.mid_cb, None
                cb()
        while pend:
            pv(*pend.pop(0))
        if getattr(self, 'mid_cb', None) is not None:
            cb, self.mid_cb = self.mid_cb, None
            cb()
        self.flush_pend()
        po3 = po[:, 0:260].r("p (j e) -> p j e", j=4)
        rden = self.rot('rden', self.rdens)
        lnd = self.rot('lnd', self.lnds)
        self.act(lnd, po3[:, :, 64], AF.Ln)
        self.act(rden, lnd, AF.Exp, scale=-1.0)
        for jj in range(4):
            self.act(mix3[:, jj, col0:col0 + 64], po3[:, jj, 0:64], AF.Copy, scale=rden[:, jj:jj + 1])

    def phase_attn(self, li, kind):
        c = self.phase('B')
        wout = c.bf16(8 * DM).r("p (c n) -> p c n", c=8)
        xin = [c.f32(DM), c.f32(DM)]
        self.load_w_chunks(wout, lambda cc: T(self.w_out[li][cc * 128:(cc + 1) * 128, :], 'const'), 8, DM, xin)
        g_t = c.f32(DM)
        b_t = c.f32(DM)
        self.dma('sp', g_t, T(self.ln1_g[li:li + 1, :].partition_broadcast(128), 'const'))
        self.dma('sp', b_t, T(self.ln1_b[li:li + 1, :].partition_broadcast(128), 'const'))
        lnb = self.ln_bufs(c)
        mix = c.bf16(4 * DM)
        mix3 = mix.r("p (j n) -> p j n", j=4)
        mixT = [c.bf16(DM), c.bf16(DM)]
        self.rdens = [c.f32(4), c.f32(4)]
        self.lnds = [c.f32(4), c.f32(4)]
        xres = self.x if self.is_first else self.XF[li % 2]
        if kind == 1:
            self.dilated_all(c, li)
        else:
            kTb = [c.bf16(SEQ, parts=64), c.bf16(SEQ, parts=64)]
            vab = [c.bf16(NT * 65), c.bf16(NT * 65)]
            qTb = [c.bf16(512, parts=64), c.bf16(512, parts=64)]
            pT = [c.bf16(512) for _ in range(4)]
            if kind == 0:
                I = c.f32(SEQ)
                Mb2 = [[c.fp8(SEQ) for _ in range(4)] for _ in range(2)]
                kiT = c.bf16(SEQ, parts=64)
                qiT = c.bf16(8 * 512, parts=64).r("p (h t) -> p h t", h=8)
                rbuf = [c.bf16(512) for _ in range(3)]
                diag = [c.bf16(8 * 128), c.bf16(8 * 128)]
                sm = dict(lo=c.f32(1), hi=c.f32(1), d0=c.f32(1), cth=c.f32(1), cnt=c.f32(1), step=c.f32(1), dtab=c.f32(NBIS))
                self.dma('sp', kiT, T(self.KIT, 'const'))
                self.act(self.absw, self.wi_all, AF.Abs)
                self.tsc('dve', self.sgn, self.wi_all, 0.0, None, ALU.is_ge)
                self.tsc('dve', self.sgn, self.sgn, 2.0, -1.0, ALU.mult, ALU.add)
            else:
                kmean = c.f32(16, parts=64)
                kmb = c.bf16(16, parts=64)
                self.memset('dve', kmb, 0.0)
                gm = c.f32(64)
                sel = c.f32(64)
                m8 = c.f32(8)
                selbs = [c.bf16(64), c.bf16(64)]
                selbT = [c.bf16(512, parts=16), c.bf16(512, parts=16)]
        for G in range(NG):
            gs = slice(G * 512, (G + 1) * 512)
            nkeys = (G + 1) * 512
            nkt = (G + 1) * 4
            if kind == 0:
                Mb = Mb2[G % 2]

                def idx_group(Gn):
                    gsn = slice(Gn * 512, (Gn + 1) * 512)
                    self.dma('sp', qiT, T(self.QIT[:, gsn].rearrange("(h d) t -> d h t", h=8), 'const'))

                if G == 0:
                    idx_group(0)
                    for jq in range(4):
                        self.dsa_index(0, jq, I, Mb2[0][jq], kiT, qiT, rbuf, diag, sm)
                if G + 1 < NG:
                    idx_group(G + 1)
            if kind != 1:
                def prep(h):
                    kT = self.rot('kTb', kTb)
                    va = self.rot('vab', vab)
                    qTg = self.rot('qTb', qTb)
                    self.dma('sp', kT[:, 0:nkeys], T(self.KT[h * 64:(h + 1) * 64, 0:nkeys], 'const'))
                    va3 = va.r("p (t e) -> p t e", t=NT)
                    self.dma('sp', va3[:, 0:nkt, :], T(self.VAP[h, :, 0:nkt, :], 'const'))
                    self.dma('sp', qTg, T(self.QT[h * 64:(h + 1) * 64, gs], 'const'))
                    sT = None
                    if kind == 2:
                        sT = self.rot('selbT', selbT)
                        sb_ = self.rot('selbuf', selbs)
                        gt = self.moba_gate(G, h, kT, qTg, kmean, kmb, gm, sel, m8, sb_, sT)
                        if h == 0:
                            gt()
                        else:
                            self.mid_cb = gt
                    return kT, va3, qTg, sT

                nxt = prep(0)
                for h in range(12):
                    kT, va3, qTg, sT = nxt
                    if kind == 0 and G + 1 < NG and h % 3 == 0:
                        self.dsa_index(G + 1, h // 3, I, Mb2[(G + 1) % 2][h // 3], kiT, qiT, rbuf, diag, sm)
                    if h + 1 < 12:
                        nxt = prep(h + 1)
                    if kind == 0:
                        def bias_fn(kt, c0, psS, Mb=Mb):
                            return [(psS[:, jj * 128:(jj + 1) * 128], Mb[jj][:, kt * 128:(kt + 1) * 128], self.ident8)
                                    for jj in range(c0, 4)]
                    else:
                        def bias_fn(kt, c0, psS, sT=sT):
                            n = kt // 2
                            out = []
                            run = None
                            for jj in range(c0, 4):
                                qt = 4 * G + jj
                                if qt // 2 == n:
                                    if run is not None:
                                        out.append(run)
                                        run = None
                                    if qt == kt:
                                        out.append((psS[:, jj * 128:(jj + 1) * 128], self.tri, self.ident))
                                else:
                                    if run is None:
                                        run = [jj, jj + 1]
                                    else:
                                        run[1] = jj + 1
                            if run is not None:
                                out.append(run)
                            res = []
                            for it in out:
                                if isinstance(it, list):
                                    cs = slice(it[0] * 128, it[1] * 128)
                                    res.append((psS[:, cs], self.ee[:, n * 128:(n + 1) * 128], sT[:, cs]))
                                else:
                                    res.append(it)
                            return res
                    self.attn_head(G, kT, qTg, va3, nkt, True, bias_fn, mix3, h * 64, pT)
            if kind == 1:
                qTb = self.dil_qTb
                pT = self.dil_pT
            mk3 = self.mkT.r("p (h t) -> p h t", h=4)
            mv4 = self.mvaug.r("p (t h e) -> p t h e", t=2, h=4)
            for m in range(4):
                qTg = self.rot('qTb', qTb)
                self.dma('sp', qTg, T(self.QMT[m * 64:(m + 1) * 64, gs], 'const'))
                self.attn_head(G, mk3[:, m, :], qTg, mv4[:, :, m, :], 2, False, None, mix3, 768 + m * 64, pT)
            for jq in range(4):
                tt_ = 4 * G + jq
                ts = slice(tt_ * 128, (tt_ + 1) * 128)
                if kind == 1:
                    self.dilated_combine(tt_, mix3[:, jq, 0:768])
                xi = self.rot('xin', xin)
                self.dma('sp', xi, T(xres[ts, :], 'const'))
                pst = self.ps[5]
                psb = T(pst.ap.bitcast(BF16), pst.res)
                for cc in range(8):
                    self.tr(psb[:, cc * 128:(cc + 1) * 128], mix3[:, jq, cc * 128:(cc + 1) * 128], self.ident)
                mT = self.rot('mixT', mixT)
                self.cp('act', mT, psb[:, 0:1024])
                mT3 = mT.r("p (c t) -> p c t", c=8)
                py = self.ps2(6)
                for nh in range(2):
                    for cc in range(8):
                        self.mm(py[:, nh * 512:(nh + 1) * 512], mT3[:, cc, :], wout[:, cc, nh * 512:(nh + 1) * 512],
                                start=(cc == 0), stop=(cc == 7))
                z = lnb['z']
                self.stt('dve', z, xi, ALPHA, py, ALU.mult, ALU.add)
                p2 = self.ln_core(lnb, z, g_t, b_t, T(self.X1F[ts, :], self.u()), trps=[self.ps[5]], defer=True)
                self.flush_pend()
                self.pend.append((p2, lambda xTt, ts=ts: self.dma(
                    'pool', T(self.X1T[:, :, ts].rearrange("c p t -> p c t"), self.u()), xTt)))

    def dsa_index(self, G, jq, I, Mbj, kiT, qiT, rbuf, diag, sm):
        qt = 4 * G + jq
        nk = (qt + 1) * 128
        dg = self.rot('diag', diag).r("p (h k) -> p h k", h=8)
        for h in range(8):
            self.tsc('pool', dg[:, h, :], self.ident, self.sgn[:, qt * 8 + h:qt * 8 + h + 1], None, ALU.mult)
        nch = (nk + 511) // 512
        LAG = 2
        for ch in range(nch):
            w = min(512, nk - ch * 512)
            pa = self.rot('psO', [self.ps[3], self.ps[4]])
            rr = [None] * 8
            for h in range(8 + LAG):
                if h < 8:
                    pd = self.rot('psS', self.ps[0:3])
                    self.mm(pd[:, 0:w], qiT[:, h, jq * 128:(jq + 1) * 128], kiT[:, ch * 512:ch * 512 + w])
                    rr[h] = self.rot('rbuf', rbuf)
                    self.act(rr[h][:, 0:w], pd[:, 0:w], AF.Relu, scale=self.absw[:, qt * 8 + h:qt * 8 + h + 1])
                if h >= LAG:
                    hh = h - LAG
                    self.mm(pa[:, 0:w], dg[:, hh, :], rr[hh][:, 0:w], start=(hh == 0), stop=(hh == 7))
            self.cp('dve', I[:, ch * 512:ch * 512 + w], pa[:, 0:w])
        dsl = slice(qt * 128, (qt + 1) * 128)
        lo, hi, d0, cth, cnt, step = sm['lo'], sm['hi'], sm['d0'], sm['cth'], sm['cnt'], sm['step']
        if qt >= 2:
            self.red('dve', lo, I[:, 0:nk], ALU.min)
            self.tt('pool', I[:, dsl], I[:, dsl], self.causneg, ALU.add)
            self.red('dve', hi, I[:, 0:nk], ALU.max)
            self.tt('dve', d0, hi, lo, ALU.subtract)
            dtab = sm['dtab']
            self.tsc('dve', dtab, self.pow2, d0[:, 0:1], None, ALU.mult)
            self.stt('dve', lo, d0, 0.5, lo, ALU.mult, ALU.add)
            for it in range(NBIS):
                self.tsc('dve', Mbj[:, 0:nk], I[:, 0:nk], lo[:, 0:1], 0.0, ALU.is_ge, ALU.add, accum=cnt)
                self.tsc('dve', step, cnt, TOPK - 0.5, 0.5, ALU.is_ge, ALU.subtract)
                self.stt('dve', lo, dtab[:, it:it + 1], step[:, 0:1], lo, ALU.mult, ALU.add)
        else:
            self.tt('pool', I[:, dsl], I[:, dsl], self.causneg, ALU.add)
            self.memset('dve', lo, -1.0e29)
        self.tsc('dve', Mbj[:, 0:nk], I[:, 0:nk], lo[:, 0:1], NEG8, ALU.is_lt, ALU.mult)

    def moba_gate(self, G, h, kT, qTg, kmean, kmb, gm, sel, m8, selb, sT):
        nb = 2 * G + 2
        self.red('dve', kmean[:, 0:nb], kT[:, 0:nb * 256].r("p (n k) -> p n k", k=256), ALU.add)
        self.tsc('dve', kmb[:, 0:nb], kmean[:, 0:nb], 1.0 / 256.0, None, ALU.mult)
        pg = self.rot('psS', self.ps[0:3])
        for jj in range(4):
            self.mm(pg[:, jj * 16:(jj + 1) * 16], qTg[:, jj * 128:(jj + 1) * 128], kmb, start=(jj == 0), stop=(jj == 3))
        self.tt('dve', gm, pg[:, 0:64], self.negown[:, G * 64:(G + 1) * 64], ALU.add)
        for jj in range(4):
            ma, ga = m8.ap, gm.ap[:, jj * 16:(jj + 1) * 16]
            self.S.add('dve', lambda e, ma=ma, ga=ga: e.max(out=ma, in_=ga), reads=_res(gm), writes=_res(m8))
            self.tsc('dve', sel[:, jj * 16:(jj + 1) * 16], gm[:, jj * 16:(jj + 1) * 16], m8[:, 2:3], None, ALU.is_ge)
        self.tsc('dve', selb, sel, -NEG, NEG, ALU.mult, ALU.add)
        def tail():
            pst = self.ps[5]
            psb = T(pst.ap.bitcast(BF16), pst.res)
            for jj in range(4):
                self.tr(psb[0:16, jj * 128:(jj + 1) * 128], selb[:, jj * 16:(jj + 1) * 16], self.ident)
            self.cp('act', sT, psb[0:16, 0:512])
        return tail

    def dilated_all(self, c, li):
        qd = [c.bf16(SEQ, parts=64) for _ in range(4)]
        kd = [c.bf16(SEQ, parts=64) for _ in range(4)]
        vsub = [c.bf16(772) for _ in range(4)]
        pTo = [c.bf16(512), c.bf16(512)]
        pTp = [c.bf16(512), c.bf16(512)]
        osb = [c.f32(772), c.f32(772)]
        dm4 = c.bf16(1024)
        self.dil_qTb = [c.bf16(512, parts=64), c.bf16(512, parts=64)]
        self.dil_pT = [c.bf16(512) for _ in range(4)]
        self.dil_nd = [c.f32(772) for _ in range(3)]
        for hh in range(4):
            self.cp('dve', dm4[:, hh * 128:(hh + 1) * 128], self.dmask[:, 0:128])
            self.cp('dve', dm4[:, 512 + hh * 128:512 + (hh + 1) * 128], self.dmask[:, 128:256])
        mU, mL = dm4[:, 0:512], dm4[:, 512:1024]
        dtails = []
        for g, (window, d) in enumerate(((128, 1), (512, 4), (2048, 16))):
            for hh in range(4):
                h = 4 * g + hh
                self.dma('sp', qd[hh], T(self.QT[h * 64:(h + 1) * 64, :], 'const'))
                self.dma('sp', kd[hh], T(self.KT[h * 64:(h + 1) * 64, :], 'const'))
            L = SEQ // d
            vdr = self.VD.rearrange("(j d) e -> d j e", d=d)
            ndr = self.ND[g].rearrange("(j d) e -> d j e", d=d)
            for r in range(d):
                vprev = None
                for jt in range(L // 128):
                    js = slice(jt * 128, (jt + 1) * 128)
                    jp = slice((jt - 1) * 128, jt * 128)
                    vown = self.rot('vsub', vsub)
                    self.dma('sp', vown, T(vdr[r, js, :], 'const'))
                    pso = self.rot('psS', self.ps[0:4])
                    psp = self.rot('psS', self.ps[0:4]) if jt > 0 else None
                    for hh in range(4):
                        qv = qd[hh].r("p (j d) -> p d j", d=d)
                        kv = kd[hh].r("p (j d) -> p d j", d=d)
                        self.mm(pso[:, hh * 128:(hh + 1) * 128], kv[:, r, js], qv[:, r, js], start=(hh == 0), stop=(hh == 3))
                    if jt > 0:
                        for hh in range(4):
                            qv = qd[hh].r("p (j d) -> p d j", d=d)
                            kv = kd[hh].r("p (j d) -> p d j", d=d)
                            self.mm(psp[:, hh * 128:(hh + 1) * 128], kv[:, r, jp], qv[:, r, js], start=(hh == 0), stop=(hh == 3))
                    po_ = self.rot('pTo', pTo)
                    self.act(po_, pso, AF.Exp, scale=0.125)
                    self.tt('dve', po_, po_, mL, ALU.mult)
                    pp_ = None
                    if jt > 0:
                        pp_ = self.rot('pTp', pTp)
                        self.act(pp_, psp, AF.Exp, scale=0.125)
                        self.tt('pool', pp_, pp_, mU, ALU.mult)

                    def tail(jt=jt, po_=po_, pp_=pp_, vown=vown, vprev=vprev, js=js, r=r, ndr=ndr):
                        py = self.ps2(6)
                        for hh in range(4):
                            oc = slice(hh * 256, hh * 256 + 193)
                            st = (hh % 2 == 0)
                            sp_ = (hh % 2 == 1)
                            if jt > 0:
                                self.mm(py[:, oc], pp_[:, hh * 128:(hh + 1) * 128], vprev[:, hh * 193:(hh + 1) * 193], start=st, stop=False)
                                self.mm(py[:, oc], po_[:, hh * 128:(hh + 1) * 128], vown[:, hh * 193:(hh + 1) * 193], start=False, stop=sp_)
                            else:
                                self.mm(py[:, oc], po_[:, hh * 128:(hh + 1) * 128], vown[:, hh * 193:(hh + 1) * 193], start=st, stop=sp_)
                        ob = self.rot('dosb', osb)
                        self.cp('act', ob.r("p (h e) -> p h e", h=4), py.r("p (h e) -> p h e", h=4)[:, :, 0:193])
                        self.dma('pool', T(ndr[r, js, :], self.u()), ob)

                    dtails.append(tail)
                    if len(dtails) > 1:
                        dtails.pop(0)()
                    vprev = vown
        while dtails:
            dtails.pop(0)()
        self.S.barrier()

    def dilated_combine(self, tt_, mix_out):
        ts = slice(tt_ * 128, (tt_ + 1) * 128)
        nd = self.dil_nd
        for g in range(3):
            self.dma('sp', nd[g], T(self.ND[g][ts, :], 'const'))
        self.tt('pool', nd[0], nd[0], nd[1], ALU.add)
        self.tt('pool', nd[0], nd[0], nd[2], ALU.add)
        a3 = nd[0].r("p (h e) -> p h e", h=4)
        rden = self.rot('rden', self.rdens)
        ra, da = rden.ap, a3.ap[:, :, 192]
        self.S.add('dve', lambda e: e.reciprocal(out=ra, in_=da), reads=_res(nd[0]), writes=_res(rden))
        for hh in range(4):
            self.tsc('dve', mix_out[:, hh * 192:(hh + 1) * 192], a3[:, hh, 0:192], rden[:, hh:hh + 1], None, ALU.mult)

    def phase_ffn1(self, li):
        c = self.phase('C')
        xT = c.bf16(8 * SEQ, name='xT1').r("p (c t) -> p c t", c=8)
        for cc in range(8):
            self.dma('sp', xT[:, cc, :], T(self.X1T[cc], 'const'))
        wst = [c.f32(1024) for _ in range(4)]
        wbf = [c.bf16(1024) for _ in range(4)]
        sg = [c.f32(512), c.f32(512)]
        aT = [c.bf16(512) for _ in range(3)]
        wgu = self.w_gate_up[li].rearrange("(c p) n -> p c n", p=128)
        def prepw(f):
            wb = []
            for col0 in (f * 128, DFF + f * 128):
                ws = self.rot('wst', wst).r("p (c n) -> p c n", c=8)
                w_ = self.rot('wbf', wbf).r("p (c n) -> p c n", c=8)
                self.dma('sp', ws, T(wgu[:, :, col0:col0 + 128], 'const'))
                self.cp('pool', w_, ws)
                wb.append(w_)
            return wb

        nxtw = prepw(0)
        for f in range(NFC):
            wb = nxtw
            if f + 1 < NFC:
                nxtw = prepw(f + 1)
            for G in range(NG):
                gs = slice(G * 512, (G + 1) * 512)
                pair = self.rot('psFF', [(0, 1), (2, 3), (4, 5)])
                pg, pu = self.ps[pair[0]], self.ps[pair[1]]
                for cc in range(8):
                    self.mm(pg, wb[0][:, cc, :], xT[:, cc, gs], start=(cc == 0), stop=(cc == 7))
                for cc in range(8):
                    self.mm(pu, wb[1][:, cc, :], xT[:, cc, gs], start=(cc == 0), stop=(cc == 7))
                s_ = self.rot('sg', sg)
                a_ = self.rot('aT', aT)
                self.act(s_, pg, AF.Silu)
                self.tt('dve', a_, s_, pu, ALU.mult)
                self.dma('pool', T(self.AT[f, :, gs], self.u()), a_)

    def phase_ffn2(self, li, last):
        c = self.phase('D')
        wd = c.bf16(NFC * DM).r("p (c n) -> p c n", c=NFC)
        xin = [c.f32(DM), c.f32(DM)]
        self.load_w_chunks(wd, lambda cc: T(self.w_down[li][cc * 128:(cc + 1) * 128, :], 'const'), NFC, DM, xin)
        g_t = c.f32(DM)
        b_t = c.f32(DM)
        self.dma('sp', g_t, T(self.ln2_g[li:li + 1, :].partition_broadcast(128), 'const'))
        self.dma('sp', b_t, T(self.ln2_b[li:li + 1, :].partition_broadcast(128), 'const'))
        lnb = self.ln_bufs(c, nz=1)
        aTg = [c.bf16(NFC * 512), c.bf16(NFC * 512)]
        for G in range(NG):
            gs = slice(G * 512, (G + 1) * 512)
            ag = self.rot('aTg', aTg).r("p (f t) -> p f t", f=NFC)
            self.dma('sp', ag, T(self.AT[:, :, gs].rearrange("f p t -> p f t"), 'const'))
            for jq in range(4):
                tt_ = 4 * G + jq
                ts = slice(tt_ * 128, (tt_ + 1) * 128)
                xi = self.rot('xin', xin)
                self.dma('sp', xi, T(self.X1F[ts, :], 'const'))
                py = self.rot('pyD', [self.ps2(0), self.ps2(2), self.ps2(6)])
                for nh in range(2):
                    for f in range(NFC):
                        self.mm(py[:, nh * 512:(nh + 1) * 512], ag[:, f, jq * 128:(jq + 1) * 128],
                                wd[:, f, nh * 512:(nh + 1) * 512], start=(f == 0), stop=(f == NFC - 1))
                z = self.rot('zD', [lnb['z']] + lnb['z2'])
                self.stt('dve', z, xi, ALPHA, py, ALU.mult, ALU.add)
                dst = self.y if last else self.XF[(li + 1) % 2]
                if last:
                    self.ln_core(lnb, z, g_t, b_t, T(dst[ts, :], self.u()), notr=True)
                else:
                    p2 = self.ln_core(lnb, z, g_t, b_t, T(dst[ts, :], self.u()), defer=True)
                    self.flush_pend()
                    self.pend.append((p2, lambda xTt, ts=ts: self.dma(
                        'pool', T(self.XT[:, :, ts].rearrange("c p t -> p c t"), self.u()), xTt)))

    def build_all(self, stop=None):
        self.stop = stop
        self.prologue()
        n = len(self.layers)
        for idx, li in enumerate(self.layers):
            if stop == 'P':
                break
            kind, j = li % 3, li // 3
            self.is_first = (idx == 0)
            w_in = (self.w_in_a, self.w_in_b, self.w_in_c)[kind][j]
            self.phase_inproj(li, kind, w_in, j)
            if stop and stop[0] == 'A':
                break
            self.phase_attn(li, kind)
            if stop == 'B':
                break
            self.phase_ffn1(li)
            if stop == 'C':
                break
            self.phase_ffn2(li, idx == n - 1)
        self.flush_pend()
        self.S.emit()


def make_consts():
    bf = ml_dtypes.bfloat16
    q = np.arange(128)[:, None]
    k = np.arange(128)[None, :]
    c = {}
    c['c_ident'] = np.eye(128, dtype=np.float32).astype(bf)
    c['c_tri'] = np.where(k <= q, 0.0, NEG).astype(np.float32).astype(bf)
    c['c_causneg'] = np.where(k <= q, 0.0, -BIG).astype(np.float32)
    r = np.zeros((128, 128), np.float32)
    inv = (500000.0 ** (-np.arange(8, dtype=np.float32) / 8.0)).astype(np.float32)
    invcol = np.zeros((128, 1), np.float32)
    for base in (0, 64):
        for d in range(8):
            r[base + d + 8, base + d] = -1.0
            r[base + d, base + d + 8] = 1.0
            invcol[base + d, 0] = inv[d]
            invcol[base + d + 8, 0] = inv[d]
    c['c_r128'] = r.astype(bf)
    c['c_invcol'] = invcol
    ee = np.zeros((16, 16, 128), np.float32)
    for n in range(16):
        ee[n, n, :] = 1.0
    c['c_ee'] = ee.reshape(16, 16 * 128).astype(bf)
    no = np.zeros((128, NT, 16), np.float32)
    for qt in range(NT):
        for n in range(16):
            if not (n < qt // 2):
                no[:, qt, n] = -BIG
    c['c_negown'] = no.reshape(128, NT * 16)
    kk = np.arange(128)[:, None]
    qq = np.arange(128)[None, :]
    dm = np.concatenate([(kk >= qq), (kk <= qq)], axis=1).astype(np.float32)
    c['c_dmask'] = dm.astype(bf)
    return c


_CACHE = {}


def get_nc(layers, dbg=False, stop=None):
    key = (tuple(layers), str(dbg), stop)
    if key not in _CACHE:
        nc = bass.Bass("TRN2", target_bir_lowering=False)
        b = Builder(nc, list(layers), dbg)
        b.build_all(stop)
        _CACHE[key] = nc
    return _CACHE[key]


def make_in_maps(inputs, n_cores=8):
    f = lambda a: np.ascontiguousarray(np.asarray(a, dtype=np.float32))
    consts = make_consts()
    shared = {
        'mem_ln_g': f(inputs['mem_ln_g']).reshape(1, DM),
        'mem_ln_b': f(inputs['mem_ln_b']).reshape(1, DM),
        'w_in_a': f(inputs['w_in_a']),
        'idx_kn_g': f(inputs['idx_kn_g']).reshape(2, 64, 1),
        'idx_kn_b': f(inputs['idx_kn_b']).reshape(2, 64, 1),
        'w_in_b': f(inputs['w_in_b']),
        'w_in_c': f(inputs['w_in_c']),
        'w_mem_kv': f(inputs['w_mem_kv']),
        'w_out': f(inputs['w_out']),
        'ln1_g': f(inputs['ln1_g']), 'ln1_b': f(inputs['ln1_b']),
        'w_gate_up': f(inputs['w_gate_up']),
        'w_down': f(inputs['w_down']),
        'ln2_g': f(inputs['ln2_g']), 'ln2_b': f(inputs['ln2_b']),
    }
    shared.update(consts)
    x = f(inputs['x'])
    mem = f(inputs['mem'])
    pos = np.ascontiguousarray(np.asarray(inputs['positions'], dtype=np.int32))
    maps = []
    for b in range(n_cores):
        m = dict(shared)
        m['x'] = x[b]
        m['mem'] = mem[b]
        m['positions'] = pos[b:b + 1]
        maps.append(m)
    return maps


def kernel(**inputs):
    nc = get_nc(range(DEPTH))
    maps = make_in_maps(inputs, 8)
    res = run_bass_kernel_spmd(nc, maps, core_ids=list(range(8)))
    return np.stack([np.asarray(r['y'], dtype=np.float32) for r in res.results], axis=0)
```

```python
import numpy as np
import ml_dtypes
import concourse.bass as bass
import concourse.mybir as mybir
from concourse.bass_utils import run_bass_kernel_spmd
from contextlib import ExitStack

F32 = mybir.dt.float32
BF16 = mybir.dt.bfloat16
I32 = mybir.dt.int32
ALU = mybir.AluOpType
AF = mybir.ActivationFunctionType
AX = mybir.AxisListType

SEQ = 4096
DM = 1024
NT = SEQ // 128
NG = SEQ // 512
DFF = 2816
NFC = DFF // 128
DEPTH = 4
ALPHA = (2 * DEPTH) ** 0.25
LN_EPS = 1e-5
NEG = -30000.0
BIG = 1.0e30
WIDTH_A = 3144
WIDTH_BC = 2560
TOPK = 256
NBIS = 14
ROPE_DEFER = 0
FP8 = mybir.dt.float8e5
NEG8 = -32768.0

ENGS = ['pe', 'act', 'dve', 'pool', 'sp']


class Op:
    __slots__ = ('eng', 'fn', 'deps', 'needed', 'dma', 'sig', 'slot', 'target', 'prev_target')

    def __init__(self, eng, fn, dma):
        self.eng = eng
        self.fn = fn
        self.dma = dma
        self.deps = []
        self.needed = False
        self.sig = 0
        self.slot = -1
        self.target = 0
        self.prev_target = 0


class Sched:
    def __init__(self, nc, n_dma_slots=32):
        self.nc = nc
        self.ops = {e: [] for e in ENGS}
        self.last_w = {}
        self.rd_eng = {}
        self.rd_dma = {}
        self.fence = {e: [] for e in ENGS}
        self.n_dma_slots = n_dma_slots
        self.qslots = {'sp': list(range(0, n_dma_slots // 2)), 'act': list(range(0, n_dma_slots // 2)),
                       'pool': list(range(n_dma_slots // 2, n_dma_slots))}
        self.slot_uses = [0] * n_dma_slots
        self.next_slot = {'sp': 0, 'act': 0, 'pool': 0}
        self.all_dma = []

    def add(self, eng, fn, reads=(), writes=(), dma=False):
        op = Op(eng, fn, dma)
        deps = {}
        excl = [r for r in reads if isinstance(r, str) and r.startswith('ps')]
        if excl:
            writes = list(writes) + excl
        for r in reads:
            w = self.last_w.get(r)
            if w is not None:
                deps[id(w)] = w
        for r in writes:
            w = self.last_w.get(r)
            if w is not None:
                deps[id(w)] = w
            for rd in self.rd_eng.get(r, {}).values():
                deps[id(rd)] = rd
            for rd in self.rd_dma.get(r, ()):
                deps[id(rd)] = rd
        for f in self.fence[eng]:
            deps[id(f)] = f
        self.fence[eng] = []
        for d in deps.values():
            if d.eng == 'pe' and eng == 'pe' and not d.dma and not dma:
                continue
            op.deps.append(d)
            d.needed = True
        for r in reads:
            if dma:
                self.rd_dma.setdefault(r, []).append(op)
            else:
                self.rd_eng.setdefault(r, {})[eng] = op
        for r in writes:
            self.last_w[r] = op
            self.rd_eng[r] = {}
            self.rd_dma[r] = []
        if dma:
            qs = self.qslots[eng]
            key = 'pool' if eng == 'pool' else 'sp'
            k = qs[self.next_slot[key] % len(qs)]
            self.next_slot[key] += 1
            op.slot = k
            op.prev_target = 16 * self.slot_uses[k]
            self.slot_uses[k] += 1
            op.target = 16 * self.slot_uses[k]
            op.needed = True
            self.all_dma.append(op)
        self.ops[eng].append(op)
        return op

    def barrier(self):
        lasts = []
        for e in ENGS:
            for op in reversed(self.ops[e]):
                if not op.dma:
                    lasts.append(op)
                    break
        lasts += self.all_dma[-self.n_dma_slots:]
        for e in ENGS:
            self.fence[e] = list(lasts)
        for d in lasts:
            d.needed = True

    def emit(self, final_wait_eng='sp'):
        nc = self.nc
        with ExitStack() as es:
            esem = {e: es.enter_context(nc.semaphore('s_' + e)) for e in ENGS}
            dsem = [es.enter_context(nc.semaphore('d_%d' % k)) for k in range(self.n_dma_slots)]
            for e in ENGS:
                c = 0
                for op in self.ops[e]:
                    if not op.dma and op.needed:
                        c += 1
                        op.sig = c
            block = es.enter_context(nc.Block())

            def run(ename, eobj):
                waited = {}

                def wait(key, sem, val):
                    if val <= 0 or waited.get(key, 0) >= val:
                        return
                    eobj.wait_ge(sem, val)
                    waited[key] = val

                for op in self.ops[ename]:
                    for d in op.deps:
                        if d.dma:
                            wait(('d', d.slot), dsem[d.slot], d.target)
                        else:
                            wait(('e', d.eng), esem[d.eng], d.sig)
                    if op.dma:
                        wait(('d', op.slot), dsem[op.slot], op.prev_target)
                        op.fn(eobj).then_inc(dsem[op.slot], 16)
                    else:
                        ins = op.fn(eobj)
                        if op.needed:
                            ins.then_inc(esem[ename], 1)
                if ename == final_wait_eng:
                    for k in range(self.n_dma_slots):
                        wait(('d', k), dsem[k], 16 * self.slot_uses[k])
                    for e2 in ENGS:
                        if e2 != ename:
                            n = sum(1 for o in self.ops[e2] if o.sig)
                            wait(('e', e2), esem[e2], n)

            @block.tensor
            def _(e):
                run('pe', e)

            @block.scalar
            def _(e):
                run('act', e)

            @block.vector
            def _(e):
                run('dve', e)

            @block.gpsimd
            def _(e):
                run('pool', e)

            @block.sync
            def _(e):
                run('sp', e)


class T:
    __slots__ = ('ap', 'res')

    def __init__(self, ap, res):
        self.ap = ap
        self.res = res if isinstance(res, (list, tuple)) else [res]

    def __getitem__(self, k):
        return T(self.ap[k], self.res)

    def r(self, pattern, **kw):
        return T(self.ap.rearrange(pattern, **kw), self.res)

    def named(self, res):
        return T(self.ap, res)


def _res(*ts):
    out = []
    for t in ts:
        if isinstance(t, T):
            out += t.res
    return out


def _a(x):
    return x.ap if isinstance(x, T) else x


class Carver:
    def __init__(self, arena_ap, prefix):
        self.a = arena_ap
        self.off = 0
        self.prefix = prefix
        self.n = 0
        self.cap = arena_ap.shape[1]

    def _take(self, nwords):
        nwords = (nwords + 7) // 8 * 8
        o = self.off
        self.off += nwords
        assert self.off <= self.cap, ('arena overflow', self.prefix, self.off, self.cap)
        self.n += 1
        return self.a[:, o:o + nwords], '%s_%d' % (self.prefix, self.n)

    def f32(self, n, parts=128, name=None):
        ap, nm = self._take(n)
        return T(ap[0:parts, 0:n], name or nm)

    def i32(self, n, parts=128):
        ap, nm = self._take(n)
        return T(ap[0:parts, 0:n].bitcast(I32), nm)

    def fp8(self, n, parts=128, name=None):
        ap, nm = self._take((n + 3) // 4)
        return T(ap.bitcast(FP8)[0:parts, 0:n], name or nm)

    def bf16(self, n, parts=128, name=None):
        ap, nm = self._take((n + 1) // 2)
        return T(ap.bitcast(BF16)[0:parts, 0:n], name or nm)


class MK:
    def __init__(self, nc):
        self.nc = nc
        self.S = Sched(nc)
        self.phase_id = 0
        rem = nc.sbuf_bytes_remaining
        self.PERS_WORDS = 6144
        self.ARENA_WORDS = (rem - 1024) // 4 - self.PERS_WORDS
        self.ARENA_WORDS = self.ARENA_WORDS // 8 * 8
        pers = nc.alloc_sbuf_tensor("pers", [128, self.PERS_WORDS], F32)
        arena = nc.alloc_sbuf_tensor("arena", [128, self.ARENA_WORDS], F32)
        self.pers = Carver(pers.ap(), 'pers')
        self.arena_ap = arena.ap()
        pst = nc.alloc_psum_tensor("psum", [128, 4096], F32)
        psap = pst.ap()
        self.ps = [T(psap[:, b * 512:(b + 1) * 512], 'ps%d' % b) for b in range(8)]
        self.psap = psap
        self.rr = {}

    def ps2(self, b):
        return T(self.psap[:, b * 512:(b + 2) * 512], ['ps%d' % b, 'ps%d' % (b + 1)])

    def flush_pend(self):
        pend = getattr(self, 'pend', [])
        self.pend = []
        for (part2, store) in pend:
            store(part2())

    def phase(self, name):
        self.flush_pend()
        self.S.barrier()
        self.phase_id += 1
        return Carver(self.arena_ap, 'ph%d%s' % (self.phase_id, name))

    def u(self):
        self.ucnt = getattr(self, 'ucnt', 0) + 1
        return 'u%d' % self.ucnt

    def rot(self, key, items):
        i = self.rr.get(key, 0)
        self.rr[key] = i + 1
        return items[i % len(items)]

    def mm(self, out, lhsT, rhs, start=True, stop=True, skip=False):
        o, l, r = out.ap, lhsT.ap, rhs.ap
        if skip:
            f = lambda e: e.matmul(o, lhsT=l, rhs=r, start=start, stop=stop, skip_group_check=True)
        else:
            f = lambda e: e.matmul(o, lhsT=l, rhs=r, start=start, stop=stop)
        return self.S.add('pe', f, reads=_res(lhsT, rhs), writes=_res(out))

    def tr(self, out, in_, ident):
        o, i, d = out.ap, in_.ap, ident.ap
        return self.S.add('pe', lambda e: e.transpose(o, i, d), reads=_res(in_, ident), writes=_res(out))

    def act(self, out, in_, func, scale=1.0, bias=None, accum=None):
        o, i = out.ap, in_.ap
        kw = {}
        if bias is not None:
            kw['bias'] = _a(bias)
        if accum is not None:
            kw['accum_out'] = accum.ap
        sc = _a(scale)
        return self.S.add('act', lambda e: e.activation(out=o, in_=i, func=func, scale=sc, **kw),
                          reads=_res(in_, scale, bias), writes=_res(out, accum))

    def tsc(self, eng, out, in0, s1, s2, op0, op1=None, accum=None):
        o, i = out.ap, in0.ap
        a1, a2 = _a(s1), _a(s2)
        kw = {}
        if op1 is not None:
            kw['op1'] = op1
        if accum is not None:
            kw['accum_out'] = accum.ap
        if o.dtype == FP8:
            kw['saturate'] = False
        return self.S.add(eng, lambda e: e.tensor_scalar(out=o, in0=i, scalar1=a1, scalar2=a2, op0=op0, **kw),
                          reads=_res(in0, s1, s2), writes=_res(out, accum))

    def tt(self, eng, out, in0, in1, op):
        o, a, b = out.ap, in0.ap, in1.ap
        return self.S.add(eng, lambda e: e.tensor_tensor(out=o, in0=a, in1=b, op=op),
                          reads=_res(in0, in1), writes=_res(out))

    def stt(self, eng, out, in0, scalar, in1, op0, op1):
        o, a, b = out.ap, in0.ap, in1.ap
        s = _a(scalar)
        return self.S.add(eng, lambda e: e.scalar_tensor_tensor(out=o, in0=a, scalar=s, in1=b, op0=op0, op1=op1),
                          reads=_res(in0, in1, scalar), writes=_res(out))

    def cp(self, eng, out, in_):
        o, i = out.ap, in_.ap
        if eng == 'act':
            return self.S.add('act', lambda e: e.copy(out=o, in_=i), reads=_res(in_), writes=_res(out))
        if o.dtype == FP8:
            return self.S.add(eng, lambda e: e.tensor_copy(out=o, in_=i, saturate=False), reads=_res(in_), writes=_res(out))
        return self.S.add(eng, lambda e: e.tensor_copy(out=o, in_=i), reads=_res(in_), writes=_res(out))

    def memset(self, eng, out, val):
        o = out.ap
        return self.S.add(eng, lambda e: e.memset(o, val), writes=_res(out))

    def red(self, eng, out, in_, op, axis=AX.X):
        o, i = out.ap, in_.ap
        return self.S.add(eng, lambda e: e.tensor_reduce(out=o, in_=i, axis=axis, op=op),
                          reads=_res(in_), writes=_res(out))

    def dma(self, q, out, in_):
        o, i = out.ap, in_.ap
        return self.S.add(q, lambda e: e.dma_start(out=o, in_=i), reads=_res(in_), writes=_res(out), dma=True)


class Builder(MK):
    def __init__(self, nc, layers, dbg=False):
        super().__init__(nc)
        self.layers = layers
        self.dbg = dbg
        dt = nc.dram_tensor
        ext = lambda n, s, d=F32: dt(n, s, d, kind="ExternalInput").ap()
        self.x = ext("x", [SEQ, DM])
        self.mem = ext("mem", [256, DM])
        self.positions = ext("positions", [1, SEQ], I32)
        self.mem_ln_g = ext("mem_ln_g", [1, DM])
        self.mem_ln_b = ext("mem_ln_b", [1, DM])
        self.w_in_a = ext("w_in_a", [2, DM, WIDTH_A])
        self.idx_kn_g = ext("idx_kn_g", [2, 64, 1])
        self.idx_kn_b = ext("idx_kn_b", [2, 64, 1])
        self.w_in_b = ext("w_in_b", [1, DM, WIDTH_BC])
        self.w_in_c = ext("w_in_c", [1, DM, WIDTH_BC])
        self.w_mem_kv = ext("w_mem_kv", [DEPTH, DM, 512])
        self.w_out = ext("w_out", [DEPTH, DM, DM])
        self.ln1_g = ext("ln1_g", [DEPTH, DM])
        self.ln1_b = ext("ln1_b", [DEPTH, DM])
        self.w_gate_up = ext("w_gate_up", [DEPTH, DM, 2 * DFF])
        self.w_down = ext("w_down", [DEPTH, DFF, DM])
        self.ln2_g = ext("ln2_g", [DEPTH, DM])
        self.ln2_b = ext("ln2_b", [DEPTH, DM])
        self.c_ident = ext("c_ident", [128, 128], BF16)
        self.c_tri = ext("c_tri", [128, 128], BF16)
        self.c_r128 = ext("c_r128", [128, 128], BF16)
        self.c_causneg = ext("c_causneg", [128, 128])
        self.c_invcol = ext("c_invcol", [128, 1])
        self.c_ee = ext("c_ee", [16, 16 * 128], BF16)
        self.c_negown = ext("c_negown", [128, NT * 16])
        self.c_dmask = ext("c_dmask", [128, 256], BF16)
        self.y = dt("y", [SEQ, DM], F32, kind="ExternalOutput").ap()
        itn = lambda n, s, d=BF16: dt(n, s, d, kind=("ExternalOutput" if (dbg and n in dbg) else "Internal")).ap()
        self.XT = itn("XT", [8, 128, SEQ])
        self.X1T = itn("X1T", [8, 128, SEQ])
        self.XF = [itn("XFa", [SEQ, DM], F32), itn("XFb", [SEQ, DM], F32)]
        self.X1F = itn("X1F", [SEQ, DM], F32)
        self.QT = itn("QT", [12 * 64, SEQ])
        self.KT = itn("KT", [12 * 64, SEQ])
        self.QMT = itn("QMT", [4 * 64, SEQ])
        self.QIT = itn("QIT", [8 * 64, SEQ])
        self.KIT = itn("KIT", [64, SEQ])
        self.VAP = itn("VAP", [12, 128, NT, 65])
        self.VD = itn("VD", [SEQ, 772])
        self.ND = itn("ND", [3, SEQ, 772], F32)
        self.AT = itn("AT", [NFC, 128, SEQ])
        self.COSD = itn("COSD", [128, SEQ], F32)
        self.SIND = itn("SIND", [128, SEQ], F32)
        D = lambda ap, res: T(ap, res)
        self.D = D
        p = self.pers
        self.ident = p.bf16(128)
        self.tri = p.bf16(128)
        self.r128 = p.bf16(128)
        self.causneg = p.f32(128)
        self.invcol = p.f32(1)
        self.ee = p.bf16(16 * 128, parts=16)
        self.negown = p.f32(NT * 16)
        self.dmask = p.bf16(256)
        self.negpi = p.f32(1)
        self.epsc = p.f32(1)
        self.ones64 = p.f32(64, parts=64)
        self.onesbf = p.bf16(128)
        self.ident8 = p.fp8(128)
        self.pow2 = p.f32(NBIS)
        self.memnT = p.bf16(8 * 256)
        self.mkT = p.bf16(4 * 256, parts=64)
        self.mvaug = p.bf16(2 * 4 * 65)
        self.wi_all = p.f32(NT * 8)
        self.absw = p.f32(NT * 8)
        self.sgn = p.f32(NT * 8)
        self.kng = p.f32(1, parts=64)
        self.knb = p.f32(1, parts=64)

    def prologue(self):
        for (t, src) in [(self.ident, self.c_ident), (self.tri, self.c_tri), (self.r128, self.c_r128),
                         (self.causneg, self.c_causneg), (self.invcol, self.c_invcol), (self.ee, self.c_ee),
                         (self.negown, self.c_negown), (self.dmask, self.c_dmask)]:
            self.dma('sp', t, T(src, 'const'))
        self.memset('dve', self.negpi, -float(np.pi))
        self.memset('dve', self.epsc, LN_EPS)
        self.memset('dve', self.ones64, 1.0 / 64.0)
        self.memset('dve', self.onesbf, 1.0)
        self.memset('dve', self.mvaug, 1.0)
        for it in range(NBIS):
            self.memset('dve', self.pow2[:, it:it + 1], 2.0 ** (-(it + 1)))
        c = self.phase('rope')
        self.cp('dve', self.ident8, self.ident)
        H = 2048
        posi = c.i32(H)
        ang = c.f32(H)
        u = c.f32(H)
        kf = c.f32(H)
        tab = c.f32(H)
        for half in range(2):
            sl = slice(half * H, (half + 1) * H)
            self.dma('sp', posi, T(self.positions[:, sl].partition_broadcast(128), 'const'))
            self.cp('dve', ang, posi)
            self.tsc('dve', ang, ang, self.invcol[:, 0:1], None, ALU.mult)
            for (off, dst) in [(0.5, self.SIND), (0.75, self.COSD)]:
                self.tsc('dve', u, ang, 1.0 / (2.0 * np.pi), off, ALU.mult, ALU.add)
                self.cp('dve', posi, u)
                self.cp('dve', kf, posi)
                self.tt('dve', u, u, kf, ALU.subtract)
                self.tsc('dve', kf, u, 0.0, None, ALU.is_lt)
                self.tt('dve', u, u, kf, ALU.add)
                self.act(tab, u, AF.Sin, scale=2.0 * float(np.pi), bias=self.negpi[:, 0:1])
                self.dma('sp', T(dst[:, sl], self.u()), tab)
        c = self.phase('memln')
        g_t = c.f32(DM)
        b_t = c.f32(DM)
        self.dma('sp', g_t, T(self.mem_ln_g.partition_broadcast(128), 'const'))
        self.dma('sp', b_t, T(self.mem_ln_b.partition_broadcast(128), 'const'))
        lnb = self.ln_bufs(c)
        for mt in range(2):
            z = lnb['z']
            self.dma('sp', z, T(self.mem[mt * 128:(mt + 1) * 128, :], 'const'))
            xTt = self.ln_core(lnb, z, g_t, b_t, None)
            self.cp('dve', self.memnT.r("p (c t) -> p c t", c=8)[:, :, mt * 128:(mt + 1) * 128], xTt)
        c = self.phase('x2xT')
        xin = [c.f32(DM), c.f32(DM)]
        xb = [c.bf16(DM), c.bf16(DM)]
        xt = [c.bf16(DM), c.bf16(DM)]
        for tt_ in range(NT):
            a, b, o = xin[tt_ % 2], xb[tt_ % 2], xt[tt_ % 2]
            self.dma('sp', a, T(self.x[tt_ * 128:(tt_ + 1) * 128, :], 'const'))
            self.cp('act', b, a)
            pst = self.rot('pstr', [self.ps[4], self.ps[5]])
            psb = T(pst.ap.bitcast(BF16), pst.res)
            for cc in range(8):
                self.tr(psb[:, cc * 128:(cc + 1) * 128], b[:, cc * 128:(cc + 1) * 128], self.ident)
            self.cp('dve', o, psb[:, 0:1024])
            self.dma('pool', T(self.XT[:, :, tt_ * 128:(tt_ + 1) * 128].rearrange("c p t -> p c t"), self.u()),
                     o.r("p (c t) -> p c t", c=8))

    def ln_bufs(self, c, nz=0):
        return dict(z=c.f32(DM), z2=[c.f32(DM) for _ in range(nz)], st=c.f32(12), mv=c.f32(2), sd=c.f32(1), rs=c.f32(1),
                    xo=[c.f32(DM), c.f32(DM)], xb=[c.bf16(DM), c.bf16(DM)], xTt=[c.bf16(DM), c.bf16(DM)])

    def ln_core(self, lnb, z, g_t, b_t, out_f32_dram, trps=None, defer=False, notr=False):
        st, mv, sd, rs = lnb['st'], lnb['mv'], lnb['sd'], lnb['rs']
        xo = self.rot(('xo', id(lnb)), lnb['xo'])
        xb = self.rot(('xb', id(lnb)), lnb['xb'])
        xTt = self.rot(('xTt', id(lnb)), lnb['xTt'])
        za, zb, sa, sb_, ma = z.ap, z.ap, st.ap, st.ap, mv.ap
        self.S.add('dve', lambda e: e.bn_stats(out=sa[:, 0:6], in_=za[:, 0:512]), reads=_res(z), writes=_res(st))
        self.S.add('dve', lambda e: e.bn_stats(out=sa[:, 6:12], in_=za[:, 512:1024]), reads=_res(z), writes=_res(st))
        self.S.add('dve', lambda e: e.bn_aggr(out=ma, in_=sa), reads=_res(st), writes=_res(mv))
        self.act(sd, mv[:, 1:2], AF.Sqrt, bias=self.epsc[:, 0:1])
        sda, rsa = sd.ap, rs.ap
        self.S.add('dve', lambda e: e.reciprocal(out=rsa, in_=sda), reads=_res(sd), writes=_res(rs))
        self.tsc('dve', z, z, mv[:, 0:1], rs[:, 0:1], ALU.subtract, ALU.mult)
        self.tt('dve', z, z, g_t, ALU.mult)
        self.tt('pool', xo, z, b_t, ALU.add)
        if out_f32_dram is not None:
            self.dma('pool', out_f32_dram, xo)
        if notr:
            return None
        self.cp('act', xb, xo)

        def part2():
            pst = self.rot('pstr', trps or [self.ps[4], self.ps[5]])
            psb = T(pst.ap.bitcast(BF16), pst.res)
            for cc in range(8):
                self.tr(psb[:, cc * 128:(cc + 1) * 128], xb[:, cc * 128:(cc + 1) * 128], self.ident)
            self.cp('act', xTt, psb[:, 0:1024])
            return xTt.r("p (c t) -> p c t", c=8)

        if defer:
            return part2
        return part2()

    def load_w_chunks(self, dst_bf, src_dram_fn, nchunks, ncols, stage, eng='pool'):
        for cc in range(nchunks):
            st = self.rot(('wst', id(stage[0])), stage)
            self.dma('sp', st[:, 0:ncols], src_dram_fn(cc))
            self.cp(eng, dst_bf[:, cc, :], st[:, 0:ncols])

    def phase_inproj(self, li, kind, w_in, j):
        c = self.phase('A')
        xT = c.bf16(8 * SEQ, name='xT').r("p (c t) -> p c t", c=8)
        for cc in range(8):
            self.dma('sp', xT[:, cc, :], T(self.XT[cc], 'const'))
        COS = c.f32(SEQ)
        SIN = c.f32(SEQ)
        self.dma('sp', COS, T(self.COSD, 'const'))
        self.dma('sp', SIN, T(self.SIND, 'const'))
        wst = [c.f32(1024), c.f32(1024)]
        wbf = [c.bf16(1024), c.bf16(1024)]
        qsb = [c.bf16(512), c.bf16(512)]
        t1 = [c.f32(512), c.f32(512)]
        t2 = [c.f32(512), c.f32(512)]
        osb = [c.bf16(512), c.bf16(512), c.bf16(512)]
        w_pcn = w_in.rearrange("(c p) n -> p c n", p=128)

        jobs = []
        self.ropetails = []

        def fm_prep(col0, M):
            ws = self.rot('wst', wst)
            wb = self.rot('wbf', wbf)
            wsv = ws[:, 0:8 * M].r("p (c n) -> p c n", c=8)
            wbv = wb[:, 0:8 * M].r("p (c n) -> p c n", c=8)
            self.dma('sp', wsv, T(w_pcn[:, :, col0:col0 + M], 'const'))
            self.cp('pool', wbv, wsv)
            return wbv

        def fm_job(col0, M, mode, dst):
            jobs.append((col0, M, mode, dst))

        def run_jobs():
            nxt = fm_prep(jobs[0][0], jobs[0][1])
            for i, (col0, M, mode, dst) in enumerate(jobs):
                wbv = nxt
                if i + 1 < len(jobs):
                    nxt = fm_prep(jobs[i + 1][0], jobs[i + 1][1])
                fm_run(wbv, M, mode, dst)
            while self.ropetails:
                self.ropetails.pop(0)()

        def fm_run(wbv, M, mode, dst):
            for G in range(NG):
                gs = slice(G * 512, (G + 1) * 512)
                ps = self.rot('psA', self.ps[0:3])
                for cc in range(8):
                    self.mm(ps[0:M, :], wbv[:, cc, :], xT[:, cc, gs], start=(cc == 0), stop=(cc == 7))
                o = self.rot('osb', osb)
                if mode == 'plain':
                    self.cp('act', o[0:M, :], ps[0:M, :])
                    self.dma('pool', T(dst[:, gs], self.u()), o[0:M, :])
                elif mode == 'rope':
                    q = self.rot('qsb', qsb)
                    self.cp('act', q[0:M, :], ps[0:M, :])

                    def tail(ps=ps, q=q, o=o, gs=gs, dst=dst, M=M):
                        a1 = self.rot('t1', t1)
                        a2 = self.rot('t2', t2)
                        pr = self.rot('psR', self.ps[3:5])
                        self.mm(pr[0:M, :], self.r128[0:M, 0:M], q[0:M, :])
                        self.tt('dve', a1[0:M, :], ps[0:M, :], COS[0:M, gs], ALU.mult)
                        self.tt('dve', a2[0:M, :], pr[0:M, :], SIN[0:M, gs], ALU.mult)
                        self.tt('pool', o[0:M, :], a1[0:M, :], a2[0:M, :], ALU.add)
                        self.dma('pool', T(dst[:, gs], self.u()), o[0:M, :])

                    tails = self.ropetails
                    tails.append(tail)
                    if len(tails) > ROPE_DEFER:
                        tails.pop(0)()

        def fm_region(col0, ncols, mode, dst):
            for i in range(ncols // 128):
                fm_job(col0 + i * 128, 128, mode, dst[i * 128:(i + 1) * 128, :])

        if self.stop == 'A00':
            return
        if self.stop in ('A01', 'A02', 'A03'):
            ws = self.rot('wst', wst)
            wb = self.rot('wbf', wbf)
            wsv = ws[:, 0:8 * 128].r("p (c n) -> p c n", c=8)
            wbv = wb[:, 0:8 * 128].r("p (c n) -> p c n", c=8)
            self.dma('sp', wsv, T(w_pcn[:, :, 0:128], 'const'))
            self.cp('pool', wbv, wsv)
            if self.stop == 'A01':
                return
            gs = slice(0, 512)
            ps = self.ps[0]
            for cc in range(8):
                self.mm(ps, wbv[:, cc, :], xT[:, cc, gs], start=(cc == 0), stop=(cc == 7))
            o = osb[0]
            if self.stop == 'A02':
                self.cp('act', o, ps)
            else:
                q = qsb[0]
                self.cp('act', q, ps)
                pr = self.ps[3]
                self.mm(pr, self.r128, q)
                self.tt('dve', t1[0], ps, COS[:, gs], ALU.mult)
                self.tt('dve', t2[0], pr, SIN[:, gs], ALU.mult)
                self.tt('pool', o, t1[0], t2[0], ALU.add)
            self.dma('pool', T(self.QT[0:128, gs], self.u()), o)
            return
        fm_region(0, 768, 'rope', self.QT)
        fm_region(768, 768, 'rope', self.KT)
        if kind == 0:
            fm_region(2304, 512, 'rope', self.QIT)
            fm_region(2888, 256, 'plain', self.QMT)
            run_jobs()
            self.inproj_ki(c, w_pcn, xT, COS, SIN, j)
        else:
            fm_region(2304, 256, 'plain', self.QMT)
            run_jobs()
        if self.stop == 'A2':
            return

        NV = 776 if kind == 0 else 768
        wv = c.bf16(8 * NV).r("p (c n) -> p c n", c=8)
        vst = [c.f32(NV), c.f32(NV)]
        for cc in range(8):
            st = self.rot('vst', vst)
            self.dma('sp', st[:, 0:768], T(w_in[cc * 128:(cc + 1) * 128, 1536:2304], 'const'))
            if kind == 0:
                self.dma('sp', st[:, 768:776], T(w_in[cc * 128:(cc + 1) * 128, 2816:2824], 'const'))
            self.cp('pool', wv[:, cc, :], st[:, 0:NV])
        nvh, dv = (4, 192) if kind == 1 else (12, 64)
        vaug = [c.bf16(nvh * (dv + 1)), c.bf16(nvh * (dv + 1))]
        for v in vaug:
            self.memset('pool', v, 1.0)
        for tt_ in range(NT):
            ts = slice(tt_ * 128, (tt_ + 1) * 128)
            psv = self.ps2(6)
            for (n0, n1) in [(0, 512), (512, NV)]:
                for cc in range(8):
                    self.mm(psv[:, n0:n1], xT[:, cc, ts], wv[:, cc, n0:n1], start=(cc == 0), stop=(cc == 7))
            va = self.rot('vaug', vaug)
            va3 = va.r("p (h e) -> p h e", h=nvh)
            self.cp('act', va3[:, :, 0:dv], psv[:, 0:768].r("p (h d) -> p h d", h=nvh))
            if kind == 0:
                self.cp('dve', self.wi_all[:, tt_ * 8:(tt_ + 1) * 8], psv[:, 768:776])
            if kind == 1:
                self.dma('pool', T(self.VD[ts, :], self.u()), va)
            else:
                self.dma('pool', T(self.VAP[:, :, tt_, :].rearrange("h p e -> p h e"), self.u()), va3)

        if self.stop == 'A3':
            return
        wm = self.w_mem_kv[li]
        wmb = c.bf16(8 * 512).r("p (c n) -> p c n", c=8)
        mst = [c.f32(512), c.f32(512)]
        for cc in range(8):
            st = self.rot('mst', mst)
            self.dma('sp', st, T(wm[cc * 128:(cc + 1) * 128, :], 'const'))
            self.cp('pool', wmb[:, cc, :], st)
        memn = self.memnT.r("p (c t) -> p c t", c=8)
        mk3 = self.mkT.r("p (h t) -> p h t", h=4)
        for h in range(4):
            ps = self.rot('psA', self.ps[0:3])
            for cc in range(8):
                self.mm(ps[0:64, 0:256], wmb[:, cc, h * 64:(h + 1) * 64], memn[:, cc, :], start=(cc == 0), stop=(cc == 7))
            self.cp('act', mk3[:, h, :], ps[0:64, 0:256])
        mv4 = self.mvaug.r("p (t h e) -> p t h e", t=2, h=4)
        for mt in range(2):
            ps = self.rot('psA', self.ps[0:3])
            for cc in range(8):
                self.mm(ps[:, 0:256], memn[:, cc, mt * 128:(mt + 1) * 128], wmb[:, cc, 256:512], start=(cc == 0), stop=(cc == 7))
            self.cp('act', mv4[:, mt, :, 0:64], ps[:, 0:256].r("p (h d) -> p h d", h=4))

    def inproj_ki(self, c, w_pcn, xT, COS, SIN, j):
        self.dma('sp', self.kng, T(self.idx_kn_g[j], 'const'))
        self.dma('sp', self.knb, T(self.idx_kn_b[j], 'const'))
        ws = c.f32(8 * 64).r("p (c n) -> p c n", c=8)
        wb = c.bf16(8 * 64).r("p (c n) -> p c n", c=8)
        self.dma('sp', ws, T(w_pcn[:, :, 2824:2888], 'const'))
        self.cp('pool', wb, ws)
        P = 64
        xs = c.f32(512, parts=P)
        x2 = c.f32(512, parts=P)
        msb = c.f32(512, parts=P)
        var = c.f32(512, parts=P)
        xn = c.f32(512, parts=P)
        xnb = c.bf16(512, parts=P)
        a1 = c.f32(512, parts=P)
        a2 = c.f32(512, parts=P)
        o = [c.bf16(512, parts=P), c.bf16(512, parts=P)]
        for G in range(NG):
            gs = slice(G * 512, (G + 1) * 512)
            ps = self.rot('psA', self.ps[0:3])
            for cc in range(8):
                self.mm(ps[0:P, :], wb[:, cc, :], xT[:, cc, gs], start=(cc == 0), stop=(cc == 7))
            self.cp('act', xs, ps[0:P, :])
            self.act(x2, ps[0:P, :], AF.Square)
            pm = self.ps[3]
            pe2 = self.ps[4]
            self.mm(pm[0:P, :], self.ones64, xs)
            self.mm(pe2[0:P, :], self.ones64, x2)
            self.act(msb, pm[0:P, :], AF.Square)
            self.tt('dve', var, pe2[0:P, :], msb, ALU.subtract)
            self.act(var, var, AF.Sqrt, bias=self.epsc[0:P, 0:1])
            va = var.ap
            self.S.add('dve', lambda e, va=va: e.reciprocal(out=va, in_=va), reads=_res(var), writes=_res(var))
            self.tt('dve', xn, xs, pm[0:P, :], ALU.subtract)
            self.tt('dve', xn, xn, var, ALU.mult)
            self.tsc('dve', xn, xn, self.kng[:, 0:1], self.knb[:, 0:1], ALU.mult, ALU.add)
            self.cp('act', xnb, xn)
            pr = self.ps[5]
            self.mm(pr[0:P, :], self.r128[0:P, 0:P], xnb)
            self.tt('dve', a1, xn, COS[0:P, gs], ALU.mult)
            self.tt('dve', a2, pr[0:P, :], SIN[0:P, gs], ALU.mult)
            oo = self.rot('kio', o)
            self.tt('pool', oo, a1, a2, ALU.add)
            self.dma('pool', T(self.KIT[:, gs], self.u()), oo)

    def attn_head(self, G, kT, qTg, vaugh, nkt, causal, bias_fn, mix3, col0, pT, scale=0.125):
        po = self.rot('psO', [self.ps[3], self.ps[4]])
        pend = []
        first = [True]

        def pv(kt, c0, pt):
            for jj in range(c0, 4):
                self.mm(po[:, jj * 65:(jj + 1) * 65], pt[:, jj * 128:(jj + 1) * 128], vaugh[:, kt, :],
                        start=first[0], stop=(kt == nkt - 1 and jj == 3))
                first[0] = False

        for kt in range(nkt):
            c0 = max(0, kt - 4 * G) if causal else 0
            psS = self.rot('psS', self.ps[0:3])
            cols = slice(c0 * 128, 512)
            biases = bias_fn(kt, c0, psS) if bias_fn is not None else []
            self.mm(psS[:, cols], kT[:, kt * 128:(kt + 1) * 128], qTg[:, cols], start=True, stop=(len(biases) == 0))
            for bi, (o, l, r) in enumerate(biases):
                self.mm(o, l, r, start=False, stop=(bi == len(biases) - 1))
            pt = self.rot('pT', pT)
            self.act(pt[:, cols], psS[:, cols], AF.Exp, scale=scale)
            pend.append((kt, c0, pt))
            if len(pend) > 2:
                pv(*pend.pop(0))
            if kt == 2 and getattr(self, 'mid_cb', None) is not None:
                cb, self.mid_cb = self.mid_cb, None
                cb()
        while pend:
            pv(*pend.pop(0))
        if getattr(self, 'mid_cb', None) is not None:
            cb, self.mid_cb = self.mid_cb, None
            cb()
        self.flush_pend()
        po3 = po[:, 0:260].r("p (j e) -> p j e", j=4)
        rden = self.rot('rden', self.rdens)
        lnd = self.rot('lnd', self.lnds)
        self.act(lnd, po3[:, :, 64], AF.Ln)
        self.act(rden, lnd, AF.Exp, scale=-1.0)
        for jj in range(4):
            self.act(mix3[:, jj, col0:col0 + 64], po3[:, jj, 0:64], AF.Copy, scale=rden[:, jj:jj + 1])

    def phase_attn(self, li, kind):
        c = self.phase('B')
        wout = c.bf16(8 * DM).r("p (c n) -> p c n", c=8)
        xin = [c.f32(DM), c.f32(DM)]
        self.load_w_chunks(wout, lambda cc: T(self.w_out[li][cc * 128:(cc + 1) * 128, :], 'const'), 8, DM, xin)
        g_t = c.f32(DM)
        b_t = c.f32(DM)
        self.dma('sp', g_t, T(self.ln1_g[li:li + 1, :].partition_broadcast(128), 'const'))
        self.dma('sp', b_t, T(self.ln1_b[li:li + 1, :].partition_broadcast(128), 'const'))
        lnb = self.ln_bufs(c)
        mix = c.bf16(4 * DM)
        mix3 = mix.r("p (j n) -> p j n", j=4)
        mixT = [c.bf16(DM), c.bf16(DM)]
        self.rdens = [c.f32(4), c.f32(4)]
        self.lnds = [c.f32(4), c.f32(4)]
        xres = self.x if self.is_first else self.XF[li % 2]
        if kind == 1:
            self.dilated_all(c, li)
        else:
            kTb = [c.bf16(SEQ, parts=64), c.bf16(SEQ, parts=64)]
            vab = [c.bf16(NT * 65), c.bf16(NT * 65)]
            qTb = [c.bf16(512, parts=64), c.bf16(512, parts=64)]
            pT = [c.bf16(512) for _ in range(4)]
            if kind == 0:
                I = c.f32(SEQ)
                Mb2 = [[c.fp8(SEQ) for _ in range(4)] for _ in range(2)]
                kiT = c.bf16(SEQ, parts=64)
                qiT = c.bf16(8 * 512, parts=64).r("p (h t) -> p h t", h=8)
                rbuf = [c.bf16(512) for _ in range(4)]
                diag = [c.bf16(8 * 128), c.bf16(8 * 128)]
                sm = dict(lo=c.f32(1), hi=c.f32(1), d0=c.f32(1), cth=c.f32(1), cnt=c.f32(1), step=c.f32(1), dtab=c.f32(NBIS))
                self.dma('sp', kiT, T(self.KIT, 'const'))
                self.act(self.absw, self.wi_all, AF.Abs)
                self.tsc('dve', self.sgn, self.wi_all, 0.0, None, ALU.is_ge)
                self.tsc('dve', self.sgn, self.sgn, 2.0, -1.0, ALU.mult, ALU.add)
            else:
                kmean = c.f32(16, parts=64)
                kmb = c.bf16(16, parts=64)
                self.memset('dve', kmb, 0.0)
                gm = c.f32(64)
                sel = c.f32(64)
                m8 = c.f32(8)
                selbs = [c.bf16(64), c.bf16(64)]
                selbT = [c.bf16(512, parts=16), c.bf16(512, parts=16)]
        for G in range(NG):
            gs = slice(G * 512, (G + 1) * 512)
            nkeys = (G + 1) * 512
            nkt = (G + 1) * 4
            if kind == 0:
                Mb = Mb2[G % 2]

                def idx_group(Gn):
                    gsn = slice(Gn * 512, (Gn + 1) * 512)
                    self.dma('sp', qiT, T(self.QIT[:, gsn].rearrange("(h d) t -> d h t", h=8), 'const'))

                if G == 0:
                    idx_group(0)
                    for jq in range(4):
                        self.dsa_index(0, jq, I, Mb2[0][jq], kiT, qiT, rbuf, diag, sm)
                if G + 1 < NG:
                    idx_group(G + 1)
            if kind != 1:
                def prep(h):
                    kT = self.rot('kTb', kTb)
                    va = self.rot('vab', vab)
                    qTg = self.rot('qTb', qTb)
                    self.dma('sp', kT[:, 0:nkeys], T(self.KT[h * 64:(h + 1) * 64, 0:nkeys], 'const'))
                    va3 = va.r("p (t e) -> p t e", t=NT)
                    self.dma('sp', va3[:, 0:nkt, :], T(self.VAP[h, :, 0:nkt, :], 'const'))
                    self.dma('sp', qTg, T(self.QT[h * 64:(h + 1) * 64, gs], 'const'))
                    sT = None
                    if kind == 2:
                        sT = self.rot('selbT', selbT)
                        sb_ = self.rot('selbuf', selbs)
                        gt = self.moba_gate(G, h, kT, qTg, kmean, kmb, gm, sel, m8, sb_, sT)
                        if h == 0:
                            gt()
                        else:
                            self.mid_cb = gt
                    return kT, va3, qTg, sT

                nxt = prep(0)
                for h in range(12):
                    kT, va3, qTg, sT = nxt
                    if kind == 0 and G + 1 < NG and h % 3 == 0:
                        self.dsa_index(G + 1, h // 3, I, Mb2[(G + 1) % 2][h // 3], kiT, qiT, rbuf, diag, sm)
                    if h + 1 < 12:
                        nxt = prep(h + 1)
                    if kind == 0:
                        def bias_fn(kt, c0, psS, Mb=Mb):
                            return [(psS[:, jj * 128:(jj + 1) * 128], Mb[jj][:, kt * 128:(kt + 1) * 128], self.ident8)
                                    for jj in range(c0, 4)]
                    else:
                        def bias_fn(kt, c0, psS, sT=sT):
                            n = kt // 2
                            out = []
                            run = None
                            for jj in range(c0, 4):
                                qt = 4 * G + jj
                                if qt // 2 == n:
                                    if run is not None:
                                        out.append(run)
                                        run = None
                                    if qt == kt:
                                        out.append((psS[:, jj * 128:(jj + 1) * 128], self.tri, self.ident))
                                else:
                                    if run is None:
                                        run = [jj, jj + 1]
                                    else:
                                        run[1] = jj + 1
                            if run is not None:
                                out.append(run)
                            res = []
                            for it in out:
                                if isinstance(it, list):
                                    cs = slice(it[0] * 128, it[1] * 128)
                                    res.append((psS[:, cs], self.ee[:, n * 128:(n + 1) * 128], sT[:, cs]))
                                else:
                                    res.append(it)
                            return res
                    self.attn_head(G, kT, qTg, va3, nkt, True, bias_fn, mix3, h * 64, pT)
            if kind == 1:
                qTb = self.dil_qTb
                pT = self.dil_pT
            mk3 = self.mkT.r("p (h t) -> p h t", h=4)
            mv4 = self.mvaug.r("p (t h e) -> p t h e", t=2, h=4)
            for m in range(4):
                qTg = self.rot('qTb', qTb)
                self.dma('sp', qTg, T(self.QMT[m * 64:(m + 1) * 64, gs], 'const'))
                self.attn_head(G, mk3[:, m, :], qTg, mv4[:, :, m, :], 2, False, None, mix3, 768 + m * 64, pT)
            for jq in range(4):
                tt_ = 4 * G + jq
                ts = slice(tt_ * 128, (tt_ + 1) * 128)
                if kind == 1:
                    self.dilated_combine(tt_, mix3[:, jq, 0:768])
                xi = self.rot('xin', xin)
                self.dma('sp', xi, T(xres[ts, :], 'const'))
                pst = self.ps[5]
                psb = T(pst.ap.bitcast(BF16), pst.res)
                for cc in range(8):
                    self.tr(psb[:, cc * 128:(cc + 1) * 128], mix3[:, jq, cc * 128:(cc + 1) * 128], self.ident)
                mT = self.rot('mixT', mixT)
                self.cp('act', mT, psb[:, 0:1024])
                mT3 = mT.r("p (c t) -> p c t", c=8)
                py = self.ps2(6)
                for nh in range(2):
                    for cc in range(8):
                        self.mm(py[:, nh * 512:(nh + 1) * 512], mT3[:, cc, :], wout[:, cc, nh * 512:(nh + 1) * 512],
                                start=(cc == 0), stop=(cc == 7))
                z = lnb['z']
                self.stt('dve', z, xi, ALPHA, py, ALU.mult, ALU.add)
                p2 = self.ln_core(lnb, z, g_t, b_t, T(self.X1F[ts, :], self.u()), trps=[self.ps[5]], defer=True)
                self.flush_pend()
                self.pend.append((p2, lambda xTt, ts=ts: self.dma(
                    'pool', T(self.X1T[:, :, ts].rearrange("c p t -> p c t"), self.u()), xTt)))

    def dsa_index(self, G, jq, I, Mbj, kiT, qiT, rbuf, diag, sm):
        qt = 4 * G + jq
        nk = (qt + 1) * 128
        dg = self.rot('diag', diag).r("p (h k) -> p h k", h=8)
        for h in range(8):
            self.tsc('pool', dg[:, h, :], self.ident, self.sgn[:, qt * 8 + h:qt * 8 + h + 1], None, ALU.mult)
        nch = (nk + 511) // 512
        LAG = 3
        for ch in range(nch):
            w = min(512, nk - ch * 512)
            pa = self.rot('psO', [self.ps[3], self.ps[4]])
            rr = [None] * 8
            for h in range(8 + LAG):
                if h < 8:
                    pd = self.rot('psS', self.ps[0:3])
                    self.mm(pd[:, 0:w], qiT[:, h, jq * 128:(jq + 1) * 128], kiT[:, ch * 512:ch * 512 + w])
                    rr[h] = self.rot('rbuf', rbuf)
                    self.act(rr[h][:, 0:w], pd[:, 0:w], AF.Relu, scale=self.absw[:, qt * 8 + h:qt * 8 + h + 1])
                if h >= LAG:
                    hh = h - LAG
                    self.mm(pa[:, 0:w], dg[:, hh, :], rr[hh][:, 0:w], start=(hh == 0), stop=(hh == 7))
            self.cp('dve', I[:, ch * 512:ch * 512 + w], pa[:, 0:w])
        dsl = slice(qt * 128, (qt + 1) * 128)
        lo, hi, d0, cth, cnt, step = sm['lo'], sm['hi'], sm['d0'], sm['cth'], sm['cnt'], sm['step']
        if qt >= 2:
            self.red('dve', lo, I[:, 0:nk], ALU.min)
            self.tt('pool', I[:, dsl], I[:, dsl], self.causneg, ALU.add)
            self.red('dve', hi, I[:, 0:nk], ALU.max)
            self.tt('dve', d0, hi, lo, ALU.subtract)
            dtab = sm['dtab']
            self.tsc('dve', dtab, self.pow2, d0[:, 0:1], None, ALU.mult)
            self.stt('dve', lo, d0, 0.5, lo, ALU.mult, ALU.add)
            for it in range(NBIS):
                self.tsc('dve', Mbj[:, 0:nk], I[:, 0:nk], lo[:, 0:1], 0.0, ALU.is_ge, ALU.add, accum=cnt)
                self.tsc('dve', step, cnt, TOPK - 0.5, 0.5, ALU.is_ge, ALU.subtract)
                self.stt('dve', lo, dtab[:, it:it + 1], step[:, 0:1], lo, ALU.mult, ALU.add)
        else:
            self.tt('pool', I[:, dsl], I[:, dsl], self.causneg, ALU.add)
            self.memset('dve', lo, -1.0e29)
        self.tsc('dve', Mbj[:, 0:nk], I[:, 0:nk], lo[:, 0:1], NEG8, ALU.is_lt, ALU.mult)

    def moba_gate(self, G, h, kT, qTg, kmean, kmb, gm, sel, m8, selb, sT):
        nb = 2 * G + 2
        self.red('dve', kmean[:, 0:nb], kT[:, 0:nb * 256].r("p (n k) -> p n k", k=256), ALU.add)
        self.tsc('dve', kmb[:, 0:nb], kmean[:, 0:nb], 1.0 / 256.0, None, ALU.mult)
        pg = self.rot('psS', self.ps[0:3])
        for jj in range(4):
            self.mm(pg[:, jj * 16:(jj + 1) * 16], qTg[:, jj * 128:(jj + 1) * 128], kmb, start=(jj == 0), stop=(jj == 3))
        self.tt('dve', gm, pg[:, 0:64], self.negown[:, G * 64:(G + 1) * 64], ALU.add)
        for jj in range(4):
            ma, ga = m8.ap, gm.ap[:, jj * 16:(jj + 1) * 16]
            self.S.add('dve', lambda e, ma=ma, ga=ga: e.max(out=ma, in_=ga), reads=_res(gm), writes=_res(m8))
            self.tsc('dve', sel[:, jj * 16:(jj + 1) * 16], gm[:, jj * 16:(jj + 1) * 16], m8[:, 2:3], None, ALU.is_ge)
        self.tsc('dve', selb, sel, -NEG, NEG, ALU.mult, ALU.add)
        def tail():
            pst = self.ps[5]
            psb = T(pst.ap.bitcast(BF16), pst.res)
            for jj in range(4):
                self.tr(psb[0:16, jj * 128:(jj + 1) * 128], selb[:, jj * 16:(jj + 1) * 16], self.ident)
            self.cp('act', sT, psb[0:16, 0:512])
        return tail

    def dilated_all(self, c, li):
        qd = [c.bf16(SEQ, parts=64) for _ in range(4)]
        kd = [c.bf16(SEQ, parts=64) for _ in range(4)]
        vsub = [c.bf16(772) for _ in range(4)]
        pTo = [c.bf16(512), c.bf16(512)]
        pTp = [c.bf16(512), c.bf16(512)]
        osb = [c.f32(772), c.f32(772)]
        dm4 = c.bf16(1024)
        self.dil_qTb = [c.bf16(512, parts=64), c.bf16(512, parts=64)]
        self.dil_pT = [c.bf16(512) for _ in range(4)]
        self.dil_nd = [c.f32(772) for _ in range(3)]
        for hh in range(4):
            self.cp('dve', dm4[:, hh * 128:(hh + 1) * 128], self.dmask[:, 0:128])
            self.cp('dve', dm4[:, 512 + hh * 128:512 + (hh + 1) * 128], self.dmask[:, 128:256])
        mU, mL = dm4[:, 0:512], dm4[:, 512:1024]
        dtails = []
        for g, (window, d) in enumerate(((128, 1), (512, 4), (2048, 16))):
            for hh in range(4):
                h = 4 * g + hh
                self.dma('sp', qd[hh], T(self.QT[h * 64:(h + 1) * 64, :], 'const'))
                self.dma('sp', kd[hh], T(self.KT[h * 64:(h + 1) * 64, :], 'const'))
            L = SEQ // d
            vdr = self.VD.rearrange("(j d) e -> d j e", d=d)
            ndr = self.ND[g].rearrange("(j d) e -> d j e", d=d)
            for r in range(d):
                vprev = None
                for jt in range(L // 128):
                    js = slice(jt * 128, (jt + 1) * 128)
                    jp = slice((jt - 1) * 128, jt * 128)
                    vown = self.rot('vsub', vsub)
                    self.dma('sp', vown, T(vdr[r, js, :], 'const'))
                    pso = self.rot('psS', self.ps[0:4])
                    psp = self.rot('psS', self.ps[0:4]) if jt > 0 else None
                    for hh in range(4):
                        qv = qd[hh].r("p (j d) -> p d j", d=d)
                        kv = kd[hh].r("p (j d) -> p d j", d=d)
                        self.mm(pso[:, hh * 128:(hh + 1) * 128], kv[:, r, js], qv[:, r, js], start=(hh == 0), stop=(hh == 3))
                    if jt > 0:
                        for hh in range(4):
                            qv = qd[hh].r("p (j d) -> p d j", d=d)
                            kv = kd[hh].r("p (j d) -> p d j", d=d)
                            self.mm(psp[:, hh * 128:(hh + 1) * 128], kv[:, r, jp], qv[:, r, js], start=(hh == 0), stop=(hh == 3))
                    po_ = self.rot('pTo', pTo)
                    self.act(po_, pso, AF.Exp, scale=0.125)
                    self.tt('dve', po_, po_, mL, ALU.mult)
                    pp_ = None
                    if jt > 0:
                        pp_ = self.rot('pTp', pTp)
                        self.act(pp_, psp, AF.Exp, scale=0.125)
                        self.tt('pool', pp_, pp_, mU, ALU.mult)

                    def tail(jt=jt, po_=po_, pp_=pp_, vown=vown, vprev=vprev, js=js, r=r, ndr=ndr):
                        py = self.ps2(6)
                        for hh in range(4):
                            oc = slice(hh * 256, hh * 256 + 193)
                            st = (hh % 2 == 0)
                            sp_ = (hh % 2 == 1)
                            if jt > 0:
                                self.mm(py[:, oc], pp_[:, hh * 128:(hh + 1) * 128], vprev[:, hh * 193:(hh + 1) * 193], start=st, stop=False)
                                self.mm(py[:, oc], po_[:, hh * 128:(hh + 1) * 128], vown[:, hh * 193:(hh + 1) * 193], start=False, stop=sp_)
                            else:
                                self.mm(py[:, oc], po_[:, hh * 128:(hh + 1) * 128], vown[:, hh * 193:(hh + 1) * 193], start=st, stop=sp_)
                        ob = self.rot('dosb', osb)
                        self.cp('act', ob.r("p (h e) -> p h e", h=4), py.r("p (h e) -> p h e", h=4)[:, :, 0:193])
                        self.dma('pool', T(ndr[r, js, :], self.u()), ob)

                    dtails.append(tail)
                    if len(dtails) > 1:
                        dtails.pop(0)()
                    vprev = vown
        while dtails:
            dtails.pop(0)()
        self.S.barrier()

    def dilated_combine(self, tt_, mix_out):
        ts = slice(tt_ * 128, (tt_ + 1) * 128)
        nd = self.dil_nd
        for g in range(3):
            self.dma('sp', nd[g], T(self.ND[g][ts, :], 'const'))
        self.tt('pool', nd[0], nd[0], nd[1], ALU.add)
        self.tt('pool', nd[0], nd[0], nd[2], ALU.add)
        a3 = nd[0].r("p (h e) -> p h e", h=4)
        rden = self.rot('rden', self.rdens)
        ra, da = rden.ap, a3.ap[:, :, 192]
        self.S.add('dve', lambda e: e.reciprocal(out=ra, in_=da), reads=_res(nd[0]), writes=_res(rden))
        for hh in range(4):
            self.tsc('dve', mix_out[:, hh * 192:(hh + 1) * 192], a3[:, hh, 0:192], rden[:, hh:hh + 1], None, ALU.mult)

    def phase_ffn1(self, li):
        c = self.phase('C')
        xT = c.bf16(8 * SEQ, name='xT1').r("p (c t) -> p c t", c=8)
        for cc in range(8):
            self.dma('sp', xT[:, cc, :], T(self.X1T[cc], 'const'))
        wst = [c.f32(1024) for _ in range(4)]
        wbf = [c.bf16(1024) for _ in range(4)]
        sg = [c.f32(512), c.f32(512)]
        aT = [c.bf16(512) for _ in range(3)]
        wgu = self.w_gate_up[li].rearrange("(c p) n -> p c n", p=128)
        def prepw(f):
            wb = []
            for col0 in (f * 128, DFF + f * 128):
                ws = self.rot('wst', wst).r("p (c n) -> p c n", c=8)
                w_ = self.rot('wbf', wbf).r("p (c n) -> p c n", c=8)
                self.dma('sp', ws, T(wgu[:, :, col0:col0 + 128], 'const'))
                self.cp('pool', w_, ws)
                wb.append(w_)
            return wb

        nxtw = prepw(0)
        for f in range(NFC):
            wb = nxtw
            if f + 1 < NFC:
                nxtw = prepw(f + 1)
            for G in range(NG):
                gs = slice(G * 512, (G + 1) * 512)
                pair = self.rot('psFF', [(0, 1), (2, 3), (4, 5)])
                pg, pu = self.ps[pair[0]], self.ps[pair[1]]
                for cc in range(8):
                    self.mm(pg, wb[0][:, cc, :], xT[:, cc, gs], start=(cc == 0), stop=(cc == 7))
                for cc in range(8):
                    self.mm(pu, wb[1][:, cc, :], xT[:, cc, gs], start=(cc == 0), stop=(cc == 7))
                s_ = self.rot('sg', sg)
                a_ = self.rot('aT', aT)
                self.act(s_, pg, AF.Silu)
                self.tt('dve', a_, s_, pu, ALU.mult)
                self.dma('pool', T(self.AT[f, :, gs], self.u()), a_)

    def phase_ffn2(self, li, last):
        c = self.phase('D')
        wd = c.bf16(NFC * DM).r("p (c n) -> p c n", c=NFC)
        xin = [c.f32(DM), c.f32(DM)]
        self.load_w_chunks(wd, lambda cc: T(self.w_down[li][cc * 128:(cc + 1) * 128, :], 'const'), NFC, DM, xin)
        g_t = c.f32(DM)
        b_t = c.f32(DM)
        self.dma('sp', g_t, T(self.ln2_g[li:li + 1, :].partition_broadcast(128), 'const'))
        self.dma('sp', b_t, T(self.ln2_b[li:li + 1, :].partition_broadcast(128), 'const'))
        lnb = self.ln_bufs(c, nz=1)
        aTg = [c.bf16(NFC * 512), c.bf16(NFC * 512)]
        for G in range(NG):
            gs = slice(G * 512, (G + 1) * 512)
            ag = self.rot('aTg', aTg).r("p (f t) -> p f t", f=NFC)
            self.dma('sp', ag, T(self.AT[:, :, gs].rearrange("f p t -> p f t"), 'const'))
            for jq in range(4):
                tt_ = 4 * G + jq
                ts = slice(tt_ * 128, (tt_ + 1) * 128)
                xi = self.rot('xin', xin)
                self.dma('sp', xi, T(self.X1F[ts, :], 'const'))
                py = self.rot('pyD', [self.ps2(0), self.ps2(2), self.ps2(6)])
                for nh in range(2):
                    for f in range(NFC):
                        self.mm(py[:, nh * 512:(nh + 1) * 512], ag[:, f, jq * 128:(jq + 1) * 128],
                                wd[:, f, nh * 512:(nh + 1) * 512], start=(f == 0), stop=(f == NFC - 1))
                z = self.rot('zD', [lnb['z']] + lnb['z2'])
                self.stt('dve', z, xi, ALPHA, py, ALU.mult, ALU.add)
                dst = self.y if last else self.XF[(li + 1) % 2]
                if last:
                    self.ln_core(lnb, z, g_t, b_t, T(dst[ts, :], self.u()), notr=True)
                else:
                    p2 = self.ln_core(lnb, z, g_t, b_t, T(dst[ts, :], self.u()), defer=True)
                    self.flush_pend()
                    self.pend.append((p2, lambda xTt, ts=ts: self.dma(
                        'pool', T(self.XT[:, :, ts].rearrange("c p t -> p c t"), self.u()), xTt)))

    def build_all(self, stop=None):
        self.stop = stop
        self.prologue()
        n = len(self.layers)
        for idx, li in enumerate(self.layers):
            if stop == 'P':
                break
            kind, j = li % 3, li // 3
            self.is_first = (idx == 0)
            w_in = (self.w_in_a, self.w_in_b, self.w_in_c)[kind][j]
            self.phase_inproj(li, kind, w_in, j)
            if stop and stop[0] == 'A':
                break
            self.phase_attn(li, kind)
            if stop == 'B':
                break
            self.phase_ffn1(li)
            if stop == 'C':
                break
            self.phase_ffn2(li, idx == n - 1)
        self.flush_pend()
        self.S.emit()


def make_consts():
    bf = ml_dtypes.bfloat16
    q = np.arange(128)[:, None]
    k = np.arange(128)[None, :]
    c = {}
    c['c_ident'] = np.eye(128, dtype=np.float32).astype(bf)
    c['c_tri'] = np.where(k <= q, 0.0, NEG).astype(np.float32).astype(bf)
    c['c_causneg'] = np.where(k <= q, 0.0, -BIG).astype(np.float32)
    r = np.zeros((128, 128), np.float32)
    inv = (500000.0 ** (-np.arange(8, dtype=np.float32) / 8.0)).astype(np.float32)
    invcol = np.zeros((128, 1), np.float32)
    for base in (0, 64):
        for d in range(8):
            r[base + d + 8, base + d] = -1.0
            r[base + d, base + d + 8] = 1.0
            invcol[base + d, 0] = inv[d]
            invcol[base + d + 8, 0] = inv[d]
    c['c_r128'] = r.astype(bf)
    c['c_invcol'] = invcol
    ee = np.zeros((16, 16, 128), np.float32)
    for n in range(16):
        ee[n, n, :] = 1.0
    c['c_ee'] = ee.reshape(16, 16 * 128).astype(bf)
    no = np.zeros((128, NT, 16), np.float32)
    for qt in range(NT):
        for n in range(16):
            if not (n < qt // 2):
                no[:, qt, n] = -BIG
    c['c_negown'] = no.reshape(128, NT * 16)
    kk = np.arange(128)[:, None]
    qq = np.arange(128)[None, :]
    dm = np.concatenate([(kk >= qq), (kk <= qq)], axis=1).astype(np.float32)
    c['c_dmask'] = dm.astype(bf)
    return c


_CACHE = {}


def get_nc(layers, dbg=False, stop=None):
    key = (tuple(layers), str(dbg), stop)
    if key not in _CACHE:
        nc = bass.Bass("TRN2", target_bir_lowering=False)
        b = Builder(nc, list(layers), dbg)
        b.build_all(stop)
        _CACHE[key] = nc
    return _CACHE[key]


def make_in_maps(inputs, n_cores=8):
    f = lambda a: np.ascontiguousarray(np.asarray(a, dtype=np.float32))
    consts = make_consts()
    shared = {
        'mem_ln_g': f(inputs['mem_ln_g']).reshape(1, DM),
        'mem_ln_b': f(inputs['mem_ln_b']).reshape(1, DM),
        'w_in_a': f(inputs['w_in_a']),
        'idx_kn_g': f(inputs['idx_kn_g']).reshape(2, 64, 1),
        'idx_kn_b': f(inputs['idx_kn_b']).reshape(2, 64, 1),
        'w_in_b': f(inputs['w_in_b']),
        'w_in_c': f(inputs['w_in_c']),
        'w_mem_kv': f(inputs['w_mem_kv']),
        'w_out': f(inputs['w_out']),
        'ln1_g': f(inputs['ln1_g']), 'ln1_b': f(inputs['ln1_b']),
        'w_gate_up': f(inputs['w_gate_up']),
        'w_down': f(inputs['w_down']),
        'ln2_g': f(inputs['ln2_g']), 'ln2_b': f(inputs['ln2_b']),
    }
    shared.update(consts)
    x = f(inputs['x'])
    mem = f(inputs['mem'])
    pos = np.ascontiguousarray(np.asarray(inputs['positions'], dtype=np.int32))
    maps = []
    for b in range(n_cores):
        m = dict(shared)
        m['x'] = x[b]
        m['mem'] = mem[b]
        m['positions'] = pos[b:b + 1]
        maps.append(m)
    return maps


def kernel(**inputs):
    nc = get_nc(range(DEPTH))
    maps = make_in_maps(inputs, 8)
    res = run_bass_kernel_spmd(nc, maps, core_ids=list(range(8)))
    return np.stack([np.asarray(r['y'], dtype=np.float32) for r in res.results], axis=0)
```
